# Optimizing a Trainium2 kernel written in Bass

```python
import math
import jax
import jax.numpy as jnp
from jax import lax
import numpy as np

D_MODEL = 1024
BATCH = 8
SEQ = 4096
DEPTH = 2

GRID_W = 64
CTX_LEN = 256
N_SUB = 3
N_MOD = 3 * N_SUB
D_FF = 2816
FFN_RES = 0.5
NORM_EPS = 1e-6

CHUNK = 128
D_A = 768
A_GROUPS = 6
A_GROUP_DIM = D_A // A_GROUPS

D_B = D_MODEL - D_A
S5_GROUP = 16
S5_GROUPS = D_B // S5_GROUP
S5_STATE = 64
N_DIR = 2

N_HEADS = 8
N_KV_HEADS = 2
HEAD_DIM = D_MODEL // N_HEADS
Q_PER_KV = N_HEADS // N_KV_HEADS
Q_DIM = N_HEADS * HEAD_DIM
KV_DIM = N_KV_HEADS * HEAD_DIM
ROPE_AXIS_DIM = HEAD_DIM // 2
ROPE_HALF = ROPE_AXIS_DIM // 2
ROPE_THETA = 10000.0
Q_BLOCK = 128

N_EVEN = (DEPTH + 1) // 2
N_ODD = DEPTH // 2

kernel_name = "hybrid_sgu_s5_gqa_prefix_dit_block"


def _rms(x, g):
    xf = x.astype(jnp.float32)
    y = xf * lax.rsqrt(jnp.mean(xf * xf, axis=-1, keepdims=True) + NORM_EPS)
    return (y * g.astype(jnp.float32)).astype(x.dtype)


def _modulate_pre(x, g, shift, scale):
    return _rms(x, g) * (1 + scale[:, None]) + shift[:, None]


def _gated_post(x, y, g, gate, weight):
    return x + weight * gate[:, None] * _rms(y, g)


def _swiglu_sub(x, mod, j, g_pre, g_post, w_in, w_out):
    h = _modulate_pre(x, g_pre, mod[:, 3 * j], mod[:, 3 * j + 1])
    gate, up = jnp.split(h @ w_in, 2, axis=-1)
    y = (jax.nn.silu(gate) * up) @ w_out
    return _gated_post(x, y, g_post, mod[:, 3 * j + 2], FFN_RES)


def _chunk_sgu(p, norm_g, w_s, b_s):
    bsz, t, _ = p.shape
    u = jax.nn.gelu(p[..., :D_A])
    v = jax.nn.gelu(p[..., D_A:]).reshape(bsz, t // CHUNK, CHUNK, A_GROUPS, A_GROUP_DIM)
    vf = v.astype(jnp.float32)
    mu = jnp.mean(vf, axis=-1, keepdims=True)
    var = jnp.mean(jnp.square(vf - mu), axis=-1, keepdims=True)
    vn = ((vf - mu) * lax.rsqrt(var + NORM_EPS)
          * norm_g.reshape(A_GROUPS, A_GROUP_DIM).astype(jnp.float32)).astype(p.dtype)
    mixed = jnp.einsum('gts,bnsgc->bntgc', w_s, vn) + b_s.T[:, :, None]
    return u * mixed.reshape(bsz, t, D_A)


def _s5_discretise(lam_re, lam_im, log_step, b_re, b_im, c_re, c_im):
    f32 = jnp.float32
    lam = lax.complex(lam_re.astype(f32), lam_im.astype(f32))
    dt = jnp.exp(log_step.astype(f32))[..., None]
    lam_bar = jnp.exp(lam * dt)
    b_mat = lax.complex(b_re.astype(f32), b_im.astype(f32))
    b_bar = ((lam_bar - 1.0) / lam)[..., None] * b_mat
    c_mat = lax.complex(c_re.astype(f32), c_im.astype(f32))
    return lam_bar, b_bar, c_mat


def _ssm_combine(left, right):
    a_l, h_l = left
    a_r, h_r = right
    return a_l * a_r, a_r * h_l + h_r


def _s5_states(u, lam_bar, b_bar, h0, reverse):
    bu = jnp.einsum('btgc,gpc->btgp', u.astype(jnp.complex64), b_bar)
    if h0 is not None:
        edge = -1 if reverse else 0
        bu = bu.at[:, edge].add(lam_bar * h0)
    a = jnp.broadcast_to(lam_bar, (1,) + bu.shape[1:])
    _, h = lax.associative_scan(_ssm_combine, (a, bu), reverse=reverse, axis=1)
    return h


def _s5_readout(u, h_f, h_b, c_mat, d_skip, glu_w, glu_b, dtype):
    y = (jnp.real(jnp.einsum('btgp,gcp->btgc', h_f, c_mat[0]))
         + jnp.real(jnp.einsum('btgp,gcp->btgc', h_b, c_mat[1])))
    y = y.reshape(*y.shape[:2], D_B) + d_skip.astype(jnp.float32) * u.reshape(*u.shape[:2], D_B)
    y = jax.nn.gelu(y).astype(dtype)
    return y * jax.nn.sigmoid(y @ glu_w + glu_b)


def _mixer_sgu_s5(hl, hc, need_ctx, w_in, w_out, sgu_g, sgu_w, sgu_b, lam_re, lam_im, log_step,
                  b_re, b_im, c_re, c_im, d_skip, glu_w, glu_b):
    lam_bar, b_bar, c_mat = _s5_discretise(lam_re, lam_im, log_step, b_re, b_im, c_re, c_im)
    pl = hl @ w_in
    pc = hc @ w_in

    def s5_in(p):
        return p[..., 2 * D_A:].astype(jnp.float32).reshape(*p.shape[:2], S5_GROUPS, S5_GROUP)

    uc, ul = s5_in(pc), s5_in(pl)
    hc_f = _s5_states(uc, lam_bar[0], b_bar[0], None, False)
    hc_b = _s5_states(uc, lam_bar[1], b_bar[1], None, True)
    hl_f = _s5_states(ul, lam_bar[0], b_bar[0], hc_f[:, -1], False)
    hl_b = _s5_states(ul, lam_bar[1], b_bar[1], hc_b[:, 0], True)

    def merge(p, u, h_f, h_b):
        ya = _chunk_sgu(p[..., :2 * D_A], sgu_g, sgu_w, sgu_b)
        yb = _s5_readout(u, h_f, h_b, c_mat, d_skip, glu_w, glu_b, p.dtype)
        return jnp.concatenate([ya, yb], axis=-1) @ w_out

    yl = merge(pl, ul, hl_f, hl_b)
    yc = merge(pc, uc, hc_f, hc_b) if need_ctx else None
    return yl, yc


def _rope_tables(rows):
    f32 = jnp.float32
    row_id = jnp.repeat(jnp.arange(rows, dtype=f32), GRID_W)
    col_id = jnp.tile(jnp.arange(GRID_W, dtype=f32), rows)
    inv_freq = ROPE_THETA ** (-jnp.arange(0, ROPE_AXIS_DIM, 2, dtype=f32) / ROPE_AXIS_DIM)
    ang = jnp.stack([row_id[:, None] * inv_freq, col_id[:, None] * inv_freq], axis=1)
    return jnp.cos(ang), jnp.sin(ang)


def _rope2d(x, cos, sin):
    xf = x.astype(jnp.float32).reshape(*x.shape[:-1], 2, 2, ROPE_HALF)
    x1, x2 = xf[..., 0, :], xf[..., 1, :]
    c = cos[None, :, None]
    s = sin[None, :, None]
    out = jnp.stack([x1 * c - x2 * s, x2 * c + x1 * s], axis=-2)
    return out.reshape(x.shape).astype(x.dtype)


def _heads(p, n):
    return p.reshape(*p.shape[:2], n, HEAD_DIM)


def _attend(q, k, v):
    bsz, t = q.shape[:2]
    nb = t // Q_BLOCK
    qb = q.reshape(bsz, nb, Q_BLOCK, N_KV_HEADS, Q_PER_KV, HEAD_DIM).transpose(1, 0, 2, 3, 4, 5)
    scale = HEAD_DIM ** -0.5

    def one_block(qi):
        s = jnp.einsum('bqkgd,blkd->bkgql', qi, k).astype(jnp.float32) * scale
        pr = jax.nn.softmax(s, axis=-1).astype(v.dtype)
        return jnp.einsum('bkgql,blkd->bqkgd', pr, v)

    o = lax.map(one_block, qb)
    return o.transpose(1, 0, 2, 3, 4, 5).reshape(bsz, t, Q_DIM)


def _mixer_gqa(hl, hc, need_ctx, w_qkv, w_out, q_g, k_g, cos, sin):
    pkv_c = hc @ w_qkv[:, Q_DIM:]
    kc = _rms(_heads(pkv_c[..., :KV_DIM], N_KV_HEADS), k_g)
    vc = _heads(pkv_c[..., KV_DIM:], N_KV_HEADS)
    pl = hl @ w_qkv
    ql = _rope2d(_rms(_heads(pl[..., :Q_DIM], N_HEADS), q_g), cos, sin)
    kl = _rope2d(_rms(_heads(pl[..., Q_DIM:Q_DIM + KV_DIM], N_KV_HEADS), k_g), cos, sin)
    vl = _heads(pl[..., Q_DIM + KV_DIM:], N_KV_HEADS)
    yl = _attend(ql, jnp.concatenate([kc, kl], axis=1), jnp.concatenate([vc, vl], axis=1)) @ w_out
    yc = None
    if need_ctx:
        qc = _rms(_heads(hc @ w_qkv[:, :Q_DIM], N_HEADS), q_g)
        yc = _attend(qc, kc, vc) @ w_out
    return yl, yc


def setup_inputs(seed: int = 0) -> dict:
    key = jax.random.key(seed)
    ks = iter(jax.random.split(key, 40))
    f32 = jnp.float32

    def nrm(shape, scale):
        return scale * jax.random.normal(next(ks), shape, f32)

    x = nrm((BATCH, SEQ, D_MODEL), 1.0)
    c = nrm((BATCH, D_MODEL), 1.0)
    ctx = nrm((BATCH, CTX_LEN, D_MODEL), 1.0)
    c_ctx = nrm((D_MODEL,), 1.0)
    w_mod = nrm((DEPTH, D_MODEL, N_MOD * D_MODEL), 0.5 * D_MODEL ** -0.5)
    b_mod = nrm((DEPTH, N_MOD * D_MODEL), 0.02)
    norm_pre = 1.0 + nrm((DEPTH, N_SUB, D_MODEL), 0.02)
    norm_post = 1.0 + nrm((DEPTH, N_SUB, D_MODEL), 0.02)
    ffn_w_in = nrm((DEPTH, 2, D_MODEL, 2 * D_FF), D_MODEL ** -0.5)
    ffn_w_out = nrm((DEPTH, 2, D_FF, D_MODEL), D_FF ** -0.5)
    ab_w_in = nrm((N_EVEN, D_MODEL, 2 * D_A + D_B), D_MODEL ** -0.5)
    ab_w_out = nrm((N_EVEN, D_A + D_B, D_MODEL), (D_A + D_B) ** -0.5)
    sgu_norm_g = 1.0 + nrm((N_EVEN, D_A), 0.02)
    sgu_w = nrm((N_EVEN, A_GROUPS, CHUNK, CHUNK), CHUNK ** -0.5)
    sgu_b = 1.0 + nrm((N_EVEN, A_GROUPS, CHUNK), 0.02)
    s5_lam_re = -0.5 * jnp.exp(nrm((N_EVEN, N_DIR, S5_GROUPS, S5_STATE), 0.05))
    s5_lam_im = (math.pi * jnp.arange(S5_STATE, dtype=f32)
                 + nrm((N_EVEN, N_DIR, S5_GROUPS, S5_STATE), 0.01))
    s5_log_step = jax.random.uniform(next(ks), (N_EVEN, N_DIR, S5_GROUPS), f32,
                                     minval=math.log(1e-3), maxval=math.log(1e-1))
    s5_b_re = nrm((N_EVEN, N_DIR, S5_GROUPS, S5_STATE, S5_GROUP), (2 * S5_GROUP) ** -0.5)
    s5_b_im = nrm((N_EVEN, N_DIR, S5_GROUPS, S5_STATE, S5_GROUP), (2 * S5_GROUP) ** -0.5)
    s5_c_re = nrm((N_EVEN, N_DIR, S5_GROUPS, S5_GROUP, S5_STATE), (2 * S5_STATE) ** -0.5)
    s5_c_im = nrm((N_EVEN, N_DIR, S5_GROUPS, S5_GROUP, S5_STATE), (2 * S5_STATE) ** -0.5)
    s5_d = nrm((N_EVEN, D_B), 1.0)
    s5_glu_w = nrm((N_EVEN, D_B, D_B), D_B ** -0.5)
    s5_glu_b = nrm((N_EVEN, D_B), 0.02)
    attn_w_qkv = nrm((N_ODD, D_MODEL, Q_DIM + 2 * KV_DIM), D_MODEL ** -0.5)
    attn_w_out = nrm((N_ODD, Q_DIM, D_MODEL), Q_DIM ** -0.5)
    attn_q_norm = 1.0 + nrm((N_ODD, HEAD_DIM), 0.02)
    attn_k_norm = 1.0 + nrm((N_ODD, HEAD_DIM), 0.02)
    return {"x": x, "c": c, "ctx": ctx, "c_ctx": c_ctx, "w_mod": w_mod, "b_mod": b_mod,
            "norm_pre": norm_pre, "norm_post": norm_post, "ffn_w_in": ffn_w_in, "ffn_w_out": ffn_w_out,
            "ab_w_in": ab_w_in, "ab_w_out": ab_w_out, "sgu_norm_g": sgu_norm_g, "sgu_w": sgu_w,
            "sgu_b": sgu_b, "s5_lam_re": s5_lam_re, "s5_lam_im": s5_lam_im, "s5_log_step": s5_log_step,
            "s5_b_re": s5_b_re, "s5_b_im": s5_b_im, "s5_c_re": s5_c_re, "s5_c_im": s5_c_im,
            "s5_d": s5_d, "s5_glu_w": s5_glu_w, "s5_glu_b": s5_glu_b, "attn_w_qkv": attn_w_qkv,
            "attn_w_out": attn_w_out, "attn_q_norm": attn_q_norm, "attn_k_norm": attn_k_norm}


def reference(x, c, ctx, c_ctx, w_mod, b_mod, norm_pre, norm_post, ffn_w_in, ffn_w_out,
              ab_w_in, ab_w_out, sgu_norm_g, sgu_w, sgu_b, s5_lam_re, s5_lam_im, s5_log_step,
              s5_b_re, s5_b_im, s5_c_re, s5_c_im, s5_d, s5_glu_w, s5_glu_b,
              attn_w_qkv, attn_w_out, attn_q_norm, attn_k_norm):
    rows = x.shape[1] // GRID_W
    cos, sin = _rope_tables(rows)
    cond_l = jax.nn.silu(c)
    cond_c = jax.nn.silu(c_ctx)[None]
    xl, xc = x, ctx
    for i in range(DEPTH):
        last = i == DEPTH - 1
        j = i // 2
        mod_l = (cond_l @ w_mod[i] + b_mod[i]).reshape(-1, N_MOD, D_MODEL)
        mod_c = (cond_c @ w_mod[i] + b_mod[i]).reshape(-1, N_MOD, D_MODEL)
        ffn1 = (norm_pre[i, 0], norm_post[i, 0], ffn_w_in[i, 0], ffn_w_out[i, 0])
        ffn2 = (norm_pre[i, 2], norm_post[i, 2], ffn_w_in[i, 1], ffn_w_out[i, 1])
        xl = _swiglu_sub(xl, mod_l, 0, *ffn1)
        xc = _swiglu_sub(xc, mod_c, 0, *ffn1)
        hl = _modulate_pre(xl, norm_pre[i, 1], mod_l[:, 3], mod_l[:, 4])
        hc = _modulate_pre(xc, norm_pre[i, 1], mod_c[:, 3], mod_c[:, 4])
        if i % 2 == 0:
            yl, yc = _mixer_sgu_s5(hl, hc, not last, ab_w_in[j], ab_w_out[j], sgu_norm_g[j], sgu_w[j],
                                   sgu_b[j], s5_lam_re[j], s5_lam_im[j], s5_log_step[j], s5_b_re[j],
                                   s5_b_im[j], s5_c_re[j], s5_c_im[j], s5_d[j], s5_glu_w[j], s5_glu_b[j])
        else:
            yl, yc = _mixer_gqa(hl, hc, not last, attn_w_qkv[j], attn_w_out[j], attn_q_norm[j],
                                attn_k_norm[j], cos, sin)
        xl = _gated_post(xl, yl, norm_post[i, 1], mod_l[:, 5], 1.0)
        xl = _swiglu_sub(xl, mod_l, 2, *ffn2)
        if not last:
            xc = _gated_post(xc, yc, norm_post[i, 1], mod_c[:, 5], 1.0)
            xc = _swiglu_sub(xc, mod_c, 2, *ffn2)
    return xl
```

```python
import numpy as np
from contextlib import ExitStack
import concourse.bass as bass
import concourse.mybir as mybir
from concourse.bass_utils import run_bass_kernel_spmd

F32 = mybir.dt.float32
BF16 = mybir.dt.bfloat16
I32 = mybir.dt.int32
AF = mybir.ActivationFunctionType
ALU = mybir.AluOpType
AX = mybir.AxisListType


class Buf:
    __slots__ = ("name", "t", "w", "r", "dsid", "dcnt")

    def __init__(self, name, t):
        self.name = name
        self.t = t
        self.w = None
        self.r = {}
        self.dsid = None
        self.dcnt = 0

    def __getitem__(self, idx):
        return self.t[idx]


class Sched:
    ENG = ("pe", "act", "dve", "pool", "sp")

    def __init__(self, nc, es):
        self.nc = nc
        self.es = es
        self.sems = []
        self.final = []
        self.e = {}
        hs = {"pe": nc.tensor, "act": nc.scalar, "dve": nc.vector, "pool": nc.gpsimd, "sp": nc.sync}
        for nm in self.ENG:
            sid = self._newsem("e_" + nm)
            self.e[nm] = {"h": hs[nm], "sid": sid, "cnt": 0, "seen": {}}
        self.bufs = []
        self.nwait = 0
        self.ninst = 0

    def _newsem(self, name):
        h = self.es.enter_context(self.nc.semaphore(name))
        self.sems.append(h)
        self.final.append(0)
        return len(self.sems) - 1

    def sb(self, name, shape, dt=F32, es=None):
        self.uid = getattr(self, "uid", 0) + 1
        name = f"{name}_{self.uid}"
        t = (es or self.es).enter_context(self.nc.sbuf_tensor(name, list(shape), dt))
        b = Buf(name, t)
        self.bufs.append(b)
        return b

    def ps(self, name, shape, dt=F32, es=None):
        self.uid = getattr(self, "uid", 0) + 1
        name = f"{name}_{self.uid}"
        t = (es or self.es).enter_context(self.nc.psum_tensor(name, list(shape), dt))
        b = Buf(name, t)
        self.bufs.append(b)
        return b

    def wrap(self, name, t):
        b = Buf(name, t)
        self.bufs.append(b)
        return b

    def _collect(self, reads, writes):
        deps = {}
        for b in reads:
            if b.w is not None:
                s, v = b.w
                if deps.get(s, 0) < v:
                    deps[s] = v
        for b in writes:
            if b.w is not None:
                s, v = b.w
                if deps.get(s, 0) < v:
                    deps[s] = v
            for s, v in b.r.items():
                if deps.get(s, 0) < v:
                    deps[s] = v
        return deps

    def _wait(self, eng, deps):
        E = self.e[eng]
        for s, v in deps.items():
            if eng == "pe" and s == E["sid"]:
                continue
            if E["seen"].get(s, 0) >= v:
                continue
            E["h"].wait_ge(self.sems[s], v)
            E["seen"][s] = v
            self.nwait += 1

    def op(self, eng, fn, reads=(), writes=()):
        E = self.e[eng]
        self._wait(eng, self._collect(reads, writes))
        inst = fn(E["h"])
        E["cnt"] += 1
        inst.then_inc(self.sems[E["sid"]], 1)
        self.final[E["sid"]] = E["cnt"]
        ev = (E["sid"], E["cnt"])
        for b in reads:
            if b.r.get(ev[0], 0) < ev[1]:
                b.r[ev[0]] = ev[1]
        for b in writes:
            b.w = ev
            b.r = {}
        self.ninst += 1
        return inst

    def dma(self, eng, pairs, semb, reads=(), writes=(), **kw):
        E = self.e[eng]
        self._wait(eng, self._collect(reads, writes))
        if semb.dsid is None:
            semb.dsid = self._newsem("d_" + semb.name)
        for (o, i) in pairs:
            E["h"].dma_start(out=o, in_=i, **kw).then_inc(self.sems[semb.dsid], 16)
            semb.dcnt += 16
            self.ninst += 1
        self.final[semb.dsid] = semb.dcnt
        ev = (semb.dsid, semb.dcnt)
        for b in reads:
            if b.r.get(ev[0], 0) < ev[1]:
                b.r[ev[0]] = ev[1]
        for b in writes:
            b.w = ev
            b.r = {}

    def barrier(self, engs=None):
        deps = {s: v for s, v in enumerate(self.final) if v > 0}
        for nm in (engs or self.ENG):
            self._wait(nm, deps)
        for b in self.bufs:
            b.w = None
            b.r = {}


D = 1024
NTOK = 4352
NT = NTOK // 128
DFF = 2816
NF = DFF // 128
EPS = 1e-6
RES_W = (0.5, 1.0, 0.5)

PARAM_SPECS = [
    ("xs", [NTOK, D]), ("cond", [2, D]),
    ("w_mod", [2, D, 9 * D]), ("b_mod", [2, 9 * D]), ("norm_pre", [2, 3, D]), ("norm_post", [2, 3, D]),
    ("ffn_w_in", [2, 2, D, 2 * DFF]), ("ffn_w_out", [2, 2, DFF, D]),
    ("ab_w_in", [D, 1792]), ("ab_w_out", [D, D]), ("sgu_norm_g", [768]), ("sgu_w", [6, 128, 128]),
    ("sgu_b", [6, 128]), ("s5_lam_re", [2, 16, 64]), ("s5_lam_im", [2, 16, 64]), ("s5_log_step", [2, 16]),
    ("s5_b_re", [2, 16, 64, 16]), ("s5_b_im", [2, 16, 64, 16]), ("s5_c_re", [2, 16, 16, 64]),
    ("s5_c_im", [2, 16, 16, 64]), ("s5_d", [256]), ("s5_glu_w", [256, 256]), ("s5_glu_b", [256]),
    ("attn_w_qkv", [D, 1536]), ("attn_w_out", [D, D]), ("attn_q_norm", [128]), ("attn_k_norm", [128]),
]


class K:
    pass


def setup_globals(k):
    S, nc = k.S, k.nc
    k.ii = S.sb("ii", [128, 128], I32)
    k.identf = S.sb("identf", [128, 128], F32)
    k.identb = S.sb("identb", [128, 128], BF16)
    S.op("pool", lambda h: h.iota(k.ii[:], pattern=[[1, 128]], base=0, channel_multiplier=-1), writes=[k.ii])
    S.op("dve", lambda h: h.tensor_single_scalar(out=k.identf[:], in_=k.ii[:], scalar=0, op=ALU.is_equal),
         reads=[k.ii], writes=[k.identf])
    S.op("dve", lambda h: h.tensor_copy(out=k.identb[:], in_=k.identf[:]), reads=[k.identf], writes=[k.identb])
    k.acol = S.sb("acol", [128, 2, 3, 2, 8], F32)
    k.bcol = S.sb("bcol", [128, 2, 3, 2, 8], F32)


def phase_mod(k):
    S, nc, d = k.S, k.nc, k.d
    with ExitStack() as es:
        condT = S.sb("condT", [128, 8, 2], F32, es)
        gpre = S.sb("gpre", [128, 2, 3, 8], F32, es)
        modrow = S.sb("modrow", [2, 9 * D], F32, es)
        bmod2 = S.sb("bmod2", [2, 9 * D], F32, es)
        gp2 = S.sb("gp2", [2, 3, D], F32, es)
        grow = S.sb("grow", [2, 3, D], F32, es)
        wm = [S.sb(f"wm{q}", [128, 8, 512], F32, es) for q in range(2)]
        pm = [S.ps(f"pm{q}", [128, 512], F32, es) for q in range(2)]
        ptr = S.ps("ptrm", [128, 144], F32, es)
        modcol = S.sb("modcol", [128, 72, 2], F32, es)
        tmpc = S.sb("tmpc", [128, 8], F32, es)
        S.dma("sp", [(condT[:, kc, :], d["cond"][:, kc * 128:(kc + 1) * 128].rearrange("r p -> p r"))
                     for kc in range(8)], condT, writes=[condT], allow_slow_non_contiguous=True)
        S.op("act", lambda h: h.activation(out=condT[:], in_=condT[:], func=AF.Silu), reads=[condT], writes=[condT])
        S.dma("sp", [(gpre[:, i, j, :], d["norm_pre"][i, j, :].rearrange("(kc p) -> p kc", p=128))
                     for i in range(2) for j in range(3)], gpre, writes=[gpre], allow_slow_non_contiguous=True)
        for i in range(2):
            S.dma("sp", [(bmod2[r:r + 1, :], d["b_mod"][i:i + 1, :]) for r in range(2)], bmod2, writes=[bmod2])
            S.dma("sp", [(gp2[r:r + 1, :, :], d["norm_post"][i:i + 1, :, :]) for r in range(2)], gp2, writes=[gp2])
            for n in range(18):
                w = wm[n % 2]
                p = pm[n % 2]
                S.dma("sp", [(w[:], d["w_mod"][i, :, n * 512:(n + 1) * 512].rearrange("(kc p) n -> p kc n", p=128))],
                      w, writes=[w])
                for kc in range(8):
                    S.op("pe", lambda h, kc=kc, w=w, p=p: h.matmul(p[0:2, :], lhsT=condT[:, kc, :], rhs=w[:, kc, :],
                                                                    start=(kc == 0), stop=(kc == 7)),
                         reads=[condT, w], writes=[p])
                S.op("dve", lambda h, p=p, n=n: h.tensor_tensor(out=modrow[:, n * 512:(n + 1) * 512], in0=p[0:2, :],
                                                               in1=bmod2[:, n * 512:(n + 1) * 512], op=ALU.add),
                     reads=[p, bmod2], writes=[modrow])
            for j in range(3):
                S.op("dve", lambda h, j=j: h.scalar_tensor_tensor(
                    out=grow[:, j, :], in0=modrow[:, (3 * j + 2) * D:(3 * j + 3) * D], scalar=float(RES_W[j]),
                    in1=gp2[:, j, :], op0=ALU.mult, op1=ALU.mult), reads=[modrow, gp2], writes=[grow])
            S.dma("sp", [(d["grow_d"][i, :, :, :].rearrange("j r n -> r j n"), grow[:])], grow, reads=[grow])
            for c in range(72):
                S.op("pe", lambda h, c=c: h.transpose(out=ptr[:, 2 * c:2 * c + 2], in_=modrow[0:2, c * 128:(c + 1) * 128],
                                                      identity=k.identf[0:2, 0:2]), reads=[modrow, k.identf], writes=[ptr])
            S.op("dve", lambda h: h.tensor_copy(out=modcol[:].rearrange("p c r -> p (c r)"), in_=ptr[:]), reads=[ptr], writes=[modcol])
            for j in range(3):
                for r in range(2):
                    S.op("dve", lambda h, j=j, r=r: h.scalar_tensor_tensor(
                        out=k.acol[:, i, j, r, :], in0=modcol[:, (3 * j + 1) * 8:(3 * j + 2) * 8, r], scalar=1.0,
                        in1=gpre[:, i, j, :], op0=ALU.add, op1=ALU.mult), reads=[modcol, gpre], writes=[k.acol])
                    S.op("dve", lambda h, j=j, r=r: h.tensor_copy(out=k.bcol[:, i, j, r, :], in_=modcol[:, (3 * j) * 8:(3 * j + 1) * 8, r]),
                         reads=[modcol], writes=[k.bcol])
        S.barrier()


def load_weight_cast(k, buf, dst_fn, src2d, nrows, ncols, colblk):
    S = k.S
    pairs = []
    for kc in range(nrows // 128):
        for c0 in range(0, ncols, colblk):
            c1 = min(ncols, c0 + colblk)
            pairs.append((dst_fn(kc, c0, c1), src2d[kc * 128:(kc + 1) * 128, c0:c1]))
    S.dma("pool", pairs, buf, writes=[buf])


def norm_prep(k, es_bufs, xt, tix, hT, i, j, r, T0):
    S = k.S
    junk, ssq, rst, ptr = es_bufs["junk"], es_bufs["ssq"], es_bufs["rst"], es_bufs["ptr"]
    S.op("act", lambda h: h.activation(out=junk[:], in_=xt[:], func=AF.Square, accum_out=ssq[:, 0:1]),
         reads=[xt], writes=[junk, ssq])
    S.op("act", lambda h: h.activation(out=rst[:, 0:1], in_=ssq[:, 0:1], func=AF.Sqrt, bias=es_bufs["epsc"][:, 0:1], scale=1.0 / D),
         reads=[ssq, es_bufs["epsc"]], writes=[rst])
    S.op("dve", lambda h: h.reciprocal(out=rst[:, 1:2], in_=rst[:, 0:1]), reads=[rst], writes=[rst])
    S.op("act", lambda h: h.activation(out=xt[:], in_=xt[:], func=AF.Identity, scale=rst[:, 1:2]), reads=[xt, rst], writes=[xt])
    for half in range(2):
        p = ptr[half]
        for q in range(4):
            kc = half * 4 + q
            S.op("pe", lambda h, kc=kc, q=q, p=p: h.transpose(out=p[:, q * 128:(q + 1) * 128], in_=xt[:, kc * 128:(kc + 1) * 128],
                                                               identity=k.identf[:]), reads=[xt, k.identf], writes=[p])
        for q in range(4):
            kc = half * 4 + q
            if q % 2 == 0:
                S.op("act", lambda h, kc=kc, q=q, p=p: h.activation(
                    out=hT[:, kc, T0:T0 + 128], in_=p[:, q * 128:(q + 1) * 128], func=AF.Identity,
                    scale=k.acol[:, i, j, r, kc:kc + 1], bias=k.bcol[:, i, j, r, kc:kc + 1]),
                    reads=[p, k.acol, k.bcol], writes=[hT])
            else:
                S.op("dve", lambda h, kc=kc, q=q, p=p: h.tensor_scalar(
                    out=hT[:, kc, T0:T0 + 128], in0=p[:, q * 128:(q + 1) * 128],
                    scalar1=k.acol[:, i, j, r, kc:kc + 1], scalar2=k.bcol[:, i, j, r, kc:kc + 1], op0=ALU.mult, op1=ALU.add),
                    reads=[p, k.acol, k.bcol], writes=[hT])


def post_res(k, eb, py, xr, gbc, dst_ap):
    S = k.S
    junk, ss2, rs2, tmp = eb["junk"], eb["ss2"], eb["rs2"], eb["tmp"]
    for n in range(2):
        S.op("act", lambda h, n=n: h.activation(out=junk[:, n * 512:(n + 1) * 512], in_=py[n][:], func=AF.Square,
                                                accum_out=ss2[:, n:n + 1]), reads=[py[n]], writes=[junk, ss2])
    S.op("dve", lambda h: h.tensor_scalar(out=rs2[:, 0:1], in0=ss2[:, 0:1], scalar1=ss2[:, 1:2], scalar2=1.0 / D,
                                          op0=ALU.add, op1=ALU.mult), reads=[ss2], writes=[rs2])
    S.op("act", lambda h: h.activation(out=rs2[:, 1:2], in_=rs2[:, 0:1], func=AF.Sqrt, bias=eb["epsc"][:, 0:1], scale=1.0),
         reads=[rs2, eb["epsc"]], writes=[rs2])
    S.op("dve", lambda h: h.reciprocal(out=rs2[:, 2:3], in_=rs2[:, 1:2]), reads=[rs2], writes=[rs2])
    for n in range(2):
        S.op("dve", lambda h, n=n: h.scalar_tensor_tensor(out=tmp[:, n * 512:(n + 1) * 512], in0=py[n][:], scalar=rs2[:, 2:3],
                                                          in1=gbc[:, n * 512:(n + 1) * 512], op0=ALU.mult, op1=ALU.mult),
             reads=[py[n], rs2, gbc], writes=[tmp])
    S.op("pool", lambda h: h.tensor_tensor(out=xr[:], in0=tmp[:], in1=xr[:], op=ALU.add), reads=[tmp, xr], writes=[xr])
    S.dma("sp", [(dst_ap, xr[:])], xr, reads=[xr])


def mk_groups(tiles):
    gs = []
    ctx = [t for t in tiles if t < 2]
    lat = [t for t in tiles if t >= 2]
    if ctx:
        gs.append(ctx)
    for a in range(0, len(lat), 4):
        gs.append(lat[a:a + 4])
    return gs


def load_gbc(k, es, i, j):
    S, d = k.S, k.d
    g = []
    for r in range(2):
        b = S.sb(f"gbc{r}", [128, D], F32, es)
        S.dma("sp", [(b[:], d["grow_d"][i, j, r:r + 1, :].partition_broadcast(128))], b, writes=[b])
        g.append(b)
    return g


def phase_ffn(k, i, w, src, dst, tiles, dst_off=0):
    S, nc, d = k.S, k.nc, k.d
    j = 0 if w == 0 else 2
    with ExitStack() as es:
        win = S.sb("win", [128, 8, 2 * DFF], BF16, es)
        wout = S.sb("wout", [128, NF, D], BF16, es)
        load_weight_cast(k, win, lambda kc, c0, c1: win[:, kc, c0:c1], d["ffn_w_in"][i, w], D, 2 * DFF, 1408)
        load_weight_cast(k, wout, lambda kc, c0, c1: wout[:, kc, c0:c1], d["ffn_w_out"][i, w], DFF, D, 1024)
        gbc = load_gbc(k, es, i, j)
        hT = S.sb("hT", [128, 8, 512], BF16, es)
        act = [S.sb(f"act{f}", [128, 512], BF16, es) for f in range(NF)]
        xts = [S.sb(f"xt{q}", [128, D], F32, es) for q in range(2)]
        xrs = [S.sb(f"xr{q}", [128, D], F32, es) for q in range(2)]
        sg = [S.sb(f"sg{q}", [128, 512], BF16, es) for q in range(2)]
        eb = {"junk": S.sb("junk", [128, D], BF16, es), "ssq": S.sb("ssq", [128, 2], F32, es),
              "rst": S.sb("rst", [128, 2], F32, es), "ss2": S.sb("ss2", [128, 2], F32, es),
              "rs2": S.sb("rs2", [128, 4], F32, es), "tmp": S.sb("tmp", [128, D], F32, es),
              "epsc": S.sb("epsc", [128, 1], F32, es),
              "ptr": [S.ps(f"ptr{q}", [128, 512], F32, es) for q in range(2)]}
        S.op("dve", lambda h: h.memset(eb["epsc"][:], EPS), writes=[eb["epsc"]])
        pg = [S.ps(f"pg{q}", [128, 512], F32, es) for q in range(2)]
        pu = [S.ps(f"pu{q}", [128, 512], F32, es) for q in range(2)]
        py = [S.ps(f"py{q}", [128, 512], F32, es) for q in range(2)]
        groups = mk_groups(tiles)
        cnt = [0, 0]

        def prep(g):
            for ti, t in enumerate(groups[g]):
                xt = xts[cnt[0] % 2]
                cnt[0] += 1
                S.dma("sp", [(xt[:], src[t * 128:(t + 1) * 128, :])], xt, writes=[xt])
                norm_prep(k, eb, xt, ti, hT, i, j, 1 if t < 2 else 0, ti * 128)

        def stage_a(g):
            T = 128 * len(groups[g])
            for f in range(NF):
                for (pp, c0) in ((pg[f % 2], f * 128), (pu[f % 2], DFF + f * 128)):
                    for kc in range(8):
                        S.op("pe", lambda h, pp=pp, c0=c0, kc=kc: h.matmul(pp[:, 0:T], lhsT=win[:, kc, c0:c0 + 128], rhs=hT[:, kc, 0:T],
                                                                           start=(kc == 0), stop=(kc == 7)),
                             reads=[win, hT], writes=[pp])
                s = sg[f % 2]
                S.op("act", lambda h, s=s, f=f: h.activation(out=s[:, 0:T], in_=pg[f % 2][:, 0:T], func=AF.Silu),
                     reads=[pg[f % 2]], writes=[s])
                S.op("dve", lambda h, s=s, f=f: h.tensor_tensor(out=act[f][:, 0:T], in0=pu[f % 2][:, 0:T], in1=s[:, 0:T], op=ALU.mult),
                     reads=[pu[f % 2], s], writes=[act[f]])

        def stage_b(g):
            for ti, t in enumerate(groups[g]):
                xr = xrs[cnt[1] % 2]
                cnt[1] += 1
                S.dma("sp", [(xr[:], src[t * 128:(t + 1) * 128, :])], xr, writes=[xr])
                for n in range(2):
                    for f in range(NF):
                        S.op("pe", lambda h, n=n, f=f, ti=ti: h.matmul(py[n][:], lhsT=act[f][:, ti * 128:(ti + 1) * 128],
                                                                      rhs=wout[:, f, n * 512:(n + 1) * 512],
                                                                      start=(f == 0), stop=(f == NF - 1)),
                             reads=[act[f], wout], writes=[py[n]])
                to = t + dst_off
                post_res(k, eb, py, xr, gbc[1 if t < 2 else 0], dst[to * 128:(to + 1) * 128, :])

        prep(0)
        for g in range(len(groups)):
            stage_a(g)
            if g + 1 < len(groups):
                prep(g + 1)
            stage_b(g)
        S.barrier()


ATT_SCALE = 128 ** -0.5
TWO_PI_SAFE = 6.283184
MAGIC = 12582912.0


def rope_prep(k, es):
    S = k.S
    t = {}
    pid = S.sb("pid", [128, 1], I32, es)
    pf = S.sb("pf", [128, 4], F32, es)
    invf = S.sb("invf", [128, 32], F32, es)
    S.op("pool", lambda h: h.iota(pid[:], pattern=[[0, 1]], base=0, channel_multiplier=1), writes=[pid])
    S.op("dve", lambda h: h.tensor_copy(out=pf[:, 0:1], in_=pid[:]), reads=[pid], writes=[pf])
    S.op("dve", lambda h: h.tensor_single_scalar(out=pf[:, 1:2], in_=pf[:, 0:1], scalar=64.0, op=ALU.is_ge), reads=[pf], writes=[pf])
    S.op("dve", lambda h: h.scalar_tensor_tensor(out=pf[:, 2:3], in0=pf[:, 1:2], scalar=-64.0, in1=pf[:, 0:1], op0=ALU.mult, op1=ALU.add),
         reads=[pf], writes=[pf])
    for q in range(32):
        S.op("dve", lambda h, q=q: h.memset(invf[:, q:q + 1], float(10000.0 ** (-(2.0 * q) / 64.0))), writes=[invf])
    t["pf"], t["invf"] = pf, invf
    t["ang"] = S.sb("ang", [128, 32], F32, es)
    t["ang2"] = S.sb("ang2", [128, 32], F32, es)
    t["CT"] = S.sb("CT", [128, 4, 32], F32, es)
    t["ST"] = S.sb("ST", [128, 4, 32], F32, es)
    t["rowpos"] = S.sb("rowpos", [128, 1], F32, es)
    return t


def rope_sincos(k, t, pos_ap, pos_buf, ax):
    S = k.S
    ang, ang2, CT, ST = t["ang"], t["ang2"], t["CT"], t["ST"]
    S.op("dve", lambda h: h.tensor_scalar(out=ang[:], in0=t["invf"][:], scalar1=pos_ap, scalar2=1.0 / (2 * np.pi), op0=ALU.mult, op1=ALU.mult),
         reads=[t["invf"], pos_buf], writes=[ang])
    S.op("dve", lambda h: h.tensor_scalar(out=ang2[:], in0=ang[:], scalar1=MAGIC, scalar2=-MAGIC, op0=ALU.add, op1=ALU.add), reads=[ang], writes=[ang2])
    S.op("dve", lambda h: h.tensor_tensor(out=ang[:], in0=ang[:], in1=ang2[:], op=ALU.subtract), reads=[ang, ang2], writes=[ang])
    S.op("act", lambda h: h.activation(out=ST[:, 2 * ax + 1, :], in_=ang[:], func=AF.Sin, scale=TWO_PI_SAFE), reads=[ang], writes=[ST])
    S.op("act", lambda h: h.activation(out=ST[:, 2 * ax, :], in_=ang[:], func=AF.Sin, scale=-TWO_PI_SAFE), reads=[ang], writes=[ST])
    S.op("dve", lambda h: h.tensor_scalar(out=ang[:], in0=ang[:], scalar1=0.25, scalar2=None, op0=ALU.add), reads=[ang], writes=[ang])
    S.op("dve", lambda h: h.tensor_scalar(out=ang2[:], in0=ang[:], scalar1=MAGIC, scalar2=-MAGIC, op0=ALU.add, op1=ALU.add), reads=[ang], writes=[ang2])
    S.op("dve", lambda h: h.tensor_tensor(out=ang[:], in0=ang[:], in1=ang2[:], op=ALU.subtract), reads=[ang, ang2], writes=[ang])
    S.op("act", lambda h: h.activation(out=CT[:, 2 * ax, :], in_=ang[:], func=AF.Sin, scale=TWO_PI_SAFE), reads=[ang], writes=[CT])
    S.op("act", lambda h: h.activation(out=CT[:, 2 * ax + 1, :], in_=ang[:], func=AF.Sin, scale=TWO_PI_SAFE), reads=[ang], writes=[CT])


def phase_attn(k, src, dst):
    S, nc, d = k.S, k.nc, k.d
    i, j = 1, 1
    with ExitStack() as es:
        wo = S.sb("wo", [128, 8, D], BF16, es)
        load_weight_cast(k, wo, lambda kc, c0, c1: wo[:, kc, c0:c1], d["attn_w_out"], D, D, 1024)
        KT = S.sb("KT", [128, 2, NTOK], BF16, es)
        V = S.sb("V", [128, NT, 256], BF16, es)
        negm = S.sb("negm", [128, 4], F32, es)
        epsc = S.sb("epsc", [128, 1], F32, es)
        S.op("dve", lambda h: h.memset(epsc[:], EPS), writes=[epsc])
        with ExitStack() as e1:
            wqkv = S.sb("wqkv", [128, 8, 1536], BF16, e1)
            load_weight_cast(k, wqkv, lambda kc, c0, c1: wqkv[:, kc, c0:c1], d["attn_w_qkv"], D, 1536, 1536)
            gq = S.sb("gq", [128, 128], F32, e1)
            gk = S.sb("gk", [128, 128], F32, e1)
            S.dma("sp", [(gq[:], d["attn_q_norm"].rearrange("(o n) -> o n", o=1).partition_broadcast(128))], gq, writes=[gq])
            S.dma("sp", [(gk[:], d["attn_k_norm"].rearrange("(o n) -> o n", o=1).partition_broadcast(128))], gk, writes=[gk])
            S.op("dve", lambda h: h.tensor_reduce(out=negm[:, 0:1], in_=gq[:], axis=AX.X, op=ALU.max, apply_absolute_value=True), reads=[gq], writes=[negm])
            S.op("dve", lambda h: h.tensor_reduce(out=negm[:, 1:2], in_=gk[:], axis=AX.X, op=ALU.max, apply_absolute_value=True), reads=[gk], writes=[negm])
            S.op("dve", lambda h: h.scalar_tensor_tensor(out=negm[:, 2:3], in0=negm[:, 0:1], scalar=-float(128 ** 0.5), in1=negm[:, 1:2],
                                                         op0=ALU.mult, op1=ALU.mult), reads=[negm], writes=[negm])
            rt = rope_prep(k, e1)
            rope_sincos(k, rt, rt["pf"][:, 2:3], rt["pf"], 1)
            hT = S.sb("hT", [128, 8, 128], BF16, e1)
            xt = S.sb("xt", [128, D], F32, e1)
            eb = {"junk": S.sb("junk", [128, D], BF16, e1), "ssq": S.sb("ssq", [128, 2], F32, e1),
                  "rst": S.sb("rst", [128, 2], F32, e1), "epsc": epsc,
                  "ptr": [S.ps(f"ptr{q}", [128, 512], F32, e1) for q in range(2)]}
            pq = [S.ps(f"pq{q}", [128, 512], F32, e1) for q in range(3)]
            ptq = S.ps("ptq", [128, 1024], BF16, e1)
            sq = S.sb("sq", [128, 1280], F32, e1)
            ss = S.sb("ssh", [128, 10], F32, e1)
            rs = S.sb("rsh", [128, 10], F32, e1)
            qg = S.sb("qg", [128, 1280], F32, e1)
            t1 = S.sb("t1", [128, 1280], F32, e1)
            t2 = S.sb("t2", [128, 1280], F32, e1)
            qb = S.sb("qb", [128, 1280], BF16, e1)
            qTs = S.sb("qTs", [128, 8, 128], BF16, e1)
            for t in range(NT):
                lat = t >= 2
                S.dma("sp", [(xt[:], src[t * 128:(t + 1) * 128, :])], xt, writes=[xt])
                norm_prep(k, eb, xt, 0, hT, i, j, 0 if lat else 1, 0)
                for b in ((0, 1, 2) if lat else (2,)):
                    for kc in range(8):
                        S.op("pe", lambda h, b=b, kc=kc: h.matmul(pq[b][:], lhsT=hT[:, kc, :], rhs=wqkv[:, kc, b * 512:(b + 1) * 512],
                                                                   start=(kc == 0), stop=(kc == 7)), reads=[hT, wqkv], writes=[pq[b]])
                S.op("act", lambda h, t=t: h.copy(out=V[:, t, :], in_=pq[2][:, 256:512]), reads=[pq[2]], writes=[V])
                S.op("act", lambda h: h.activation(out=sq[:, 0:256], in_=pq[2][:, 0:256], func=AF.Square), reads=[pq[2]], writes=[sq])
                nh = 10 if lat else 2
                if lat:
                    for b in range(2):
                        S.op("act", lambda h, b=b: h.activation(out=sq[:, 256 + b * 512:256 + (b + 1) * 512], in_=pq[b][:], func=AF.Square),
                             reads=[pq[b]], writes=[sq])
                W = nh * 128
                S.op("dve", lambda h: h.tensor_reduce(out=ss[:, 0:nh], in_=sq[:, 0:W].rearrange("p (h c) -> p h c", c=128), axis=AX.X, op=ALU.add),
                     reads=[sq], writes=[ss])
                S.op("act", lambda h: h.activation(out=rs[:, 0:nh], in_=ss[:, 0:nh], func=AF.Sqrt, bias=epsc[:, 0:1], scale=1.0 / 128), reads=[ss, epsc], writes=[rs])
                S.op("dve", lambda h: h.reciprocal(out=ss[:, 0:nh], in_=rs[:, 0:nh]), reads=[rs], writes=[ss])
                S.op("dve", lambda h: h.tensor_tensor(out=qg[:, 0:256].rearrange("p (h c) -> p h c", c=128), in0=pq[2][:, 0:256].rearrange("p (h c) -> p h c", c=128),
                                                      in1=ss[:, 0:2].unsqueeze(2).to_broadcast([128, 2, 128]), op=ALU.mult), reads=[pq[2], ss], writes=[qg])
                S.op("pool", lambda h: h.tensor_tensor(out=qg[:, 0:256].rearrange("p (h c) -> p h c", c=128), in0=qg[:, 0:256].rearrange("p (h c) -> p h c", c=128),
                                                       in1=gk[:].unsqueeze(1).to_broadcast([128, 2, 128]), op=ALU.mult), reads=[qg, gk], writes=[qg])
                if lat:
                    for b in range(2):
                        S.op("dve", lambda h, b=b: h.tensor_tensor(out=qg[:, 256 + b * 512:256 + (b + 1) * 512].rearrange("p (h c) -> p h c", c=128),
                                                                   in0=pq[b][:].rearrange("p (h c) -> p h c", c=128),
                                                                   in1=ss[:, 2 + 4 * b:6 + 4 * b].unsqueeze(2).to_broadcast([128, 4, 128]), op=ALU.mult),
                             reads=[pq[b], ss], writes=[qg])
                    S.op("pool", lambda h: h.tensor_tensor(out=qg[:, 256:1280].rearrange("p (h c) -> p h c", c=128), in0=qg[:, 256:1280].rearrange("p (h c) -> p h c", c=128),
                                                           in1=gq[:].unsqueeze(1).to_broadcast([128, 8, 128]), op=ALU.mult), reads=[qg, gq], writes=[qg])
                    l = t - 2
                    S.op("dve", lambda h, l=l: h.tensor_scalar(out=rt["rowpos"][:], in0=rt["pf"][:, 1:2], scalar1=float(2 * l), scalar2=None, op0=ALU.add),
                         reads=[rt["pf"]], writes=[rt["rowpos"]])
                    rope_sincos(k, rt, rt["rowpos"][:, 0:1], rt["rowpos"], 0)
                    CTf = rt["CT"][:].rearrange("p b c -> p (b c)")
                    S.op("dve", lambda h: h.tensor_tensor(out=t1[:].rearrange("p (h c) -> p h c", c=128), in0=qg[:].rearrange("p (h c) -> p h c", c=128),
                                                          in1=CTf.unsqueeze(1).to_broadcast([128, 10, 128]), op=ALU.mult), reads=[qg, rt["CT"]], writes=[t1])
                    for ax in range(2):
                        qv = qg[:].rearrange("p (h a s c) -> p h a s c", a=2, s=2, c=32)[:, :, ax, ::-1, :]
                        tv = t2[:].rearrange("p (h a s c) -> p h a s c", a=2, s=2, c=32)[:, :, ax, :, :]
                        sv = rt["ST"][:, 2 * ax:2 * ax + 2, :].unsqueeze(1).to_broadcast([128, 10, 2, 32])
                        S.op("pool", lambda h, qv=qv, tv=tv, sv=sv: h.tensor_tensor(out=tv, in0=qv, in1=sv, op=ALU.mult), reads=[qg, rt["ST"]], writes=[t2])
                    S.op("dve", lambda h: h.tensor_tensor(out=qb[:], in0=t1[:], in1=t2[:], op=ALU.add), reads=[t1, t2], writes=[qb])
                else:
                    S.op("dve", lambda h: h.tensor_copy(out=qb[:, 0:256], in_=qg[:, 0:256]), reads=[qg], writes=[qb])
                for kh in range(2):
                    S.op("pe", lambda h, kh=kh: h.transpose(out=ptq[:, kh * 128:(kh + 1) * 128], in_=qb[:, kh * 128:(kh + 1) * 128], identity=k.identb[:]),
                         reads=[qb, k.identb], writes=[ptq])
                S.op("act", lambda h, t=t: h.copy(out=KT[:, :, t * 128:(t + 1) * 128], in_=ptq[:, 0:256].rearrange("p (h c) -> p h c", c=128)), reads=[ptq], writes=[KT])
                if lat:
                    for hh in range(8):
                        S.op("pe", lambda h, hh=hh: h.transpose(out=ptq[:, hh * 128:(hh + 1) * 128], in_=qb[:, 256 + hh * 128:256 + (hh + 1) * 128], identity=k.identb[:]),
                             reads=[qb, k.identb], writes=[ptq])
                    S.op("dve", lambda h: h.tensor_copy(out=qTs[:].rearrange("p h c -> p (h c)"), in_=ptq[:]), reads=[ptq], writes=[qTs])
                    S.dma("sp", [(d["qT_d"][t - 2], qTs[:])], qTs, reads=[qTs])
        S.barrier()
        with ExitStack() as e2:
            gbc = load_gbc(k, e2, i, j)[0]
            P = [S.sb(f"P{q}", [128, NTOK], BF16, e2) for q in range(4)]
            PT = S.sb("PT", [128, NT, 512], BF16, e2)
            qT = [S.sb(f"qT{q}", [128, 8, 128], BF16, e2) for q in range(2)]
            OT = S.sb("OT", [128, 512], BF16, e2)
            acc = [S.sb(f"acc{q}", [128, 512], F32, e2) for q in range(2)]
            xrs = [S.sb(f"xr{q}", [128, D], F32, e2) for q in range(2)]
            rsum = S.sb("rsum", [128, 8, 5], F32, e2)
            rinv = S.sb("rinv", [128, 8], F32, e2)
            eb = {"junk": S.sb("junk", [128, D], BF16, e2), "ss2": S.sb("ss2", [128, 2], F32, e2),
                  "rs2": S.sb("rs2", [128, 4], F32, e2), "tmp": S.sb("tmp", [128, D], F32, e2), "epsc": epsc}
            pS = [S.ps(f"pS{q}", [128, 1024], F32, e2) for q in range(2)]
            pT = [S.ps(f"pT{q}", [128, 1024], BF16, e2) for q in range(2)]
            pO = S.ps("pO", [128, 512], F32, e2)
            py = S.ps("py", [128, 512], F32, e2)
            blocks = [(0, 1024), (1024, 1024), (2048, 1024), (3072, 1024), (4096, 256)]
            sc = [0]
            for l in range(32):
                q_ = qT[l % 2]
                xr = xrs[l % 2]
                S.dma("sp", [(q_[:], d["qT_d"][l])], q_, writes=[q_])
                S.dma("sp", [(xr[:], src[(l + 2) * 128:(l + 3) * 128, :])], xr, writes=[xr])
                for kvh in range(2):
                    for hl in range(4):
                        hh = kvh * 4 + hl
                        Pb = P[hl]
                        for bi, (k0, kn) in enumerate(blocks):
                            ps = pS[sc[0] % 2]
                            sc[0] += 1
                            for c0 in range(0, kn, 512):
                                cn = min(512, kn - c0)
                                S.op("pe", lambda h, ps=ps, c0=c0, cn=cn, k0=k0, hh=hh: h.matmul(
                                    ps[:, c0:c0 + cn], lhsT=q_[:, hh, :], rhs=KT[:, kvh, k0 + c0:k0 + c0 + cn], start=True, stop=True),
                                    reads=[q_, KT], writes=[ps])
                            S.op("act", lambda h, ps=ps, k0=k0, kn=kn, bi=bi, hh=hh, Pb=Pb: h.activation(
                                out=Pb[:, k0:k0 + kn], in_=ps[:, 0:kn], func=AF.Exp, scale=ATT_SCALE, bias=negm[:, 2:3],
                                accum_out=rsum[:, hh, bi:bi + 1]), reads=[ps, negm], writes=[Pb, rsum])
                        for kt0 in range(0, NT, 8):
                            kn8 = min(8, NT - kt0)
                            pt = pT[sc[0] % 2]
                            sc[0] += 1
                            for q in range(kn8):
                                S.op("pe", lambda h, pt=pt, q=q, kt0=kt0, Pb=Pb: h.transpose(
                                    out=pt[:, q * 128:(q + 1) * 128], in_=Pb[:, (kt0 + q) * 128:(kt0 + q + 1) * 128], identity=k.identb[:]),
                                    reads=[Pb, k.identb], writes=[pt])
                            eng = "dve" if (kt0 // 8) % 2 == 0 else "act"
                            if eng == "dve":
                                S.op("dve", lambda h, pt=pt, kt0=kt0, kn8=kn8, hl=hl: h.tensor_copy(
                                    out=PT[:, kt0:kt0 + kn8, hl * 128:(hl + 1) * 128], in_=pt[:, 0:kn8 * 128].rearrange("p (a c) -> p a c", c=128)),
                                    reads=[pt], writes=[PT])
                            else:
                                S.op("act", lambda h, pt=pt, kt0=kt0, kn8=kn8, hl=hl: h.copy(
                                    out=PT[:, kt0:kt0 + kn8, hl * 128:(hl + 1) * 128], in_=pt[:, 0:kn8 * 128].rearrange("p (a c) -> p a c", c=128)),
                                    reads=[pt], writes=[PT])
                    S.op("dve", lambda h, kvh=kvh: h.tensor_reduce(out=rinv[:, kvh * 4:kvh * 4 + 4], in_=rsum[:, kvh * 4:kvh * 4 + 4, :], axis=AX.X, op=ALU.add),
                         reads=[rsum], writes=[rinv])
                    S.op("dve", lambda h, kvh=kvh: h.reciprocal(out=rinv[:, kvh * 4:kvh * 4 + 4], in_=rinv[:, kvh * 4:kvh * 4 + 4]), reads=[rinv], writes=[rinv])
                    for kt in range(NT):
                        S.op("pe", lambda h, kt=kt, kvh=kvh: h.matmul(pO[:], lhsT=V[:, kt, kvh * 128:(kvh + 1) * 128], rhs=PT[:, kt, :],
                                                                      start=(kt == 0), stop=(kt == NT - 1)), reads=[V, PT], writes=[pO])
                    S.op("act", lambda h: h.copy(out=OT[:], in_=pO[:]), reads=[pO], writes=[OT])
                    for hl in range(4):
                        hh = kvh * 4 + hl
                        for n in range(2):
                            S.op("pe", lambda h, hl=hl, hh=hh, n=n: h.matmul(py[:], lhsT=OT[:, hl * 128:(hl + 1) * 128], rhs=wo[:, hh, n * 512:(n + 1) * 512],
                                                                            start=True, stop=True), reads=[OT, wo], writes=[py])
                            if hh == 0:
                                S.op("dve", lambda h, n=n, hh=hh: h.tensor_scalar(out=acc[n][:], in0=py[:], scalar1=rinv[:, hh:hh + 1], scalar2=None, op0=ALU.mult),
                                     reads=[py, rinv], writes=[acc[n]])
                            else:
                                S.op("dve", lambda h, n=n, hh=hh: h.scalar_tensor_tensor(out=acc[n][:], in0=py[:], scalar=rinv[:, hh:hh + 1], in1=acc[n][:],
                                                                                         op0=ALU.mult, op1=ALU.add), reads=[py, rinv, acc[n]], writes=[acc[n]])
                post_res(k, eb, acc, xr, gbc, dst[(l + 2) * 128:(l + 3) * 128, :])
        S.barrier()


S5_CHUNKS = [(0, 256)] + [(256 + 512 * q, 512) for q in range(8)]


def tab(T, dd, c0, cn):
    if dd == 0:
        return T[:, c0:c0 + cn]
    hi = (255 - c0) if c0 < 256 else (4607 - c0)
    lo = hi - cn
    return T[:, hi:(lo if lo >= 0 else None):-1]


def s5_params(k, es):
    S, d = k.S, k.d
    P = {}

    def col(name, n=16):
        return S.sb(name, [128, n], F32, es)
    lr, li, ls = col("lr"), col("li"), col("ls")
    S.dma("sp", [(lr[:, dd * 8:(dd + 1) * 8], d["s5_lam_re"][dd].rearrange("g p -> (g p)").rearrange("(ct q) -> q ct", q=128)) for dd in range(2)],
          lr, writes=[lr], allow_slow_non_contiguous=True)
    S.dma("sp", [(li[:, dd * 8:(dd + 1) * 8], d["s5_lam_im"][dd].rearrange("g p -> (g p)").rearrange("(ct q) -> q ct", q=128)) for dd in range(2)],
          li, writes=[li], allow_slow_non_contiguous=True)
    S.dma("sp", [(ls[gl * 64:(gl + 1) * 64, dd * 8:(dd + 1) * 8],
                  d["s5_log_step"][dd].rearrange("(ct gl) -> gl ct", gl=2)[gl:gl + 1, :].partition_broadcast(64))
                 for dd in range(2) for gl in range(2)], ls, writes=[ls], allow_slow_non_contiguous=True)
    dt, zr, zi, em1, er = col("dt"), col("zr"), col("zi"), col("em1"), col("er")
    S.op("act", lambda h: h.activation(out=dt[:], in_=ls[:], func=AF.Exp), reads=[ls], writes=[dt])
    S.op("dve", lambda h: h.tensor_tensor(out=zr[:], in0=lr[:], in1=dt[:], op=ALU.mult), reads=[lr, dt], writes=[zr])
    S.op("dve", lambda h: h.tensor_tensor(out=zi[:], in0=li[:], in1=dt[:], op=ALU.mult), reads=[li, dt], writes=[zi])
    S.op("dve", lambda h: h.tensor_scalar(out=em1[:], in0=zr[:], scalar1=1.0 / 6, scalar2=1.0, op0=ALU.mult, op1=ALU.add), reads=[zr], writes=[em1])
    for n in (5, 4, 3, 2):
        S.op("dve", lambda h: h.tensor_tensor(out=em1[:], in0=em1[:], in1=zr[:], op=ALU.mult), reads=[em1, zr], writes=[em1])
        S.op("dve", lambda h, n=n: h.tensor_scalar(out=em1[:], in0=em1[:], scalar1=1.0 / n, scalar2=1.0, op0=ALU.mult, op1=ALU.add), reads=[em1], writes=[em1])
    S.op("dve", lambda h: h.tensor_tensor(out=em1[:], in0=em1[:], in1=zr[:], op=ALU.mult), reads=[em1, zr], writes=[em1])
    S.op("dve", lambda h: h.tensor_scalar(out=er[:], in0=em1[:], scalar1=1.0, scalar2=None, op0=ALU.add), reads=[em1], writes=[er])
    a, a2, sh, sn, cs, cm1 = col("a"), col("a2"), col("sh"), col("sn"), col("cs"), col("cm1")

    def reduce_turns(scale):
        S.op("dve", lambda h: h.tensor_scalar(out=a[:], in0=zi[:], scalar1=scale, scalar2=None, op0=ALU.mult), reads=[zi], writes=[a])
        S.op("dve", lambda h: h.tensor_scalar(out=a2[:], in0=a[:], scalar1=MAGIC, scalar2=-MAGIC, op0=ALU.add, op1=ALU.add), reads=[a], writes=[a2])
        S.op("dve", lambda h: h.tensor_tensor(out=a[:], in0=a[:], in1=a2[:], op=ALU.subtract), reads=[a, a2], writes=[a])
    reduce_turns(1.0 / (4 * np.pi))
    S.op("act", lambda h: h.activation(out=sh[:], in_=a[:], func=AF.Sin, scale=TWO_PI_SAFE), reads=[a], writes=[sh])
    S.op("dve", lambda h: h.scalar_tensor_tensor(out=cm1[:], in0=sh[:], scalar=-2.0, in1=sh[:], op0=ALU.mult, op1=ALU.mult), reads=[sh], writes=[cm1])
    reduce_turns(1.0 / (2 * np.pi))
    S.op("act", lambda h: h.activation(out=sn[:], in_=a[:], func=AF.Sin, scale=TWO_PI_SAFE), reads=[a], writes=[sn])
    S.op("dve", lambda h: h.tensor_scalar(out=cs[:], in0=cm1[:], scalar1=1.0, scalar2=None, op0=ALU.add), reads=[cm1], writes=[cs])
    l1r, l1i, den, qr, qi, t0 = col("l1r"), col("l1i"), col("den"), col("qr"), col("qi"), col("t0")
    S.op("dve", lambda h: h.tensor_tensor(out=l1r[:], in0=em1[:], in1=cs[:], op=ALU.mult), reads=[em1, cs], writes=[l1r])
    S.op("dve", lambda h: h.tensor_tensor(out=l1r[:], in0=l1r[:], in1=cm1[:], op=ALU.add), reads=[l1r, cm1], writes=[l1r])
    S.op("dve", lambda h: h.tensor_tensor(out=l1i[:], in0=er[:], in1=sn[:], op=ALU.mult), reads=[er, sn], writes=[l1i])
    S.op("dve", lambda h: h.tensor_tensor(out=den[:], in0=lr[:], in1=lr[:], op=ALU.mult), reads=[lr], writes=[den])
    S.op("dve", lambda h: h.tensor_tensor(out=t0[:], in0=li[:], in1=li[:], op=ALU.mult), reads=[li], writes=[t0])
    S.op("dve", lambda h: h.tensor_tensor(out=den[:], in0=den[:], in1=t0[:], op=ALU.add), reads=[den, t0], writes=[den])
    S.op("dve", lambda h: h.reciprocal(out=den[:], in_=den[:]), reads=[den], writes=[den])
    S.op("dve", lambda h: h.tensor_tensor(out=qr[:], in0=l1r[:], in1=lr[:], op=ALU.mult), reads=[l1r, lr], writes=[qr])
    S.op("dve", lambda h: h.tensor_tensor(out=t0[:], in0=l1i[:], in1=li[:], op=ALU.mult), reads=[l1i, li], writes=[t0])
    S.op("dve", lambda h: h.tensor_tensor(out=qr[:], in0=qr[:], in1=t0[:], op=ALU.add), reads=[qr, t0], writes=[qr])
    S.op("dve", lambda h: h.tensor_tensor(out=qr[:], in0=qr[:], in1=den[:], op=ALU.mult), reads=[qr, den], writes=[qr])
    S.op("dve", lambda h: h.tensor_tensor(out=qi[:], in0=l1i[:], in1=lr[:], op=ALU.mult), reads=[l1i, lr], writes=[qi])
    S.op("dve", lambda h: h.tensor_tensor(out=t0[:], in0=l1r[:], in1=li[:], op=ALU.mult), reads=[l1r, li], writes=[t0])
    S.op("dve", lambda h: h.tensor_tensor(out=qi[:], in0=qi[:], in1=t0[:], op=ALU.subtract), reads=[qi, t0], writes=[qi])
    S.op("dve", lambda h: h.tensor_tensor(out=qi[:], in0=qi[:], in1=den[:], op=ALU.mult), reads=[qi, den], writes=[qi])
    wr = S.sb("wr", [128, 13, 16], F32, es)
    wi = S.sb("wi", [128, 13, 16], F32, es)
    S.op("dve", lambda h: h.tensor_tensor(out=t0[:], in0=cs[:], in1=cs[:], op=ALU.mult), reads=[cs], writes=[t0])
    S.op("dve", lambda h: h.tensor_tensor(out=a[:], in0=sn[:], in1=sn[:], op=ALU.mult), reads=[sn], writes=[a])
    S.op("dve", lambda h: h.tensor_tensor(out=t0[:], in0=t0[:], in1=a[:], op=ALU.add), reads=[t0, a], writes=[t0])
    S.op("act", lambda h: h.activation(out=t0[:], in_=t0[:], func=AF.Sqrt), reads=[t0], writes=[t0])
    S.op("dve", lambda h: h.reciprocal(out=t0[:], in_=t0[:]), reads=[t0], writes=[t0])
    S.op("dve", lambda h: h.tensor_tensor(out=wr[:, 0, :], in0=cs[:], in1=t0[:], op=ALU.mult), reads=[cs, t0], writes=[wr])
    S.op("dve", lambda h: h.tensor_tensor(out=wi[:, 0, :], in0=sn[:], in1=t0[:], op=ALU.mult), reads=[sn, t0], writes=[wi])
    for lv in range(12):
        S.op("dve", lambda h, lv=lv: h.tensor_tensor(out=a[:], in0=wr[:, lv, :], in1=wr[:, lv, :], op=ALU.mult), reads=[wr], writes=[a])
        S.op("dve", lambda h, lv=lv: h.tensor_tensor(out=a2[:], in0=wi[:, lv, :], in1=wi[:, lv, :], op=ALU.mult), reads=[wi], writes=[a2])
        S.op("dve", lambda h, lv=lv: h.scalar_tensor_tensor(out=wi[:, lv + 1, :], in0=wr[:, lv, :], scalar=2.0, in1=wi[:, lv, :], op0=ALU.mult, op1=ALU.mult),
             reads=[wr, wi], writes=[wi])
        S.op("dve", lambda h, lv=lv: h.tensor_tensor(out=wr[:, lv + 1, :], in0=a[:], in1=a2[:], op=ALU.subtract), reads=[a, a2], writes=[wr])
    P.update(er=er, qr=qr, qi=qi, wr=wr, wi=wi)
    P["bre"] = S.sb("bre", [128, 16, 16], F32, es)
    P["bim"] = S.sb("bim", [128, 16, 16], F32, es)
    for nm, key in (("bre", "s5_b_re"), ("bim", "s5_b_im")):
        S.dma("sp", [(P[nm][:, dd * 8:(dd + 1) * 8, :], d[key][dd].rearrange("g p c -> (g p) c").rearrange("(ct q) c -> q ct c", q=128)) for dd in range(2)],
              P[nm], writes=[P[nm]])
    P["cf"] = {}
    for nm, key in (("cre", "s5_c_re"), ("cim", "s5_c_im")):
        b = S.sb(nm, [128, 4, 128], F32, es)
        S.dma("sp", [(b[:, dd * 2 + ft, h2 * 64:(h2 + 1) * 64], d[key][dd].rearrange("g c p -> (g c) p")[ft * 128:(ft + 1) * 128, :])
                     for dd in range(2) for ft in range(2) for h2 in range(2)], b, writes=[b])
        P["cf"][nm] = b
    P["dcol"] = S.sb("dcol", [128, 2], F32, es)
    P["gbcol"] = S.sb("gbcol", [128, 2], F32, es)
    S.dma("sp", [(P["dcol"][:], d["s5_d"].rearrange("(ft p) -> p ft", p=128))], P["dcol"], writes=[P["dcol"]], allow_slow_non_contiguous=True)
    S.dma("sp", [(P["gbcol"][:], d["s5_glu_b"].rearrange("(ft p) -> p ft", p=128))], P["gbcol"], writes=[P["gbcol"]], allow_slow_non_contiguous=True)
    return P


def phase_mix0(k, src, dst):
    S, nc, d = k.S, k.nc, k.d
    i, j = 0, 1
    with ExitStack() as es:
        epsc = S.sb("epsc", [128, 1], F32, es)
        S.op("dve", lambda h: h.memset(epsc[:], EPS), writes=[epsc])
        ybT = S.sb("ybT", [128, 2, NTOK], BF16, es)
        with ExitStack() as e1:
            win = S.sb("win", [128, 8, 1792], BF16, e1)
            load_weight_cast(k, win, lambda kc, c0, c1: win[:, kc, c0:c1], d["ab_w_in"], D, 1792, 1792)
            wsr = S.sb("wsr", [128, 6, 128], F32, e1)
            wsT = S.sb("wsT", [128, 6, 128], BF16, e1)
            S.dma("sp", [(wsr[:], d["sgu_w"].rearrange("g t s -> t g s"))], wsr, writes=[wsr])
            sbc = S.sb("sbc", [128, 6], F32, e1)
            S.dma("sp", [(sbc[:], d["sgu_b"].rearrange("g t -> t g"))], sbc, writes=[sbc], allow_slow_non_contiguous=True)
            gsb = S.sb("gsb", [128, 768], F32, e1)
            S.dma("sp", [(gsb[:], d["sgu_norm_g"].rearrange("(o n) -> o n", o=1).partition_broadcast(128))], gsb, writes=[gsb])
            hT = S.sb("hT", [128, 8, 128], BF16, e1)
            xt = S.sb("xt", [128, D], F32, e1)
            eb = {"junk": S.sb("junk", [128, D], BF16, e1), "ssq": S.sb("ssq", [128, 2], F32, e1),
                  "rst": S.sb("rst", [128, 2], F32, e1), "epsc": epsc,
                  "ptr": [S.ps(f"ptr{q}", [128, 512], F32, e1) for q in range(2)]}
            pq = [S.ps(f"pq{q}", [128, 512], F32, e1) for q in range(3)]
            pmA = S.ps("pmA", [128, 512], F32, e1)
            pBt = e1.enter_context(nc.psum_tensor("pBshared", [128, 512], F32))
            pmB = S.wrap("pmB", pBt)
            puT = S.wrap("puT", pBt)
            pya = S.ps("pya", [128, 1024], BF16, e1)
            for g in range(6):
                S.op("pe", lambda h, g=g: h.transpose(out=eb["ptr"][0][:, 0:128], in_=wsr[:, g, :], identity=k.identf[:]), reads=[wsr, k.identf], writes=[eb["ptr"][0]])
                S.op("dve", lambda h, g=g: h.tensor_copy(out=wsT[:, g, :], in_=eb["ptr"][0][:, 0:128]), reads=[eb["ptr"][0]], writes=[wsT])
            ug = S.sb("ug", [128, 768], F32, e1)
            vg = S.sb("vg", [128, 768], F32, e1)
            vh = S.sb("vh", [128, 768], BF16, e1)
            st6 = S.sb("st6", [128, 6, 6], F32, e1)
            mv = S.sb("mv", [128, 6, 2], F32, e1)
            rsd = S.sb("rsd", [128, 6], F32, e1)
            nmr = S.sb("nmr", [128, 6], F32, e1)
            tma = S.sb("tma", [128, 768], F32, e1)
            yab = S.sb("yab", [128, 768], BF16, e1)
            yaTs = S.sb("yaTs", [128, 6, 128], BF16, e1)
            uTs = S.sb("uTs", [128, 2, 128], F32, e1)
            GEL = AF.Gelu_apprx_tanh
            for t in range(NT):
                S.dma("sp", [(xt[:], src[t * 128:(t + 1) * 128, :])], xt, writes=[xt])
                norm_prep(k, eb, xt, 0, hT, i, j, 1 if t < 2 else 0, 0)
                for b in range(3):
                    for kc in range(8):
                        S.op("pe", lambda h, b=b, kc=kc: h.matmul(pq[b][:], lhsT=hT[:, kc, :], rhs=win[:, kc, b * 512:(b + 1) * 512],
                                                                   start=(kc == 0), stop=(kc == 7)), reads=[hT, win], writes=[pq[b]])
                for ft in range(2):
                    for kc in range(8):
                        S.op("pe", lambda h, ft=ft, kc=kc: h.matmul(puT[:, 256 + ft * 128:256 + (ft + 1) * 128], lhsT=win[:, kc, 1536 + ft * 128:1536 + (ft + 1) * 128],
                                                                     rhs=hT[:, kc, :], start=(kc == 0), stop=(kc == 7)), reads=[hT, win], writes=[puT])
                S.op("dve", lambda h: h.tensor_copy(out=uTs[:].rearrange("p f c -> p (f c)"), in_=puT[:, 256:512]), reads=[puT], writes=[uTs])
                S.dma("sp", [(d["uT_d"][:, :, t * 128:(t + 1) * 128].rearrange("f p c -> p f c"), uTs[:])], uTs, reads=[uTs])
                S.op("act", lambda h: h.activation(out=ug[:, 0:512], in_=pq[0][:], func=GEL), reads=[pq[0]], writes=[ug])
                S.op("act", lambda h: h.activation(out=ug[:, 512:768], in_=pq[1][:, 0:256], func=GEL), reads=[pq[1]], writes=[ug])
                S.op("act", lambda h: h.activation(out=vg[:, 0:256], in_=pq[1][:, 256:512], func=GEL), reads=[pq[1]], writes=[vg])
                S.op("act", lambda h: h.activation(out=vg[:, 256:768], in_=pq[2][:], func=GEL), reads=[pq[2]], writes=[vg])
                for g in range(6):
                    S.op("dve", lambda h, g=g: h.bn_stats(out=st6[:, g, :], in_=vg[:, g * 128:(g + 1) * 128]), reads=[vg], writes=[st6])
                for g in range(6):
                    S.op("dve", lambda h, g=g: h.bn_aggr(out=mv[:, g, :], in_=st6[:, g, :]), reads=[st6], writes=[mv])
                S.op("act", lambda h: h.activation(out=rsd[:], in_=mv[:, :, 1], func=AF.Sqrt, bias=epsc[:, 0:1], scale=1.0), reads=[mv, epsc], writes=[rsd])
                S.op("dve", lambda h: h.reciprocal(out=rsd[:], in_=rsd[:]), reads=[rsd], writes=[rsd])
                S.op("dve", lambda h: h.scalar_tensor_tensor(out=nmr[:], in0=mv[:, :, 0], scalar=-1.0, in1=rsd[:], op0=ALU.mult, op1=ALU.mult), reads=[mv, rsd], writes=[nmr])
                for g in range(6):
                    if g % 2 == 0:
                        S.op("act", lambda h, g=g: h.activation(out=vh[:, g * 128:(g + 1) * 128], in_=vg[:, g * 128:(g + 1) * 128], func=AF.Identity,
                                                                 scale=rsd[:, g:g + 1], bias=nmr[:, g:g + 1]), reads=[vg, rsd, nmr], writes=[vh])
                    else:
                        S.op("dve", lambda h, g=g: h.tensor_scalar(out=vh[:, g * 128:(g + 1) * 128], in0=vg[:, g * 128:(g + 1) * 128],
                                                                   scalar1=rsd[:, g:g + 1], scalar2=nmr[:, g:g + 1], op0=ALU.mult, op1=ALU.add), reads=[vg, rsd, nmr], writes=[vh])
                for g in range(6):
                    pm, c0 = (pmA, g * 128) if g < 4 else (pmB, (g - 4) * 128)
                    S.op("pe", lambda h, g=g, pm=pm, c0=c0: h.matmul(pm[:, c0:c0 + 128], lhsT=wsT[:, g, :], rhs=vh[:, g * 128:(g + 1) * 128], start=True, stop=True),
                         reads=[wsT, vh], writes=[pm])
                S.op("dve", lambda h: h.tensor_tensor(out=tma[:, 0:512], in0=pmA[:], in1=gsb[:, 0:512], op=ALU.mult), reads=[pmA, gsb], writes=[tma])
                S.op("dve", lambda h: h.tensor_tensor(out=tma[:, 512:768], in0=pmB[:, 0:256], in1=gsb[:, 512:768], op=ALU.mult), reads=[pmB, gsb], writes=[tma])
                for g in range(6):
                    S.op("dve" if g % 2 else "pool", lambda h, g=g: (h.scalar_tensor_tensor(
                        out=yab[:, g * 128:(g + 1) * 128], in0=tma[:, g * 128:(g + 1) * 128], scalar=sbc[:, g:g + 1], in1=ug[:, g * 128:(g + 1) * 128],
                        op0=ALU.add, op1=ALU.mult)), reads=[tma, sbc, ug], writes=[yab]) if g % 2 else None
                    if g % 2 == 0:
                        S.op("dve", lambda h, g=g: h.scalar_tensor_tensor(
                            out=yab[:, g * 128:(g + 1) * 128], in0=tma[:, g * 128:(g + 1) * 128], scalar=sbc[:, g:g + 1], in1=ug[:, g * 128:(g + 1) * 128],
                            op0=ALU.add, op1=ALU.mult), reads=[tma, sbc, ug], writes=[yab])
                for g in range(6):
                    S.op("pe", lambda h, g=g: h.transpose(out=pya[:, g * 128:(g + 1) * 128], in_=yab[:, g * 128:(g + 1) * 128], identity=k.identb[:]),
                         reads=[yab, k.identb], writes=[pya])
                S.op("act", lambda h: h.copy(out=yaTs[:].rearrange("p g c -> p (g c)"), in_=pya[:, 0:768]), reads=[pya], writes=[yaTs])
                S.dma("sp", [(d["yaT_d"][t], yaTs[:])], yaTs, reads=[yaTs])
        S.barrier()
        with ExitStack() as e2:
            Pm = s5_params(k, e2)
            gluw = S.sb("gluw", [128, 2, 256], BF16, e2)
            load_weight_cast(k, gluw, lambda kc, c0, c1: gluw[:, kc, c0:c1], d["s5_glu_w"], 256, 256, 256)
            uT = S.sb("uT", [128, NTOK], F32, e2)
            yc = S.sb("yc", [128, NTOK], F32, e2)
            ygb = S.sb("ygb", [128, 2, NTOK], BF16, e2)
            Tc = S.sb("Tc", [128, NTOK], F32, e2)
            Ts = S.sb("Ts", [128, NTOK], F32, e2)
            bpr = S.sb("bpr", [128, NTOK], F32, e2)
            bpi = S.sb("bpi", [128, NTOK], F32, e2)
            gr, gi = bpr, bpi
            tw1 = S.sb("tw1", [128, 2048], F32, e2)
            tw2 = S.sb("tw2", [128, 2048], F32, e2)
            Bpad = [S.sb(f"Bpad{q}", [128, 128], F32, e2) for q in range(2)]
            Bl = [S.sb(f"Bl{q}", [128, 128], F32, e2) for q in range(2)]
            Cpad = [S.sb(f"Cpad{q}", [128, 128], F32, e2) for q in range(2)]
            cT = {nm: S.sb("cT" + nm, [128, 4, 128], F32, e2) for nm in ("cre", "cim")}
            tq = S.sb("tq", [128, 16], F32, e2)
            ck = {n: [S.sb(f"{n}{q}", [128, 512], F32, e2) for q in range(2)] for n in ("br", "bi", "hr", "hi")}
            tt = [S.sb(f"tt{q}", [128, 512], F32, e2) for q in range(4)]
            sgm = S.sb("sgm", [128, 512], F32, e2)
            pbr = [S.ps(f"pbr{q}", [128, 512], F32, e2) for q in range(2)]
            pbi = [S.ps(f"pbi{q}", [128, 512], F32, e2) for q in range(2)]
            pyc = [S.ps(f"pyc{q}", [128, 512], F32, e2) for q in range(2)]
            ptb = S.ps("ptb", [128, 128], F32, e2)
            for nm in ("cre", "cim"):
                for q in range(4):
                    S.op("pe", lambda h, nm=nm, q=q: h.transpose(out=ptb[:], in_=Pm["cf"][nm][:, q, :], identity=k.identf[:]), reads=[Pm["cf"][nm], k.identf], writes=[ptb])
                    S.op("dve", lambda h, nm=nm, q=q: h.tensor_copy(out=cT[nm][:, q, :], in_=ptb[:]), reads=[ptb], writes=[cT[nm]])
            cc = [0]
            for ft in range(2):
                S.dma("sp", [(uT[:], d["uT_d"][ft])], uT, writes=[uT])
                first = True
                for ctl in range(4):
                    ct = ft * 4 + ctl
                    for dd in range(2):
                        ix = dd * 8 + ct
                        for part in range(2):
                            S.op("pool", lambda h, part=part: h.memset(Bpad[part][:], 0.0), writes=[Bpad[part]])
                            S.op("pool", lambda h, part=part: h.memset(Cpad[part][:], 0.0), writes=[Cpad[part]])
                        S.op("dve", lambda h, ix=ix: h.tensor_scalar(out=tq[:], in0=Pm["bim"][:, ix, :], scalar1=Pm["qi"][:, ix:ix + 1], scalar2=None, op0=ALU.mult),
                             reads=[Pm["bim"], Pm["qi"]], writes=[tq])
                        for gl in range(2):
                            cb = (2 * ctl + gl) * 16
                            sl = slice(gl * 64, (gl + 1) * 64)
                            S.op("dve", lambda h, ix=ix, sl=sl, cb=cb: h.scalar_tensor_tensor(
                                out=Bpad[0][sl, cb:cb + 16], in0=Pm["bre"][sl, ix, :], scalar=Pm["qr"][sl, ix:ix + 1], in1=tq[sl, :], op0=ALU.mult, op1=ALU.subtract),
                                reads=[Pm["bre"], Pm["qr"], tq], writes=[Bpad[0]])
                        S.op("dve", lambda h, ix=ix: h.tensor_scalar(out=tq[:], in0=Pm["bre"][:, ix, :], scalar1=Pm["qi"][:, ix:ix + 1], scalar2=None, op0=ALU.mult),
                             reads=[Pm["bre"], Pm["qi"], Bpad[0]], writes=[tq])
                        for gl in range(2):
                            cb = (2 * ctl + gl) * 16
                            sl = slice(gl * 64, (gl + 1) * 64)
                            S.op("dve", lambda h, ix=ix, sl=sl, cb=cb: h.scalar_tensor_tensor(
                                out=Bpad[1][sl, cb:cb + 16], in0=Pm["bim"][sl, ix, :], scalar=Pm["qr"][sl, ix:ix + 1], in1=tq[sl, :], op0=ALU.mult, op1=ALU.add),
                                reads=[Pm["bim"], Pm["qr"], tq], writes=[Bpad[1]])
                            S.op("dve", lambda h, sl=sl, cb=cb, dd=dd: h.tensor_copy(out=Cpad[0][sl, cb:cb + 16], in_=cT["cre"][sl, dd * 2 + ft, cb:cb + 16]),
                                 reads=[cT["cre"]], writes=[Cpad[0]])
                            S.op("dve", lambda h, sl=sl, cb=cb, dd=dd: h.tensor_scalar(out=Cpad[1][sl, cb:cb + 16], in0=cT["cim"][sl, dd * 2 + ft, cb:cb + 16],
                                                                                      scalar1=-1.0, scalar2=None, op0=ALU.mult), reads=[cT["cim"]], writes=[Cpad[1]])
                        for part in range(2):
                            S.op("pe", lambda h, part=part: h.transpose(out=ptb[:], in_=Bpad[part][:], identity=k.identf[:]), reads=[Bpad[part], k.identf], writes=[ptb])
                            S.op("act", lambda h, part=part: h.copy(out=Bl[part][:], in_=ptb[:]), reads=[ptb], writes=[Bl[part]])
                        S.op("dve", lambda h: h.memset(Tc[:, 0:1], 1.0), writes=[Tc])
                        S.op("dve", lambda h: h.memset(Ts[:, 0:1], 0.0), writes=[Ts])
                        for lv in range(13):
                            n = 1 << lv
                            m = min(n, NTOK - n)
                            wr_ = Pm["wr"][:, lv, ix:ix + 1]
                            wi_ = Pm["wi"][:, lv, ix:ix + 1]
                            S.op("dve", lambda h, m=m, wi_=wi_: h.tensor_scalar(out=tw1[:, 0:m], in0=Ts[:, 0:m], scalar1=wi_, scalar2=None, op0=ALU.mult), reads=[Ts, Pm["wi"]], writes=[tw1])
                            S.op("dve", lambda h, m=m, wi_=wi_: h.tensor_scalar(out=tw2[:, 0:m], in0=Tc[:, 0:m], scalar1=wi_, scalar2=None, op0=ALU.mult), reads=[Tc, Pm["wi"]], writes=[tw2])
                            S.op("dve", lambda h, n=n, m=m, wr_=wr_: h.scalar_tensor_tensor(out=Tc[:, n:n + m], in0=Tc[:, 0:m], scalar=wr_, in1=tw1[:, 0:m], op0=ALU.mult, op1=ALU.subtract),
                                 reads=[Tc, Pm["wr"], tw1], writes=[Tc])
                            S.op("dve", lambda h, n=n, m=m, wr_=wr_: h.scalar_tensor_tensor(out=Ts[:, n:n + m], in0=Ts[:, 0:m], scalar=wr_, in1=tw2[:, 0:m], op0=ALU.mult, op1=ALU.add),
                                 reads=[Ts, Pm["wr"], tw2], writes=[Ts])
                        for (c0, cn) in S5_CHUNKS:
                            q2 = cc[0] % 2
                            cc[0] += 1
                            S.op("pe", lambda h, q2=q2, c0=c0, cn=cn: h.matmul(pbr[q2][:, 0:cn], lhsT=Bl[0][:], rhs=uT[:, c0:c0 + cn], start=True, stop=True), reads=[Bl[0], uT], writes=[pbr[q2]])
                            S.op("pe", lambda h, q2=q2, c0=c0, cn=cn: h.matmul(pbi[q2][:, 0:cn], lhsT=Bl[1][:], rhs=uT[:, c0:c0 + cn], start=True, stop=True), reads=[Bl[1], uT], writes=[pbi[q2]])
                            br, bi = ck["br"][q2], ck["bi"][q2]
                            S.op("act", lambda h, q2=q2, cn=cn, br=br: h.copy(out=br[:, 0:cn], in_=pbr[q2][:, 0:cn]), reads=[pbr[q2]], writes=[br])
                            S.op("act", lambda h, q2=q2, cn=cn, bi=bi: h.copy(out=bi[:, 0:cn], in_=pbi[q2][:, 0:cn]), reads=[pbi[q2]], writes=[bi])
                            tc_, ts_ = tab(Tc, dd, c0, cn), tab(Ts, dd, c0, cn)
                            S.op("dve", lambda h, cn=cn, br=br, tc_=tc_: h.tensor_tensor(out=tt[0][:, 0:cn], in0=br[:, 0:cn], in1=tc_, op=ALU.mult), reads=[br, Tc], writes=[tt[0]])
                            S.op("dve", lambda h, cn=cn, bi=bi, ts_=ts_: h.tensor_tensor(out=tt[1][:, 0:cn], in0=bi[:, 0:cn], in1=ts_, op=ALU.mult), reads=[bi, Ts], writes=[tt[1]])
                            S.op("dve", lambda h, c0=c0, cn=cn: h.tensor_tensor(out=bpr[:, c0:c0 + cn], in0=tt[0][:, 0:cn], in1=tt[1][:, 0:cn], op=ALU.add), reads=[tt[0], tt[1]], writes=[bpr])
                            S.op("pool", lambda h, cn=cn, bi=bi, tc_=tc_: h.tensor_tensor(out=tt[2][:, 0:cn], in0=bi[:, 0:cn], in1=tc_, op=ALU.mult), reads=[bi, Tc], writes=[tt[2]])
                            S.op("pool", lambda h, cn=cn, br=br, ts_=ts_: h.tensor_tensor(out=tt[3][:, 0:cn], in0=br[:, 0:cn], in1=ts_, op=ALU.mult), reads=[br, Ts], writes=[tt[3]])
                            S.op("pool", lambda h, c0=c0, cn=cn: h.tensor_tensor(out=bpi[:, c0:c0 + cn], in0=tt[2][:, 0:cn], in1=tt[3][:, 0:cn], op=ALU.subtract), reads=[tt[2], tt[3]], writes=[bpi])
                        erb = Pm["er"][:, ix:ix + 1]
                        for (src_b, dst_b) in ((bpr, bpr), (bpi, bpi)):
                            if dd == 0:
                                S.op("dve", lambda h, src_b=src_b, dst_b=dst_b: h.tensor_tensor_scan(
                                    out=dst_b[:], data0=erb.to_broadcast([128, NTOK]), data1=src_b[:], initial=0.0, op0=ALU.mult, op1=ALU.add),
                                    reads=[src_b, Pm["er"]], writes=[dst_b])
                            else:
                                S.op("dve", lambda h, src_b=src_b, dst_b=dst_b: h.tensor_tensor_scan(
                                    out=dst_b[:, 255::-1], data0=erb.to_broadcast([128, 256]), data1=src_b[:, 255::-1], initial=0.0, op0=ALU.mult, op1=ALU.add),
                                    reads=[src_b, Pm["er"]], writes=[dst_b])
                                S.op("dve", lambda h, src_b=src_b, dst_b=dst_b: h.tensor_tensor_scan(
                                    out=dst_b[:, NTOK - 1:255:-1], data0=erb.to_broadcast([128, NTOK - 256]), data1=src_b[:, NTOK - 1:255:-1],
                                    initial=dst_b[:, 0:1], op0=ALU.mult, op1=ALU.add), reads=[src_b, Pm["er"]], writes=[dst_b])
                        for (c0, cn) in S5_CHUNKS:
                            q2 = cc[0] % 2
                            cc[0] += 1
                            tc_, ts_ = tab(Tc, dd, c0, cn), tab(Ts, dd, c0, cn)
                            hr, hi = ck["hr"][q2], ck["hi"][q2]
                            S.op("dve", lambda h, c0=c0, cn=cn, tc_=tc_: h.tensor_tensor(out=tt[0][:, 0:cn], in0=gr[:, c0:c0 + cn], in1=tc_, op=ALU.mult), reads=[gr, Tc], writes=[tt[0]])
                            S.op("dve", lambda h, c0=c0, cn=cn, ts_=ts_: h.tensor_tensor(out=tt[1][:, 0:cn], in0=gi[:, c0:c0 + cn], in1=ts_, op=ALU.mult), reads=[gi, Ts], writes=[tt[1]])
                            S.op("dve", lambda h, cn=cn, hr=hr: h.tensor_tensor(out=hr[:, 0:cn], in0=tt[0][:, 0:cn], in1=tt[1][:, 0:cn], op=ALU.subtract), reads=[tt[0], tt[1]], writes=[hr])
                            S.op("pool", lambda h, c0=c0, cn=cn, ts_=ts_: h.tensor_tensor(out=tt[2][:, 0:cn], in0=gr[:, c0:c0 + cn], in1=ts_, op=ALU.mult), reads=[gr, Ts], writes=[tt[2]])
                            S.op("pool", lambda h, c0=c0, cn=cn, tc_=tc_: h.tensor_tensor(out=tt[3][:, 0:cn], in0=gi[:, c0:c0 + cn], in1=tc_, op=ALU.mult), reads=[gi, Tc], writes=[tt[3]])
                            S.op("pool", lambda h, cn=cn, hi=hi: h.tensor_tensor(out=hi[:, 0:cn], in0=tt[2][:, 0:cn], in1=tt[3][:, 0:cn], op=ALU.add), reads=[tt[2], tt[3]], writes=[hi])
                            S.op("pe", lambda h, q2=q2, cn=cn, hr=hr: h.matmul(pyc[q2][:, 0:cn], lhsT=Cpad[0][:], rhs=hr[:, 0:cn], start=True, stop=False), reads=[Cpad[0], hr], writes=[pyc[q2]])
                            S.op("pe", lambda h, q2=q2, cn=cn, hi=hi: h.matmul(pyc[q2][:, 0:cn], lhsT=Cpad[1][:], rhs=hi[:, 0:cn], start=False, stop=True), reads=[Cpad[1], hi], writes=[pyc[q2]])
                            if first:
                                S.op("act", lambda h, q2=q2, c0=c0, cn=cn: h.copy(out=yc[:, c0:c0 + cn], in_=pyc[q2][:, 0:cn]), reads=[pyc[q2]], writes=[yc])
                            else:
                                S.op("dve", lambda h, q2=q2, c0=c0, cn=cn: h.tensor_tensor(out=yc[:, c0:c0 + cn], in0=pyc[q2][:, 0:cn], in1=yc[:, c0:c0 + cn], op=ALU.add),
                                     reads=[pyc[q2], yc], writes=[yc])
                        first = False
                S.op("dve", lambda h, ft=ft: h.scalar_tensor_tensor(out=yc[:], in0=uT[:], scalar=Pm["dcol"][:, ft:ft + 1], in1=yc[:], op0=ALU.mult, op1=ALU.add),
                     reads=[uT, Pm["dcol"], yc], writes=[yc])
                S.op("act", lambda h, ft=ft: h.activation(out=ygb[:, ft, :], in_=yc[:], func=AF.Gelu_apprx_tanh), reads=[yc], writes=[ygb])
            for jt in range(2):
                for (c0, cn) in S5_CHUNKS:
                    q2 = cc[0] % 2
                    cc[0] += 1
                    for kc in range(2):
                        S.op("pe", lambda h, q2=q2, kc=kc, jt=jt, c0=c0, cn=cn: h.matmul(pyc[q2][:, 0:cn], lhsT=gluw[:, kc, jt * 128:(jt + 1) * 128], rhs=ygb[:, kc, c0:c0 + cn],
                                                                                          start=(kc == 0), stop=(kc == 1)), reads=[gluw, ygb], writes=[pyc[q2]])
                    S.op("act", lambda h, q2=q2, jt=jt, cn=cn: h.activation(out=sgm[:, 0:cn], in_=pyc[q2][:, 0:cn], func=AF.Sigmoid, bias=Pm["gbcol"][:, jt:jt + 1], scale=1.0),
                         reads=[pyc[q2], Pm["gbcol"]], writes=[sgm])
                    S.op("dve", lambda h, jt=jt, c0=c0, cn=cn: h.tensor_tensor(out=ybT[:, jt, c0:c0 + cn], in0=ygb[:, jt, c0:c0 + cn], in1=sgm[:, 0:cn], op=ALU.mult),
                         reads=[ygb, sgm], writes=[ybT])
        S.barrier()
        with ExitStack() as e3:
            wout = S.sb("wout", [128, 8, D], BF16, e3)
            load_weight_cast(k, wout, lambda kc, c0, c1: wout[:, kc, c0:c1], d["ab_w_out"], D, D, 1024)
            gbc = load_gbc(k, e3, i, j)
            yaT = [S.sb(f"yaT{q}", [128, 6, 128], BF16, e3) for q in range(2)]
            xrs = [S.sb(f"xr{q}", [128, D], F32, e3) for q in range(2)]
            eb = {"junk": S.sb("junk", [128, D], BF16, e3), "ss2": S.sb("ss2", [128, 2], F32, e3),
                  "rs2": S.sb("rs2", [128, 4], F32, e3), "tmp": S.sb("tmp", [128, D], F32, e3), "epsc": epsc}
            pys = [[S.ps(f"py{q}{n}", [128, 512], F32, e3) for n in range(2)] for q in range(2)]
            for t in range(NT):
                ya, xr, py = yaT[t % 2], xrs[t % 2], pys[t % 2]
                S.dma("sp", [(ya[:], d["yaT_d"][t])], ya, writes=[ya])
                S.dma("sp", [(xr[:], src[t * 128:(t + 1) * 128, :])], xr, writes=[xr])
                for n in range(2):
                    for kc in range(8):
                        lh = ya[:, kc, :] if kc < 6 else ybT[:, kc - 6, t * 128:(t + 1) * 128]
                        S.op("pe", lambda h, n=n, kc=kc, lh=lh, py=py: h.matmul(py[n][:], lhsT=lh, rhs=wout[:, kc, n * 512:(n + 1) * 512], start=(kc == 0), stop=(kc == 7)),
                             reads=[ya, ybT, wout], writes=[py[n]])
                post_res(k, eb, py, xr, gbc[1 if t < 2 else 0], dst[t * 128:(t + 1) * 128, :])
        S.barrier()


def build(debug=None):
    nc = bass.Bass("TRN2", target_bir_lowering=False)
    k = K()
    k.nc = nc
    k.d = {}
    for nm, shp in PARAM_SPECS:
        k.d[nm] = nc.dram_tensor(nm, list(shp), F32, kind="ExternalInput").ap()
    k.d["out"] = nc.dram_tensor("out", [4096, D], F32, kind="ExternalOutput").ap()
    k.d["grow_d"] = nc.dram_tensor("grow_d", [2, 3, 2, D], F32, kind="Internal").ap()
    k.d["qT_d"] = nc.dram_tensor("qT_d", [32, 128, 8, 128], BF16, kind="Internal").ap()
    k.d["uT_d"] = nc.dram_tensor("uT_d", [2, 128, NTOK], F32, kind="Internal").ap()
    k.d["yaT_d"] = nc.dram_tensor("yaT_d", [NT, 128, 6, 128], BF16, kind="Internal").ap()
    for nm in ("sA", "sB", "sC", "sD", "sE"):
        k.d[nm] = nc.dram_tensor(nm, [NTOK, D], F32, kind="Internal").ap()
    if debug:
        k.d["dbg_in"] = nc.dram_tensor("dbg_in", [NTOK, D], F32, kind="ExternalInput").ap()
        k.d["dbg_out"] = nc.dram_tensor("dbg_out", [NTOK, D], F32, kind="ExternalOutput").ap()
    with ExitStack() as es:
        k.S = Sched(nc, es)
        setup_globals(k)
        allt = list(range(NT))
        lat = list(range(2, NT))
        phase_mod(k)
        dd = k.d
        if debug is None:
            phase_ffn(k, 0, 0, dd["xs"], dd["sA"], allt)
            phase_mix0(k, dd["sA"], dd["sB"])
            phase_ffn(k, 0, 1, dd["sB"], dd["sC"], allt)
            phase_ffn(k, 1, 0, dd["sC"], dd["sD"], allt)
            phase_attn(k, dd["sD"], dd["sE"])
            phase_ffn(k, 1, 1, dd["sE"], dd["out"], lat, dst_off=-2)
        elif debug == "attn":
            phase_attn(k, dd["dbg_in"], dd["dbg_out"])
        elif debug == "mix0":
            phase_mix0(k, dd["dbg_in"], dd["dbg_out"])
        elif debug == "ffn":
            phase_ffn(k, 0, 0, dd["dbg_in"], dd["dbg_out"], allt)
        k.S.barrier()
        print("ninst", k.S.ninst, "nwait", k.S.nwait, "nsem", len(k.S.sems))
    return nc

def make_in_maps(inputs):
    f = lambda a: np.ascontiguousarray(np.asarray(a, dtype=np.float32))
    shared = {
        "w_mod": f(inputs["w_mod"]), "b_mod": f(inputs["b_mod"]), "norm_pre": f(inputs["norm_pre"]),
        "norm_post": f(inputs["norm_post"]), "ffn_w_in": f(inputs["ffn_w_in"]), "ffn_w_out": f(inputs["ffn_w_out"]),
        "ab_w_in": f(inputs["ab_w_in"][0]), "ab_w_out": f(inputs["ab_w_out"][0]), "sgu_norm_g": f(inputs["sgu_norm_g"][0]),
        "sgu_w": f(inputs["sgu_w"][0]), "sgu_b": f(inputs["sgu_b"][0]), "s5_lam_re": f(inputs["s5_lam_re"][0]),
        "s5_lam_im": f(inputs["s5_lam_im"][0]), "s5_log_step": f(inputs["s5_log_step"][0]),
        "s5_b_re": f(inputs["s5_b_re"][0]), "s5_b_im": f(inputs["s5_b_im"][0]), "s5_c_re": f(inputs["s5_c_re"][0]),
        "s5_c_im": f(inputs["s5_c_im"][0]), "s5_d": f(inputs["s5_d"][0]), "s5_glu_w": f(inputs["s5_glu_w"][0]),
        "s5_glu_b": f(inputs["s5_glu_b"][0]), "attn_w_qkv": f(inputs["attn_w_qkv"][0]), "attn_w_out": f(inputs["attn_w_out"][0]),
        "attn_q_norm": f(inputs["attn_q_norm"][0]), "attn_k_norm": f(inputs["attn_k_norm"][0]),
    }
    maps = []
    for b in range(8):
        m = dict(shared)
        m["xs"] = np.ascontiguousarray(np.concatenate([inputs["ctx"][b], inputs["x"][b]], axis=0).astype(np.float32))
        m["cond"] = np.ascontiguousarray(np.stack([inputs["c"][b], inputs["c_ctx"]], axis=0).astype(np.float32))
        maps.append(m)
    return maps


def kernel(**inputs):
    nc = build()
    maps = make_in_maps(inputs)
    res = run_bass_kernel_spmd(nc, maps, core_ids=list(range(8)))
    return np.stack([np.asarray(r["out"]) for r in res.results], axis=0).astype(np.float32)
```

```python
import numpy as np
from contextlib import ExitStack
import concourse.bass as bass
import concourse.mybir as mybir
from concourse.bass_utils import run_bass_kernel_spmd

F32 = mybir.dt.float32
BF16 = mybir.dt.bfloat16
I32 = mybir.dt.int32
AF = mybir.ActivationFunctionType
ALU = mybir.AluOpType
AX = mybir.AxisListType


class Buf:
    __slots__ = ("name", "t", "w", "r", "dsid", "dcnt")

    def __init__(self, name, t):
        self.name = name
        self.t = t
        self.w = None
        self.r = {}
        self.dsid = None
        self.dcnt = 0

    def __getitem__(self, idx):
        return self.t[idx]


class Sched:
    ENG = ("pe", "act", "dve", "pool", "sp")

    def __init__(self, nc, es):
        self.nc = nc
        self.es = es
        self.sems = []
        self.final = []
        self.e = {}
        hs = {"pe": nc.tensor, "act": nc.scalar, "dve": nc.vector, "pool": nc.gpsimd, "sp": nc.sync}
        for nm in self.ENG:
            sid = self._newsem("e_" + nm)
            self.e[nm] = {"h": hs[nm], "sid": sid, "cnt": 0, "seen": {}}
        self.bufs = []
        self.nwait = 0
        self.ninst = 0

    def _newsem(self, name):
        h = self.es.enter_context(self.nc.semaphore(name))
        self.sems.append(h)
        self.final.append(0)
        return len(self.sems) - 1

    def sb(self, name, shape, dt=F32, es=None):
        self.uid = getattr(self, "uid", 0) + 1
        name = f"{name}_{self.uid}"
        t = (es or self.es).enter_context(self.nc.sbuf_tensor(name, list(shape), dt))
        b = Buf(name, t)
        self.bufs.append(b)
        return b

    def ps(self, name, shape, dt=F32, es=None):
        self.uid = getattr(self, "uid", 0) + 1
        name = f"{name}_{self.uid}"
        t = (es or self.es).enter_context(self.nc.psum_tensor(name, list(shape), dt))
        b = Buf(name, t)
        self.bufs.append(b)
        return b

    def wrap(self, name, t):
        b = Buf(name, t)
        self.bufs.append(b)
        return b

    def _collect(self, reads, writes):
        deps = {}
        for b in reads:
            if b.w is not None:
                s, v = b.w
                if deps.get(s, 0) < v:
                    deps[s] = v
        for b in writes:
            if b.w is not None:
                s, v = b.w
                if deps.get(s, 0) < v:
                    deps[s] = v
            for s, v in b.r.items():
                if deps.get(s, 0) < v:
                    deps[s] = v
        return deps

    def _wait(self, eng, deps):
        E = self.e[eng]
        for s, v in deps.items():
            if eng == "pe" and s == E["sid"]:
                continue
            if E["seen"].get(s, 0) >= v:
                continue
            E["h"].wait_ge(self.sems[s], v)
            E["seen"][s] = v
            self.nwait += 1

    def op(self, eng, fn, reads=(), writes=()):
        E = self.e[eng]
        self._wait(eng, self._collect(reads, writes))
        inst = fn(E["h"])
        E["cnt"] += 1
        inst.then_inc(self.sems[E["sid"]], 1)
        self.final[E["sid"]] = E["cnt"]
        ev = (E["sid"], E["cnt"])
        for b in reads:
            if b.r.get(ev[0], 0) < ev[1]:
                b.r[ev[0]] = ev[1]
        for b in writes:
            b.w = ev
            b.r = {}
        self.ninst += 1
        return inst

    def dma(self, eng, pairs, semb, reads=(), writes=(), **kw):
        E = self.e[eng]
        self._wait(eng, self._collect(reads, writes))
        if semb.dsid is None:
            semb.dsid = self._newsem("d_" + semb.name)
        for (o, i) in pairs:
            E["h"].dma_start(out=o, in_=i, **kw).then_inc(self.sems[semb.dsid], 16)
            semb.dcnt += 16
            self.ninst += 1
        self.final[semb.dsid] = semb.dcnt
        ev = (semb.dsid, semb.dcnt)
        for b in reads:
            if b.r.get(ev[0], 0) < ev[1]:
                b.r[ev[0]] = ev[1]
        for b in writes:
            b.w = ev
            b.r = {}

    def barrier(self, engs=None):
        deps = {s: v for s, v in enumerate(self.final) if v > 0}
        for nm in (engs or self.ENG):
            self._wait(nm, deps)
        for b in self.bufs:
            b.w = None
            b.r = {}


D = 1024
NTOK = 4352
NT = NTOK // 128
DFF = 2816
NF = DFF // 128
EPS = 1e-6
RES_W = (0.5, 1.0, 0.5)

PARAM_SPECS = [
    ("xs", [NTOK, D]), ("cond", [2, D]),
    ("w_mod", [2, D, 9 * D]), ("b_mod", [2, 9 * D]), ("norm_pre", [2, 3, D]), ("norm_post", [2, 3, D]),
    ("ffn_w_in", [2, 2, D, 2 * DFF]), ("ffn_w_out", [2, 2, DFF, D]),
    ("ab_w_in", [D, 1792]), ("ab_w_out", [D, D]), ("sgu_norm_g", [768]), ("sgu_w", [6, 128, 128]),
    ("sgu_b", [6, 128]), ("s5_lam_re", [2, 16, 64]), ("s5_lam_im", [2, 16, 64]), ("s5_log_step", [2, 16]),
    ("s5_b_re", [2, 16, 64, 16]), ("s5_b_im", [2, 16, 64, 16]), ("s5_c_re", [2, 16, 16, 64]),
    ("s5_c_im", [2, 16, 16, 64]), ("s5_d", [256]), ("s5_glu_w", [256, 256]), ("s5_glu_b", [256]),
    ("attn_w_qkv", [D, 1536]), ("attn_w_out", [D, D]), ("attn_q_norm", [128]), ("attn_k_norm", [128]),
]


class K:
    pass


def setup_globals(k):
    S, nc = k.S, k.nc
    k.ii = S.sb("ii", [128, 128], I32)
    k.identf = S.sb("identf", [128, 128], F32)
    k.identb = S.sb("identb", [128, 128], BF16)
    S.op("pool", lambda h: h.iota(k.ii[:], pattern=[[1, 128]], base=0, channel_multiplier=-1), writes=[k.ii])
    S.op("dve", lambda h: h.tensor_single_scalar(out=k.identf[:], in_=k.ii[:], scalar=0, op=ALU.is_equal),
         reads=[k.ii], writes=[k.identf])
    S.op("dve", lambda h: h.tensor_copy(out=k.identb[:], in_=k.identf[:]), reads=[k.identf], writes=[k.identb])
    k.acol = S.sb("acol", [128, 2, 3, 2, 8], F32)
    k.bcol = S.sb("bcol", [128, 2, 3, 2, 8], F32)


def phase_mod(k):
    S, nc, d = k.S, k.nc, k.d
    with ExitStack() as es:
        condT = S.sb("condT", [128, 8, 2], F32, es)
        gpre = S.sb("gpre", [128, 2, 3, 8], F32, es)
        modrow = S.sb("modrow", [2, 9 * D], F32, es)
        bmod2 = S.sb("bmod2", [2, 9 * D], F32, es)
        gp2 = S.sb("gp2", [2, 3, D], F32, es)
        grow = S.sb("grow", [2, 3, D], F32, es)
        wm = [S.sb(f"wm{q}", [128, 8, 512], F32, es) for q in range(2)]
        pm = [S.ps(f"pm{q}", [128, 512], F32, es) for q in range(2)]
        ptr = S.ps("ptrm", [128, 144], F32, es)
        modcol = S.sb("modcol", [128, 72, 2], F32, es)
        tmpc = S.sb("tmpc", [128, 8], F32, es)
        S.dma("sp", [(condT[:, kc, :], d["cond"][:, kc * 128:(kc + 1) * 128].rearrange("r p -> p r"))
                     for kc in range(8)], condT, writes=[condT], allow_slow_non_contiguous=True)
        S.op("act", lambda h: h.activation(out=condT[:], in_=condT[:], func=AF.Silu), reads=[condT], writes=[condT])
        S.dma("sp", [(gpre[:, i, j, :], d["norm_pre"][i, j, :].rearrange("(kc p) -> p kc", p=128))
                     for i in range(2) for j in range(3)], gpre, writes=[gpre], allow_slow_non_contiguous=True)
        for i in range(2):
            S.dma("sp", [(bmod2[r:r + 1, :], d["b_mod"][i:i + 1, :]) for r in range(2)], bmod2, writes=[bmod2])
            S.dma("sp", [(gp2[r:r + 1, :, :], d["norm_post"][i:i + 1, :, :]) for r in range(2)], gp2, writes=[gp2])
            for n in range(18):
                w = wm[n % 2]
                p = pm[n % 2]
                S.dma("sp", [(w[:], d["w_mod"][i, :, n * 512:(n + 1) * 512].rearrange("(kc p) n -> p kc n", p=128))],
                      w, writes=[w])
                for kc in range(8):
                    S.op("pe", lambda h, kc=kc, w=w, p=p: h.matmul(p[0:2, :], lhsT=condT[:, kc, :], rhs=w[:, kc, :],
                                                                    start=(kc == 0), stop=(kc == 7)),
                         reads=[condT, w], writes=[p])
                S.op("dve", lambda h, p=p, n=n: h.tensor_tensor(out=modrow[:, n * 512:(n + 1) * 512], in0=p[0:2, :],
                                                               in1=bmod2[:, n * 512:(n + 1) * 512], op=ALU.add),
                     reads=[p, bmod2], writes=[modrow])
            for j in range(3):
                S.op("dve", lambda h, j=j: h.scalar_tensor_tensor(
                    out=grow[:, j, :], in0=modrow[:, (3 * j + 2) * D:(3 * j + 3) * D], scalar=float(RES_W[j]),
                    in1=gp2[:, j, :], op0=ALU.mult, op1=ALU.mult), reads=[modrow, gp2], writes=[grow])
            S.dma("sp", [(d["grow_d"][i, :, :, :].rearrange("j r n -> r j n"), grow[:])], grow, reads=[grow])
            for c in range(72):
                S.op("pe", lambda h, c=c: h.transpose(out=ptr[:, 2 * c:2 * c + 2], in_=modrow[0:2, c * 128:(c + 1) * 128],
                                                      identity=k.identf[0:2, 0:2]), reads=[modrow, k.identf], writes=[ptr])
            S.op("dve", lambda h: h.tensor_copy(out=modcol[:].rearrange("p c r -> p (c r)"), in_=ptr[:]), reads=[ptr], writes=[modcol])
            for j in range(3):
                for r in range(2):
                    S.op("dve", lambda h, j=j, r=r: h.scalar_tensor_tensor(
                        out=k.acol[:, i, j, r, :], in0=modcol[:, (3 * j + 1) * 8:(3 * j + 2) * 8, r], scalar=1.0,
                        in1=gpre[:, i, j, :], op0=ALU.add, op1=ALU.mult), reads=[modcol, gpre], writes=[k.acol])
                    S.op("dve", lambda h, j=j, r=r: h.tensor_copy(out=k.bcol[:, i, j, r, :], in_=modcol[:, (3 * j) * 8:(3 * j + 1) * 8, r]),
                         reads=[modcol], writes=[k.bcol])
        S.barrier()


def load_weight_cast(k, buf, dst_fn, src2d, nrows, ncols, colblk):
    S = k.S
    pairs = []
    for kc in range(nrows // 128):
        for c0 in range(0, ncols, colblk):
            c1 = min(ncols, c0 + colblk)
            pairs.append((dst_fn(kc, c0, c1), src2d[kc * 128:(kc + 1) * 128, c0:c1]))
    S.dma("pool", pairs, buf, writes=[buf])


def norm_prep(k, es_bufs, xt, tix, hT, i, j, r, T0):
    S = k.S
    junk, ssq, rst, ptr = es_bufs["junk"], es_bufs["ssq"], es_bufs["rst"], es_bufs["ptr"]
    S.op("act", lambda h: h.activation(out=junk[:], in_=xt[:], func=AF.Square, accum_out=ssq[:, 0:1]),
         reads=[xt], writes=[junk, ssq])
    S.op("act", lambda h: h.activation(out=rst[:, 0:1], in_=ssq[:, 0:1], func=AF.Sqrt, bias=es_bufs["epsc"][:, 0:1], scale=1.0 / D),
         reads=[ssq, es_bufs["epsc"]], writes=[rst])
    S.op("dve", lambda h: h.reciprocal(out=rst[:, 1:2], in_=rst[:, 0:1]), reads=[rst], writes=[rst])
    S.op("act", lambda h: h.activation(out=xt[:], in_=xt[:], func=AF.Identity, scale=rst[:, 1:2]), reads=[xt, rst], writes=[xt])
    for half in range(2):
        p = ptr[half]
        for q in range(4):
            kc = half * 4 + q
            S.op("pe", lambda h, kc=kc, q=q, p=p: h.transpose(out=p[:, q * 128:(q + 1) * 128], in_=xt[:, kc * 128:(kc + 1) * 128],
                                                               identity=k.identf[:]), reads=[xt, k.identf], writes=[p])
        for q in range(4):
            kc = half * 4 + q
            if q % 2 == 0:
                S.op("act", lambda h, kc=kc, q=q, p=p: h.activation(
                    out=hT[:, kc, T0:T0 + 128], in_=p[:, q * 128:(q + 1) * 128], func=AF.Identity,
                    scale=k.acol[:, i, j, r, kc:kc + 1], bias=k.bcol[:, i, j, r, kc:kc + 1]),
                    reads=[p, k.acol, k.bcol], writes=[hT])
            else:
                S.op("dve", lambda h, kc=kc, q=q, p=p: h.tensor_scalar(
                    out=hT[:, kc, T0:T0 + 128], in0=p[:, q * 128:(q + 1) * 128],
                    scalar1=k.acol[:, i, j, r, kc:kc + 1], scalar2=k.bcol[:, i, j, r, kc:kc + 1], op0=ALU.mult, op1=ALU.add),
                    reads=[p, k.acol, k.bcol], writes=[hT])


def post_res(k, eb, py, xr, gbc, dst_ap):
    S = k.S
    junk, ss2, rs2, tmp = eb["junk"], eb["ss2"], eb["rs2"], eb["tmp"]
    for n in range(2):
        S.op("act", lambda h, n=n: h.activation(out=junk[:, n * 512:(n + 1) * 512], in_=py[n][:], func=AF.Square,
                                                accum_out=ss2[:, n:n + 1]), reads=[py[n]], writes=[junk, ss2])
    S.op("dve", lambda h: h.tensor_scalar(out=rs2[:, 0:1], in0=ss2[:, 0:1], scalar1=ss2[:, 1:2], scalar2=1.0 / D,
                                          op0=ALU.add, op1=ALU.mult), reads=[ss2], writes=[rs2])
    S.op("act", lambda h: h.activation(out=rs2[:, 1:2], in_=rs2[:, 0:1], func=AF.Sqrt, bias=eb["epsc"][:, 0:1], scale=1.0),
         reads=[rs2, eb["epsc"]], writes=[rs2])
    S.op("dve", lambda h: h.reciprocal(out=rs2[:, 2:3], in_=rs2[:, 1:2]), reads=[rs2], writes=[rs2])
    for n in range(2):
        S.op("dve", lambda h, n=n: h.scalar_tensor_tensor(out=tmp[:, n * 512:(n + 1) * 512], in0=py[n][:], scalar=rs2[:, 2:3],
                                                          in1=gbc[:, n * 512:(n + 1) * 512], op0=ALU.mult, op1=ALU.mult),
             reads=[py[n], rs2, gbc], writes=[tmp])
    S.op("pool", lambda h: h.tensor_tensor(out=xr[:], in0=tmp[:], in1=xr[:], op=ALU.add), reads=[tmp, xr], writes=[xr])
    S.dma("sp", [(dst_ap, xr[:])], xr, reads=[xr])


def mk_groups(tiles):
    gs = []
    ctx = [t for t in tiles if t < 2]
    lat = [t for t in tiles if t >= 2]
    if ctx:
        gs.append(ctx)
    for a in range(0, len(lat), 4):
        gs.append(lat[a:a + 4])
    return gs


def load_gbc(k, es, i, j):
    S, d = k.S, k.d
    g = []
    for r in range(2):
        b = S.sb(f"gbc{r}", [128, D], F32, es)
        S.dma("sp", [(b[:], d["grow_d"][i, j, r:r + 1, :].partition_broadcast(128))], b, writes=[b])
        g.append(b)
    return g


def phase_ffn(k, i, w, src, dst, tiles, dst_off=0):
    S, nc, d = k.S, k.nc, k.d
    j = 0 if w == 0 else 2
    with ExitStack() as es:
        win = S.sb("win", [128, 8, 2 * DFF], BF16, es)
        wout = S.sb("wout", [128, NF, D], BF16, es)
        load_weight_cast(k, win, lambda kc, c0, c1: win[:, kc, c0:c1], d["ffn_w_in"][i, w], D, 2 * DFF, 1408)
        load_weight_cast(k, wout, lambda kc, c0, c1: wout[:, kc, c0:c1], d["ffn_w_out"][i, w], DFF, D, 1024)
        gbc = load_gbc(k, es, i, j)
        hT = S.sb("hT", [128, 8, 512], BF16, es)
        act = [S.sb(f"act{f}", [128, 512], BF16, es) for f in range(NF)]
        xts = [S.sb(f"xt{q}", [128, D], F32, es) for q in range(2)]
        xrs = [S.sb(f"xr{q}", [128, D], F32, es) for q in range(2)]
        sg = [S.sb(f"sg{q}", [128, 512], BF16, es) for q in range(2)]
        eb = {"junk": S.sb("junk", [128, D], BF16, es), "ssq": S.sb("ssq", [128, 2], F32, es),
              "rst": S.sb("rst", [128, 2], F32, es), "ss2": S.sb("ss2", [128, 2], F32, es),
              "rs2": S.sb("rs2", [128, 4], F32, es), "tmp": S.sb("tmp", [128, D], F32, es),
              "epsc": S.sb("epsc", [128, 1], F32, es),
              "ptr": [S.ps(f"ptr{q}", [128, 512], F32, es) for q in range(2)]}
        S.op("dve", lambda h: h.memset(eb["epsc"][:], EPS), writes=[eb["epsc"]])
        pg = [S.ps(f"pg{q}", [128, 512], F32, es) for q in range(2)]
        pu = [S.ps(f"pu{q}", [128, 512], F32, es) for q in range(2)]
        py = [S.ps(f"py{q}", [128, 512], F32, es) for q in range(2)]
        groups = mk_groups(tiles)
        cnt = [0, 0]

        def prep(g):
            for ti, t in enumerate(groups[g]):
                xt = xts[cnt[0] % 2]
                cnt[0] += 1
                S.dma("sp", [(xt[:], src[t * 128:(t + 1) * 128, :])], xt, writes=[xt])
                norm_prep(k, eb, xt, ti, hT, i, j, 1 if t < 2 else 0, ti * 128)

        def stage_a(g):
            T = 128 * len(groups[g])
            for f in range(NF):
                for (pp, c0) in ((pg[f % 2], f * 128), (pu[f % 2], DFF + f * 128)):
                    for kc in range(8):
                        S.op("pe", lambda h, pp=pp, c0=c0, kc=kc: h.matmul(pp[:, 0:T], lhsT=win[:, kc, c0:c0 + 128], rhs=hT[:, kc, 0:T],
                                                                           start=(kc == 0), stop=(kc == 7)),
                             reads=[win, hT], writes=[pp])
                s = sg[f % 2]
                S.op("act", lambda h, s=s, f=f: h.activation(out=s[:, 0:T], in_=pg[f % 2][:, 0:T], func=AF.Silu),
                     reads=[pg[f % 2]], writes=[s])
                S.op("dve", lambda h, s=s, f=f: h.tensor_tensor(out=act[f][:, 0:T], in0=pu[f % 2][:, 0:T], in1=s[:, 0:T], op=ALU.mult),
                     reads=[pu[f % 2], s], writes=[act[f]])

        def stage_b(g):
            for ti, t in enumerate(groups[g]):
                xr = xrs[cnt[1] % 2]
                cnt[1] += 1
                S.dma("sp", [(xr[:], src[t * 128:(t + 1) * 128, :])], xr, writes=[xr])
                for n in range(2):
                    for f in range(NF):
                        S.op("pe", lambda h, n=n, f=f, ti=ti: h.matmul(py[n][:], lhsT=act[f][:, ti * 128:(ti + 1) * 128],
                                                                      rhs=wout[:, f, n * 512:(n + 1) * 512],
                                                                      start=(f == 0), stop=(f == NF - 1)),
                             reads=[act[f], wout], writes=[py[n]])
                to = t + dst_off
                post_res(k, eb, py, xr, gbc[1 if t < 2 else 0], dst[to * 128:(to + 1) * 128, :])

        prep(0)
        for g in range(len(groups)):
            stage_a(g)
            if g + 1 < len(groups):
                prep(g + 1)
            stage_b(g)
        S.barrier()


ATT_SCALE = 128 ** -0.5
TWO_PI_SAFE = 6.283184
MAGIC = 12582912.0


def rope_prep(k, es):
    S = k.S
    t = {}
    pid = S.sb("pid", [128, 1], I32, es)
    pf = S.sb("pf", [128, 4], F32, es)
    invf = S.sb("invf", [128, 32], F32, es)
    S.op("pool", lambda h: h.iota(pid[:], pattern=[[0, 1]], base=0, channel_multiplier=1), writes=[pid])
    S.op("dve", lambda h: h.tensor_copy(out=pf[:, 0:1], in_=pid[:]), reads=[pid], writes=[pf])
    S.op("dve", lambda h: h.tensor_single_scalar(out=pf[:, 1:2], in_=pf[:, 0:1], scalar=64.0, op=ALU.is_ge), reads=[pf], writes=[pf])
    S.op("dve", lambda h: h.scalar_tensor_tensor(out=pf[:, 2:3], in0=pf[:, 1:2], scalar=-64.0, in1=pf[:, 0:1], op0=ALU.mult, op1=ALU.add),
         reads=[pf], writes=[pf])
    for q in range(32):
        S.op("dve", lambda h, q=q: h.memset(invf[:, q:q + 1], float(10000.0 ** (-(2.0 * q) / 64.0))), writes=[invf])
    t["pf"], t["invf"] = pf, invf
    t["ang"] = S.sb("ang", [128, 32], F32, es)
    t["ang2"] = S.sb("ang2", [128, 32], F32, es)
    t["CT"] = S.sb("CT", [128, 4, 32], F32, es)
    t["ST"] = S.sb("ST", [128, 4, 32], F32, es)
    t["rowpos"] = S.sb("rowpos", [128, 1], F32, es)
    return t


def rope_sincos(k, t, pos_ap, pos_buf, ax):
    S = k.S
    ang, ang2, CT, ST = t["ang"], t["ang2"], t["CT"], t["ST"]
    S.op("dve", lambda h: h.tensor_scalar(out=ang[:], in0=t["invf"][:], scalar1=pos_ap, scalar2=1.0 / (2 * np.pi), op0=ALU.mult, op1=ALU.mult),
         reads=[t["invf"], pos_buf], writes=[ang])
    S.op("dve", lambda h: h.tensor_scalar(out=ang2[:], in0=ang[:], scalar1=MAGIC, scalar2=-MAGIC, op0=ALU.add, op1=ALU.add), reads=[ang], writes=[ang2])
    S.op("dve", lambda h: h.tensor_tensor(out=ang[:], in0=ang[:], in1=ang2[:], op=ALU.subtract), reads=[ang, ang2], writes=[ang])
    S.op("act", lambda h: h.activation(out=ST[:, 2 * ax + 1, :], in_=ang[:], func=AF.Sin, scale=TWO_PI_SAFE), reads=[ang], writes=[ST])
    S.op("act", lambda h: h.activation(out=ST[:, 2 * ax, :], in_=ang[:], func=AF.Sin, scale=-TWO_PI_SAFE), reads=[ang], writes=[ST])
    S.op("dve", lambda h: h.tensor_scalar(out=ang[:], in0=ang[:], scalar1=0.25, scalar2=None, op0=ALU.add), reads=[ang], writes=[ang])
    S.op("dve", lambda h: h.tensor_scalar(out=ang2[:], in0=ang[:], scalar1=MAGIC, scalar2=-MAGIC, op0=ALU.add, op1=ALU.add), reads=[ang], writes=[ang2])
    S.op("dve", lambda h: h.tensor_tensor(out=ang[:], in0=ang[:], in1=ang2[:], op=ALU.subtract), reads=[ang, ang2], writes=[ang])
    S.op("act", lambda h: h.activation(out=CT[:, 2 * ax, :], in_=ang[:], func=AF.Sin, scale=TWO_PI_SAFE), reads=[ang], writes=[CT])
    S.op("act", lambda h: h.activation(out=CT[:, 2 * ax + 1, :], in_=ang[:], func=AF.Sin, scale=TWO_PI_SAFE), reads=[ang], writes=[CT])


def phase_attn(k, src, dst):
    S, nc, d = k.S, k.nc, k.d
    i, j = 1, 1
    with ExitStack() as es:
        wo = S.sb("wo", [128, 8, D], BF16, es)
        load_weight_cast(k, wo, lambda kc, c0, c1: wo[:, kc, c0:c1], d["attn_w_out"], D, D, 1024)
        KT = S.sb("KT", [128, 2, NTOK], BF16, es)
        V = S.sb("V", [128, NT, 256], BF16, es)
        negm = S.sb("negm", [128, 4], F32, es)
        epsc = S.sb("epsc", [128, 1], F32, es)
        S.op("dve", lambda h: h.memset(epsc[:], EPS), writes=[epsc])
        with ExitStack() as e1:
            wqkv = S.sb("wqkv", [128, 8, 1536], BF16, e1)
            load_weight_cast(k, wqkv, lambda kc, c0, c1: wqkv[:, kc, c0:c1], d["attn_w_qkv"], D, 1536, 1536)
            gq = S.sb("gq", [128, 128], F32, e1)
            gk = S.sb("gk", [128, 128], F32, e1)
            S.dma("sp", [(gq[:], d["attn_q_norm"].rearrange("(o n) -> o n", o=1).partition_broadcast(128))], gq, writes=[gq])
            S.dma("sp", [(gk[:], d["attn_k_norm"].rearrange("(o n) -> o n", o=1).partition_broadcast(128))], gk, writes=[gk])
            S.op("dve", lambda h: h.tensor_reduce(out=negm[:, 0:1], in_=gq[:], axis=AX.X, op=ALU.max, apply_absolute_value=True), reads=[gq], writes=[negm])
            S.op("dve", lambda h: h.tensor_reduce(out=negm[:, 1:2], in_=gk[:], axis=AX.X, op=ALU.max, apply_absolute_value=True), reads=[gk], writes=[negm])
            S.op("dve", lambda h: h.scalar_tensor_tensor(out=negm[:, 2:3], in0=negm[:, 0:1], scalar=-float(128 ** 0.5), in1=negm[:, 1:2],
                                                         op0=ALU.mult, op1=ALU.mult), reads=[negm], writes=[negm])
            rts = [rope_prep(k, e1) for _ in range(2)]
            for rt in rts:
                rope_sincos(k, rt, rt["pf"][:, 2:3], rt["pf"], 1)
            pq = [S.ps(f"pq{q}", [128, 512], F32, e1) for q in range(3)]
            ptq = S.ps("ptq", [128, 1024], BF16, e1)
            ptrs = [S.ps(f"ptr{q}", [128, 512], F32, e1) for q in range(2)]
            sets = []
            for z in range(2):
                ebz = {"junk": S.sb("junk", [128, D], BF16, e1), "ssq": S.sb("ssq", [128, 2], F32, e1),
                       "rst": S.sb("rst", [128, 2], F32, e1), "epsc": epsc, "ptr": ptrs}
                sets.append((S.sb("hT", [128, 8, 128], BF16, e1), S.sb("xt", [128, D], F32, e1), ebz,
                             S.sb("sq", [128, 1280], F32, e1), S.sb("ssh", [128, 10], F32, e1), S.sb("rsh", [128, 10], F32, e1),
                             S.sb("qg", [128, 1280], F32, e1), S.sb("t1", [128, 1280], F32, e1), S.sb("t2", [128, 1280], F32, e1),
                             S.sb("qb", [128, 1280], BF16, e1), S.sb("qTs", [128, 8, 128], BF16, e1)))
            def stA(t):
                lat = t >= 2
                hT, xt, eb, sq, ss, rs, qg, t1, t2, qb, qTs = sets[t % 2]
                rt = rts[t % 2]
                S.dma("sp", [(xt[:], src[t * 128:(t + 1) * 128, :])], xt, writes=[xt])
                norm_prep(k, eb, xt, 0, hT, i, j, 0 if lat else 1, 0)
                for b in ((0, 1, 2) if lat else (2,)):
                    for kc in range(8):
                        S.op("pe", lambda h, b=b, kc=kc: h.matmul(pq[b][:], lhsT=hT[:, kc, :], rhs=wqkv[:, kc, b * 512:(b + 1) * 512],
                                                                   start=(kc == 0), stop=(kc == 7)), reads=[hT, wqkv], writes=[pq[b]])

            def stM(t):
                lat = t >= 2
                hT, xt, eb, sq, ss, rs, qg, t1, t2, qb, qTs = sets[t % 2]
                rt = rts[t % 2]
                S.op("act", lambda h, t=t: h.copy(out=V[:, t, :], in_=pq[2][:, 256:512]), reads=[pq[2]], writes=[V])
                S.op("act", lambda h: h.activation(out=sq[:, 0:256], in_=pq[2][:, 0:256], func=AF.Square), reads=[pq[2]], writes=[sq])
                nh = 10 if lat else 2
                if lat:
                    for b in range(2):
                        S.op("act", lambda h, b=b: h.activation(out=sq[:, 256 + b * 512:256 + (b + 1) * 512], in_=pq[b][:], func=AF.Square),
                             reads=[pq[b]], writes=[sq])
                W = nh * 128
                S.op("dve", lambda h: h.tensor_reduce(out=ss[:, 0:nh], in_=sq[:, 0:W].rearrange("p (h c) -> p h c", c=128), axis=AX.X, op=ALU.add),
                     reads=[sq], writes=[ss])
                S.op("act", lambda h: h.activation(out=rs[:, 0:nh], in_=ss[:, 0:nh], func=AF.Sqrt, bias=epsc[:, 0:1], scale=1.0 / 128), reads=[ss, epsc], writes=[rs])
                S.op("dve", lambda h: h.reciprocal(out=ss[:, 0:nh], in_=rs[:, 0:nh]), reads=[rs], writes=[ss])
                S.op("dve", lambda h: h.tensor_tensor(out=qg[:, 0:256].rearrange("p (h c) -> p h c", c=128), in0=pq[2][:, 0:256].rearrange("p (h c) -> p h c", c=128),
                                                      in1=ss[:, 0:2].unsqueeze(2).to_broadcast([128, 2, 128]), op=ALU.mult), reads=[pq[2], ss], writes=[qg])
                S.op("pool", lambda h: h.tensor_tensor(out=qg[:, 0:256].rearrange("p (h c) -> p h c", c=128), in0=qg[:, 0:256].rearrange("p (h c) -> p h c", c=128),
                                                       in1=gk[:].unsqueeze(1).to_broadcast([128, 2, 128]), op=ALU.mult), reads=[qg, gk], writes=[qg])
                if lat:
                    for b in range(2):
                        S.op("dve", lambda h, b=b: h.tensor_tensor(out=qg[:, 256 + b * 512:256 + (b + 1) * 512].rearrange("p (h c) -> p h c", c=128),
                                                                   in0=pq[b][:].rearrange("p (h c) -> p h c", c=128),
                                                                   in1=ss[:, 2 + 4 * b:6 + 4 * b].unsqueeze(2).to_broadcast([128, 4, 128]), op=ALU.mult),
                             reads=[pq[b], ss], writes=[qg])
                    S.op("pool", lambda h: h.tensor_tensor(out=qg[:, 256:1280].rearrange("p (h c) -> p h c", c=128), in0=qg[:, 256:1280].rearrange("p (h c) -> p h c", c=128),
                                                           in1=gq[:].unsqueeze(1).to_broadcast([128, 8, 128]), op=ALU.mult), reads=[qg, gq], writes=[qg])

            def stB(t):
                lat = t >= 2
                hT, xt, eb, sq, ss, rs, qg, t1, t2, qb, qTs = sets[t % 2]
                rt = rts[t % 2]
                if lat:
                    l = t - 2
                    S.op("dve", lambda h, l=l: h.tensor_scalar(out=rt["rowpos"][:], in0=rt["pf"][:, 1:2], scalar1=float(2 * l), scalar2=None, op0=ALU.add),
                         reads=[rt["pf"]], writes=[rt["rowpos"]])
                    rope_sincos(k, rt, rt["rowpos"][:, 0:1], rt["rowpos"], 0)
                    CTf = rt["CT"][:].rearrange("p b c -> p (b c)")
                    S.op("dve", lambda h: h.tensor_tensor(out=t1[:].rearrange("p (h c) -> p h c", c=128), in0=qg[:].rearrange("p (h c) -> p h c", c=128),
                                                          in1=CTf.unsqueeze(1).to_broadcast([128, 10, 128]), op=ALU.mult), reads=[qg, rt["CT"]], writes=[t1])
                    for ax in range(2):
                        qv = qg[:].rearrange("p (h a s c) -> p h a s c", a=2, s=2, c=32)[:, :, ax, ::-1, :]
                        tv = t2[:].rearrange("p (h a s c) -> p h a s c", a=2, s=2, c=32)[:, :, ax, :, :]
                        sv = rt["ST"][:, 2 * ax:2 * ax + 2, :].unsqueeze(1).to_broadcast([128, 10, 2, 32])
                        S.op("pool", lambda h, qv=qv, tv=tv, sv=sv: h.tensor_tensor(out=tv, in0=qv, in1=sv, op=ALU.mult), reads=[qg, rt["ST"]], writes=[t2])
                    S.op("dve", lambda h: h.tensor_tensor(out=qb[:], in0=t1[:], in1=t2[:], op=ALU.add), reads=[t1, t2], writes=[qb])
                else:
                    S.op("dve", lambda h: h.tensor_copy(out=qb[:, 0:256], in_=qg[:, 0:256]), reads=[qg], writes=[qb])
                for kh in range(2):
                    S.op("pe", lambda h, kh=kh: h.transpose(out=ptq[:, kh * 128:(kh + 1) * 128], in_=qb[:, kh * 128:(kh + 1) * 128], identity=k.identb[:]),
                         reads=[qb, k.identb], writes=[ptq])
                S.op("act", lambda h, t=t: h.copy(out=KT[:, :, t * 128:(t + 1) * 128], in_=ptq[:, 0:256].rearrange("p (h c) -> p h c", c=128)), reads=[ptq], writes=[KT])
                if lat:
                    for hh in range(8):
                        S.op("pe", lambda h, hh=hh: h.transpose(out=ptq[:, hh * 128:(hh + 1) * 128], in_=qb[:, 256 + hh * 128:256 + (hh + 1) * 128], identity=k.identb[:]),
                             reads=[qb, k.identb], writes=[ptq])
                    S.op("dve", lambda h: h.tensor_copy(out=qTs[:].rearrange("p h c -> p (h c)"), in_=ptq[:]), reads=[ptq], writes=[qTs])
                    S.dma("sp", [(d["qT_d"][t - 2], qTs[:])], qTs, reads=[qTs])

            stA(0)
            stM(0)
            for t in range(NT):
                if t + 1 < NT:
                    stA(t + 1)
                stB(t)
                if t + 1 < NT:
                    stM(t + 1)
        S.barrier()
        with ExitStack() as e2:
            gbc = load_gbc(k, e2, i, j)[0]
            NP2 = NT // 2
            PTt = [e2.enter_context(nc.sbuf_tensor(f"PTt{q}", [128, NT, 512], BF16)) for q in range(2)]
            PT = [[S.wrap(f"PT{q}_{p}", PTt[q]) for p in range(NP2)] for q in range(2)]
            qT = [S.sb(f"qT{q}", [128, 8, 128], BF16, e2) for q in range(2)]
            OT = [S.sb(f"OT{q}", [128, 512], BF16, e2) for q in range(2)]
            rcp = S.sb("rcp", [128, 512], F32, e2)
            onesb = S.sb("onesb", [128, 128], BF16, e2)
            S.op("dve", lambda h: h.memset(onesb[:], 1.0), writes=[onesb])
            xrs = [S.sb(f"xr{q}", [128, D], F32, e2) for q in range(2)]
            eb = {"junk": S.sb("junk", [128, D], BF16, e2), "ss2": S.sb("ss2", [128, 2], F32, e2),
                  "rs2": S.sb("rs2", [128, 4], F32, e2), "tmp": S.sb("tmp", [128, D], F32, e2), "epsc": epsc}
            pS = [S.ps(f"pS{q}", [128, 1024], F32, e2) for q in range(2)]
            pO = S.ps("pO", [128, 512], F32, e2)
            prs = S.ps("prs", [128, 512], F32, e2)
            py = [S.ps(f"py{n}", [128, 512], F32, e2) for n in range(2)]
            un = [0]
            for l in range(32):
                q_ = qT[l % 2]
                xr = xrs[l % 2]
                S.dma("sp", [(q_[:], d["qT_d"][l])], q_, writes=[q_])
                S.dma("sp", [(xr[:], src[(l + 2) * 128:(l + 3) * 128, :])], xr, writes=[xr])
                for kvh in range(2):
                    par = un[0] % 2
                    un[0] += 1
                    ptt, ptb = PTt[par], PT[par]

                    def pv(p):
                        for z in range(2):
                            kt = 2 * p + z
                            S.op("pe", lambda h, kt=kt: h.matmul(pO[:], lhsT=V[:, kt, kvh * 128:(kvh + 1) * 128], rhs=ptt[:, kt, :],
                                                                 start=(kt == 0), stop=(kt == NT - 1)), reads=[V, ptb[p]], writes=[pO])
                            S.op("pe", lambda h, kt=kt: h.matmul(prs[:], lhsT=onesb[:], rhs=ptt[:, kt, :],
                                                                 start=(kt == 0), stop=(kt == NT - 1)), reads=[onesb, ptb[p]], writes=[prs])
                    for p in range(NP2):
                        ps = pS[p % 2]
                        for z in range(2):
                            kt = 2 * p + z
                            S.op("pe", lambda h, ps=ps, z=z, kt=kt: h.matmul(
                                ps[:, z * 512:(z + 1) * 512], lhsT=KT[:, kvh, kt * 128:(kt + 1) * 128],
                                rhs=q_[:, kvh * 4:(kvh + 1) * 4, :].rearrange("p h c -> p (h c)"), start=True, stop=True),
                                reads=[KT, q_], writes=[ps])
                        S.op("act", lambda h, ps=ps, p=p: h.activation(
                            out=ptt[:, 2 * p:2 * p + 2, :].rearrange("p a c -> p (a c)"), in_=ps[:], func=AF.Exp, scale=ATT_SCALE, bias=negm[:, 2:3]),
                            reads=[ps, negm], writes=[ptb[p]])
                        if p >= 2:
                            pv(p - 2)
                    pv(NP2 - 2)
                    pv(NP2 - 1)
                    S.op("dve", lambda h: h.reciprocal(out=rcp[:], in_=prs[:]), reads=[prs], writes=[rcp])
                    S.op("dve", lambda h, kvh=kvh: h.tensor_tensor(out=OT[kvh][:], in0=pO[:], in1=rcp[:], op=ALU.mult), reads=[pO, rcp], writes=[OT[kvh]])
                for n in range(2):
                    for hh in range(8):
                        S.op("pe", lambda h, hh=hh, n=n: h.matmul(py[n][:], lhsT=OT[hh // 4][:, (hh % 4) * 128:(hh % 4 + 1) * 128], rhs=wo[:, hh, n * 512:(n + 1) * 512],
                                                                  start=(hh == 0), stop=(hh == 7)), reads=[OT[hh // 4], wo], writes=[py[n]])
                post_res(k, eb, py, xr, gbc, dst[(l + 2) * 128:(l + 3) * 128, :])
        S.barrier()


S5_CHUNKS = [(0, 256)] + [(256 + 512 * q, 512) for q in range(8)]


def tab(T, dd, c0, cn):
    if dd == 0:
        return T[:, c0:c0 + cn]
    hi = (255 - c0) if c0 < 256 else (4607 - c0)
    lo = hi - cn
    return T[:, hi:(lo if lo >= 0 else None):-1]


def s5_params(k, es):
    S, d = k.S, k.d
    P = {}

    def col(name, n=16):
        return S.sb(name, [128, n], F32, es)
    lr, li, ls = col("lr"), col("li"), col("ls")
    S.dma("sp", [(lr[:, dd * 8:(dd + 1) * 8], d["s5_lam_re"][dd].rearrange("g p -> (g p)").rearrange("(ct q) -> q ct", q=128)) for dd in range(2)],
          lr, writes=[lr], allow_slow_non_contiguous=True)
    S.dma("sp", [(li[:, dd * 8:(dd + 1) * 8], d["s5_lam_im"][dd].rearrange("g p -> (g p)").rearrange("(ct q) -> q ct", q=128)) for dd in range(2)],
          li, writes=[li], allow_slow_non_contiguous=True)
    S.dma("sp", [(ls[gl * 64:(gl + 1) * 64, dd * 8:(dd + 1) * 8],
                  d["s5_log_step"][dd].rearrange("(ct gl) -> gl ct", gl=2)[gl:gl + 1, :].partition_broadcast(64))
                 for dd in range(2) for gl in range(2)], ls, writes=[ls], allow_slow_non_contiguous=True)
    dt, zr, zi, em1, er = col("dt"), col("zr"), col("zi"), col("em1"), col("er")
    S.op("act", lambda h: h.activation(out=dt[:], in_=ls[:], func=AF.Exp), reads=[ls], writes=[dt])
    S.op("dve", lambda h: h.tensor_tensor(out=zr[:], in0=lr[:], in1=dt[:], op=ALU.mult), reads=[lr, dt], writes=[zr])
    S.op("dve", lambda h: h.tensor_tensor(out=zi[:], in0=li[:], in1=dt[:], op=ALU.mult), reads=[li, dt], writes=[zi])
    S.op("dve", lambda h: h.tensor_scalar(out=em1[:], in0=zr[:], scalar1=1.0 / 6, scalar2=1.0, op0=ALU.mult, op1=ALU.add), reads=[zr], writes=[em1])
    for n in (5, 4, 3, 2):
        S.op("dve", lambda h: h.tensor_tensor(out=em1[:], in0=em1[:], in1=zr[:], op=ALU.mult), reads=[em1, zr], writes=[em1])
        S.op("dve", lambda h, n=n: h.tensor_scalar(out=em1[:], in0=em1[:], scalar1=1.0 / n, scalar2=1.0, op0=ALU.mult, op1=ALU.add), reads=[em1], writes=[em1])
    S.op("dve", lambda h: h.tensor_tensor(out=em1[:], in0=em1[:], in1=zr[:], op=ALU.mult), reads=[em1, zr], writes=[em1])
    S.op("dve", lambda h: h.tensor_scalar(out=er[:], in0=em1[:], scalar1=1.0, scalar2=None, op0=ALU.add), reads=[em1], writes=[er])
    a, a2, sh, sn, cs, cm1 = col("a"), col("a2"), col("sh"), col("sn"), col("cs"), col("cm1")

    def reduce_turns(scale):
        S.op("dve", lambda h: h.tensor_scalar(out=a[:], in0=zi[:], scalar1=scale, scalar2=None, op0=ALU.mult), reads=[zi], writes=[a])
        S.op("dve", lambda h: h.tensor_scalar(out=a2[:], in0=a[:], scalar1=MAGIC, scalar2=-MAGIC, op0=ALU.add, op1=ALU.add), reads=[a], writes=[a2])
        S.op("dve", lambda h: h.tensor_tensor(out=a[:], in0=a[:], in1=a2[:], op=ALU.subtract), reads=[a, a2], writes=[a])
    reduce_turns(1.0 / (4 * np.pi))
    S.op("act", lambda h: h.activation(out=sh[:], in_=a[:], func=AF.Sin, scale=TWO_PI_SAFE), reads=[a], writes=[sh])
    S.op("dve", lambda h: h.scalar_tensor_tensor(out=cm1[:], in0=sh[:], scalar=-2.0, in1=sh[:], op0=ALU.mult, op1=ALU.mult), reads=[sh], writes=[cm1])
    reduce_turns(1.0 / (2 * np.pi))
    S.op("act", lambda h: h.activation(out=sn[:], in_=a[:], func=AF.Sin, scale=TWO_PI_SAFE), reads=[a], writes=[sn])
    S.op("dve", lambda h: h.tensor_scalar(out=cs[:], in0=cm1[:], scalar1=1.0, scalar2=None, op0=ALU.add), reads=[cm1], writes=[cs])
    l1r, l1i, den, qr, qi, t0 = col("l1r"), col("l1i"), col("den"), col("qr"), col("qi"), col("t0")
    S.op("dve", lambda h: h.tensor_tensor(out=l1r[:], in0=em1[:], in1=cs[:], op=ALU.mult), reads=[em1, cs], writes=[l1r])
    S.op("dve", lambda h: h.tensor_tensor(out=l1r[:], in0=l1r[:], in1=cm1[:], op=ALU.add), reads=[l1r, cm1], writes=[l1r])
    S.op("dve", lambda h: h.tensor_tensor(out=l1i[:], in0=er[:], in1=sn[:], op=ALU.mult), reads=[er, sn], writes=[l1i])
    S.op("dve", lambda h: h.tensor_tensor(out=den[:], in0=lr[:], in1=lr[:], op=ALU.mult), reads=[lr], writes=[den])
    S.op("dve", lambda h: h.tensor_tensor(out=t0[:], in0=li[:], in1=li[:], op=ALU.mult), reads=[li], writes=[t0])
    S.op("dve", lambda h: h.tensor_tensor(out=den[:], in0=den[:], in1=t0[:], op=ALU.add), reads=[den, t0], writes=[den])
    S.op("dve", lambda h: h.reciprocal(out=den[:], in_=den[:]), reads=[den], writes=[den])
    S.op("dve", lambda h: h.tensor_tensor(out=qr[:], in0=l1r[:], in1=lr[:], op=ALU.mult), reads=[l1r, lr], writes=[qr])
    S.op("dve", lambda h: h.tensor_tensor(out=t0[:], in0=l1i[:], in1=li[:], op=ALU.mult), reads=[l1i, li], writes=[t0])
    S.op("dve", lambda h: h.tensor_tensor(out=qr[:], in0=qr[:], in1=t0[:], op=ALU.add), reads=[qr, t0], writes=[qr])
    S.op("dve", lambda h: h.tensor_tensor(out=qr[:], in0=qr[:], in1=den[:], op=ALU.mult), reads=[qr, den], writes=[qr])
    S.op("dve", lambda h: h.tensor_tensor(out=qi[:], in0=l1i[:], in1=lr[:], op=ALU.mult), reads=[l1i, lr], writes=[qi])
    S.op("dve", lambda h: h.tensor_tensor(out=t0[:], in0=l1r[:], in1=li[:], op=ALU.mult), reads=[l1r, li], writes=[t0])
    S.op("dve", lambda h: h.tensor_tensor(out=qi[:], in0=qi[:], in1=t0[:], op=ALU.subtract), reads=[qi, t0], writes=[qi])
    S.op("dve", lambda h: h.tensor_tensor(out=qi[:], in0=qi[:], in1=den[:], op=ALU.mult), reads=[qi, den], writes=[qi])
    wr = S.sb("wr", [128, 13, 16], F32, es)
    wi = S.sb("wi", [128, 13, 16], F32, es)
    S.op("dve", lambda h: h.tensor_tensor(out=t0[:], in0=cs[:], in1=cs[:], op=ALU.mult), reads=[cs], writes=[t0])
    S.op("dve", lambda h: h.tensor_tensor(out=a[:], in0=sn[:], in1=sn[:], op=ALU.mult), reads=[sn], writes=[a])
    S.op("dve", lambda h: h.tensor_tensor(out=t0[:], in0=t0[:], in1=a[:], op=ALU.add), reads=[t0, a], writes=[t0])
    S.op("act", lambda h: h.activation(out=t0[:], in_=t0[:], func=AF.Sqrt), reads=[t0], writes=[t0])
    S.op("dve", lambda h: h.reciprocal(out=t0[:], in_=t0[:]), reads=[t0], writes=[t0])
    S.op("dve", lambda h: h.tensor_tensor(out=wr[:, 0, :], in0=cs[:], in1=t0[:], op=ALU.mult), reads=[cs, t0], writes=[wr])
    S.op("dve", lambda h: h.tensor_tensor(out=wi[:, 0, :], in0=sn[:], in1=t0[:], op=ALU.mult), reads=[sn, t0], writes=[wi])
    for lv in range(12):
        S.op("dve", lambda h, lv=lv: h.tensor_tensor(out=a[:], in0=wr[:, lv, :], in1=wr[:, lv, :], op=ALU.mult), reads=[wr], writes=[a])
        S.op("dve", lambda h, lv=lv: h.tensor_tensor(out=a2[:], in0=wi[:, lv, :], in1=wi[:, lv, :], op=ALU.mult), reads=[wi], writes=[a2])
        S.op("dve", lambda h, lv=lv: h.scalar_tensor_tensor(out=wi[:, lv + 1, :], in0=wr[:, lv, :], scalar=2.0, in1=wi[:, lv, :], op0=ALU.mult, op1=ALU.mult),
             reads=[wr, wi], writes=[wi])
        S.op("dve", lambda h, lv=lv: h.tensor_tensor(out=wr[:, lv + 1, :], in0=a[:], in1=a2[:], op=ALU.subtract), reads=[a, a2], writes=[wr])
    P.update(er=er, qr=qr, qi=qi, wr=wr, wi=wi)
    P["bre"] = S.sb("bre", [128, 16, 16], F32, es)
    P["bim"] = S.sb("bim", [128, 16, 16], F32, es)
    for nm, key in (("bre", "s5_b_re"), ("bim", "s5_b_im")):
        S.dma("sp", [(P[nm][:, dd * 8:(dd + 1) * 8, :], d[key][dd].rearrange("g p c -> (g p) c").rearrange("(ct q) c -> q ct c", q=128)) for dd in range(2)],
              P[nm], writes=[P[nm]])
    P["cf"] = {}
    for nm, key in (("cre", "s5_c_re"), ("cim", "s5_c_im")):
        b = S.sb(nm, [128, 4, 128], F32, es)
        S.dma("sp", [(b[:, dd * 2 + ft, h2 * 64:(h2 + 1) * 64], d[key][dd].rearrange("g c p -> (g c) p")[ft * 128:(ft + 1) * 128, :])
                     for dd in range(2) for ft in range(2) for h2 in range(2)], b, writes=[b])
        P["cf"][nm] = b
    P["dcol"] = S.sb("dcol", [128, 2], F32, es)
    P["gbcol"] = S.sb("gbcol", [128, 2], F32, es)
    S.dma("sp", [(P["dcol"][:], d["s5_d"].rearrange("(ft p) -> p ft", p=128))], P["dcol"], writes=[P["dcol"]], allow_slow_non_contiguous=True)
    S.dma("sp", [(P["gbcol"][:], d["s5_glu_b"].rearrange("(ft p) -> p ft", p=128))], P["gbcol"], writes=[P["gbcol"]], allow_slow_non_contiguous=True)
    return P


def phase_mix0(k, src, dst):
    S, nc, d = k.S, k.nc, k.d
    i, j = 0, 1
    with ExitStack() as es:
        epsc = S.sb("epsc", [128, 1], F32, es)
        S.op("dve", lambda h: h.memset(epsc[:], EPS), writes=[epsc])
        ybT = S.sb("ybT", [128, 2, NTOK], BF16, es)
        with ExitStack() as e1:
            win = S.sb("win", [128, 8, 1792], BF16, e1)
            load_weight_cast(k, win, lambda kc, c0, c1: win[:, kc, c0:c1], d["ab_w_in"], D, 1792, 1792)
            wsr = S.sb("wsr", [128, 6, 128], F32, e1)
            wsT = S.sb("wsT", [128, 6, 128], BF16, e1)
            S.dma("sp", [(wsr[:], d["sgu_w"].rearrange("g t s -> t g s"))], wsr, writes=[wsr])
            sbc = S.sb("sbc", [128, 6], F32, e1)
            S.dma("sp", [(sbc[:], d["sgu_b"].rearrange("g t -> t g"))], sbc, writes=[sbc], allow_slow_non_contiguous=True)
            gsb = S.sb("gsb", [128, 768], F32, e1)
            S.dma("sp", [(gsb[:], d["sgu_norm_g"].rearrange("(o n) -> o n", o=1).partition_broadcast(128))], gsb, writes=[gsb])
            ptrs = [S.ps(f"ptr{q}", [128, 512], F32, e1) for q in range(2)]
            eb = {"ptr": ptrs}
            pq = [S.ps(f"pq{q}", [128, 512], F32, e1) for q in range(3)]
            pmA = S.ps("pmA", [128, 512], F32, e1)
            pBt = e1.enter_context(nc.psum_tensor("pBshared", [128, 512], F32))
            pmB = S.wrap("pmB", pBt)
            puT = S.wrap("puT", pBt)
            pya = S.ps("pya", [128, 1024], BF16, e1)
            for g in range(6):
                S.op("pe", lambda h, g=g: h.transpose(out=eb["ptr"][0][:, 0:128], in_=wsr[:, g, :], identity=k.identf[:]), reads=[wsr, k.identf], writes=[eb["ptr"][0]])
                S.op("dve", lambda h, g=g: h.tensor_copy(out=wsT[:, g, :], in_=eb["ptr"][0][:, 0:128]), reads=[eb["ptr"][0]], writes=[wsT])
            sets = []
            for z in range(2):
                ebz = {"junk": S.sb("junk", [128, D], BF16, e1), "ssq": S.sb("ssq", [128, 2], F32, e1),
                       "rst": S.sb("rst", [128, 2], F32, e1), "epsc": epsc, "ptr": ptrs}
                sets.append((S.sb("hT", [128, 8, 128], BF16, e1), S.sb("xt", [128, D], F32, e1), ebz,
                             S.sb("ug", [128, 768], F32, e1), S.sb("vg", [128, 768], F32, e1), S.sb("vh", [128, 768], BF16, e1),
                             S.sb("st6", [128, 6, 6], F32, e1), S.sb("mv", [128, 6, 2], F32, e1), S.sb("rsd", [128, 6], F32, e1),
                             S.sb("nmr", [128, 6], F32, e1), S.sb("tma", [128, 768], F32, e1), S.sb("yab", [128, 768], BF16, e1),
                             S.sb("yaTs", [128, 6, 128], BF16, e1), S.sb("uTs", [128, 2, 128], F32, e1)))
            GEL = AF.Gelu_apprx_tanh
            def mA(t):
                hT, xt, eb, ug, vg, vh, st6, mv, rsd, nmr, tma, yab, yaTs, uTs = sets[t % 2]
                S.dma("sp", [(xt[:], src[t * 128:(t + 1) * 128, :])], xt, writes=[xt])
                norm_prep(k, eb, xt, 0, hT, i, j, 1 if t < 2 else 0, 0)
                for b in range(3):
                    for kc in range(8):
                        S.op("pe", lambda h, b=b, kc=kc: h.matmul(pq[b][:], lhsT=hT[:, kc, :], rhs=win[:, kc, b * 512:(b + 1) * 512],
                                                                   start=(kc == 0), stop=(kc == 7)), reads=[hT, win], writes=[pq[b]])
                for ft in range(2):
                    for kc in range(8):
                        S.op("pe", lambda h, ft=ft, kc=kc: h.matmul(puT[:, 256 + ft * 128:256 + (ft + 1) * 128], lhsT=win[:, kc, 1536 + ft * 128:1536 + (ft + 1) * 128],
                                                                     rhs=hT[:, kc, :], start=(kc == 0), stop=(kc == 7)), reads=[hT, win], writes=[puT])

            def mM(t):
                hT, xt, eb, ug, vg, vh, st6, mv, rsd, nmr, tma, yab, yaTs, uTs = sets[t % 2]
                S.op("dve", lambda h: h.tensor_copy(out=uTs[:].rearrange("p f c -> p (f c)"), in_=puT[:, 256:512]), reads=[puT], writes=[uTs])
                S.dma("sp", [(d["uT_d"][:, :, t * 128:(t + 1) * 128].rearrange("f p c -> p f c"), uTs[:])], uTs, reads=[uTs])
                S.op("act", lambda h: h.activation(out=ug[:, 0:512], in_=pq[0][:], func=GEL), reads=[pq[0]], writes=[ug])
                S.op("act", lambda h: h.activation(out=ug[:, 512:768], in_=pq[1][:, 0:256], func=GEL), reads=[pq[1]], writes=[ug])
                S.op("act", lambda h: h.activation(out=vg[:, 0:256], in_=pq[1][:, 256:512], func=GEL), reads=[pq[1]], writes=[vg])
                S.op("act", lambda h: h.activation(out=vg[:, 256:768], in_=pq[2][:], func=GEL), reads=[pq[2]], writes=[vg])

            def mB(t):
                hT, xt, eb, ug, vg, vh, st6, mv, rsd, nmr, tma, yab, yaTs, uTs = sets[t % 2]
                for g in range(6):
                    S.op("dve", lambda h, g=g: h.bn_stats(out=st6[:, g, :], in_=vg[:, g * 128:(g + 1) * 128]), reads=[vg], writes=[st6])
                for g in range(6):
                    S.op("dve", lambda h, g=g: h.bn_aggr(out=mv[:, g, :], in_=st6[:, g, :]), reads=[st6], writes=[mv])
                S.op("act", lambda h: h.activation(out=rsd[:], in_=mv[:, :, 1], func=AF.Sqrt, bias=epsc[:, 0:1], scale=1.0), reads=[mv, epsc], writes=[rsd])
                S.op("dve", lambda h: h.reciprocal(out=rsd[:], in_=rsd[:]), reads=[rsd], writes=[rsd])
                S.op("dve", lambda h: h.scalar_tensor_tensor(out=nmr[:], in0=mv[:, :, 0], scalar=-1.0, in1=rsd[:], op0=ALU.mult, op1=ALU.mult), reads=[mv, rsd], writes=[nmr])
                for g in range(6):
                    if g % 2 == 0:
                        S.op("act", lambda h, g=g: h.activation(out=vh[:, g * 128:(g + 1) * 128], in_=vg[:, g * 128:(g + 1) * 128], func=AF.Identity,
                                                                 scale=rsd[:, g:g + 1], bias=nmr[:, g:g + 1]), reads=[vg, rsd, nmr], writes=[vh])
                    else:
                        S.op("dve", lambda h, g=g: h.tensor_scalar(out=vh[:, g * 128:(g + 1) * 128], in0=vg[:, g * 128:(g + 1) * 128],
                                                                   scalar1=rsd[:, g:g + 1], scalar2=nmr[:, g:g + 1], op0=ALU.mult, op1=ALU.add), reads=[vg, rsd, nmr], writes=[vh])
                for g in range(6):
                    pm, c0 = (pmA, g * 128) if g < 4 else (pmB, (g - 4) * 128)
                    S.op("pe", lambda h, g=g, pm=pm, c0=c0: h.matmul(pm[:, c0:c0 + 128], lhsT=wsT[:, g, :], rhs=vh[:, g * 128:(g + 1) * 128], start=True, stop=True),
                         reads=[wsT, vh], writes=[pm])
                S.op("dve", lambda h: h.tensor_tensor(out=tma[:, 0:512], in0=pmA[:], in1=gsb[:, 0:512], op=ALU.mult), reads=[pmA, gsb], writes=[tma])
                S.op("dve", lambda h: h.tensor_tensor(out=tma[:, 512:768], in0=pmB[:, 0:256], in1=gsb[:, 512:768], op=ALU.mult), reads=[pmB, gsb], writes=[tma])
                for g in range(6):
                    S.op("dve" if g % 2 else "pool", lambda h, g=g: (h.scalar_tensor_tensor(
                        out=yab[:, g * 128:(g + 1) * 128], in0=tma[:, g * 128:(g + 1) * 128], scalar=sbc[:, g:g + 1], in1=ug[:, g * 128:(g + 1) * 128],
                        op0=ALU.add, op1=ALU.mult)), reads=[tma, sbc, ug], writes=[yab]) if g % 2 else None
                    if g % 2 == 0:
                        S.op("dve", lambda h, g=g: h.scalar_tensor_tensor(
                            out=yab[:, g * 128:(g + 1) * 128], in0=tma[:, g * 128:(g + 1) * 128], scalar=sbc[:, g:g + 1], in1=ug[:, g * 128:(g + 1) * 128],
                            op0=ALU.add, op1=ALU.mult), reads=[tma, sbc, ug], writes=[yab])
                for g in range(6):
                    S.op("pe", lambda h, g=g: h.transpose(out=pya[:, g * 128:(g + 1) * 128], in_=yab[:, g * 128:(g + 1) * 128], identity=k.identb[:]),
                         reads=[yab, k.identb], writes=[pya])
                S.op("act", lambda h: h.copy(out=yaTs[:].rearrange("p g c -> p (g c)"), in_=pya[:, 0:768]), reads=[pya], writes=[yaTs])
                S.dma("sp", [(d["yaT_d"][t], yaTs[:])], yaTs, reads=[yaTs])

            mA(0)
            mM(0)
            for t in range(NT):
                if t + 1 < NT:
                    mA(t + 1)
                mB(t)
                if t + 1 < NT:
                    mM(t + 1)
        S.barrier()
        with ExitStack() as e2:
            Pm = s5_params(k, e2)
            gluw = S.sb("gluw", [128, 2, 256], BF16, e2)
            load_weight_cast(k, gluw, lambda kc, c0, c1: gluw[:, kc, c0:c1], d["s5_glu_w"], 256, 256, 256)
            uT = S.sb("uT", [128, NTOK], F32, e2)
            yc = S.sb("yc", [128, NTOK], F32, e2)
            ygb = S.sb("ygb", [128, 2, NTOK], BF16, e2)
            Tc = S.sb("Tc", [128, NTOK], F32, e2)
            Ts = S.sb("Ts", [128, NTOK], F32, e2)
            bpr = S.sb("bpr", [128, NTOK], F32, e2)
            bpi = S.sb("bpi", [128, NTOK], F32, e2)
            gr, gi = bpr, bpi
            tw1 = S.sb("tw1", [128, 1024], F32, e2)
            tw2 = S.sb("tw2", [128, 1024], F32, e2)
            tw3 = S.sb("tw3", [128, 1024], F32, e2)
            tw4 = S.sb("tw4", [128, 1024], F32, e2)
            Bpad = [S.sb(f"Bpad{q}", [128, 128], F32, e2) for q in range(2)]
            Bl = [S.sb(f"Bl{q}", [128, 128], F32, e2) for q in range(2)]
            Cpad = [S.sb(f"Cpad{q}", [128, 128], F32, e2) for q in range(2)]
            cT = {nm: S.sb("cT" + nm, [128, 4, 128], F32, e2) for nm in ("cre", "cim")}
            tq = S.sb("tq", [128, 16], F32, e2)
            ck = {n: [S.sb(f"{n}{q}", [128, 512], F32, e2) for q in range(2)] for n in ("br", "bi", "hr", "hi")}
            tt = [S.sb(f"tt{q}", [128, 512], F32, e2) for q in range(4)]
            sgm = S.sb("sgm", [128, 512], F32, e2)
            pbr = [S.ps(f"pbr{q}", [128, 512], F32, e2) for q in range(2)]
            pbi = [S.ps(f"pbi{q}", [128, 512], F32, e2) for q in range(2)]
            pyc = [S.ps(f"pyc{q}", [128, 512], F32, e2) for q in range(2)]
            ptb = S.ps("ptb", [128, 128], F32, e2)
            for nm in ("cre", "cim"):
                for q in range(4):
                    S.op("pe", lambda h, nm=nm, q=q: h.transpose(out=ptb[:], in_=Pm["cf"][nm][:, q, :], identity=k.identf[:]), reads=[Pm["cf"][nm], k.identf], writes=[ptb])
                    S.op("dve", lambda h, nm=nm, q=q: h.tensor_copy(out=cT[nm][:, q, :], in_=ptb[:]), reads=[ptb], writes=[cT[nm]])
            cc = [0]
            for ft in range(2):
                S.dma("sp", [(uT[:], d["uT_d"][ft])], uT, writes=[uT])
                first = True
                for ctl in range(4):
                    ct = ft * 4 + ctl
                    for dd in range(2):
                        ix = dd * 8 + ct
                        for part in range(2):
                            S.op("pool", lambda h, part=part: h.memset(Bpad[part][:], 0.0), writes=[Bpad[part]])
                            S.op("pool", lambda h, part=part: h.memset(Cpad[part][:], 0.0), writes=[Cpad[part]])
                        S.op("dve", lambda h, ix=ix: h.tensor_scalar(out=tq[:], in0=Pm["bim"][:, ix, :], scalar1=Pm["qi"][:, ix:ix + 1], scalar2=None, op0=ALU.mult),
                             reads=[Pm["bim"], Pm["qi"]], writes=[tq])
                        for gl in range(2):
                            cb = (2 * ctl + gl) * 16
                            sl = slice(gl * 64, (gl + 1) * 64)
                            S.op("dve", lambda h, ix=ix, sl=sl, cb=cb: h.scalar_tensor_tensor(
                                out=Bpad[0][sl, cb:cb + 16], in0=Pm["bre"][sl, ix, :], scalar=Pm["qr"][sl, ix:ix + 1], in1=tq[sl, :], op0=ALU.mult, op1=ALU.subtract),
                                reads=[Pm["bre"], Pm["qr"], tq], writes=[Bpad[0]])
                        S.op("dve", lambda h, ix=ix: h.tensor_scalar(out=tq[:], in0=Pm["bre"][:, ix, :], scalar1=Pm["qi"][:, ix:ix + 1], scalar2=None, op0=ALU.mult),
                             reads=[Pm["bre"], Pm["qi"], Bpad[0]], writes=[tq])
                        for gl in range(2):
                            cb = (2 * ctl + gl) * 16
                            sl = slice(gl * 64, (gl + 1) * 64)
                            S.op("dve", lambda h, ix=ix, sl=sl, cb=cb: h.scalar_tensor_tensor(
                                out=Bpad[1][sl, cb:cb + 16], in0=Pm["bim"][sl, ix, :], scalar=Pm["qr"][sl, ix:ix + 1], in1=tq[sl, :], op0=ALU.mult, op1=ALU.add),
                                reads=[Pm["bim"], Pm["qr"], tq], writes=[Bpad[1]])
                            S.op("dve", lambda h, sl=sl, cb=cb, dd=dd: h.tensor_copy(out=Cpad[0][sl, cb:cb + 16], in_=cT["cre"][sl, dd * 2 + ft, cb:cb + 16]),
                                 reads=[cT["cre"]], writes=[Cpad[0]])
                            S.op("dve", lambda h, sl=sl, cb=cb, dd=dd: h.tensor_scalar(out=Cpad[1][sl, cb:cb + 16], in0=cT["cim"][sl, dd * 2 + ft, cb:cb + 16],
                                                                                      scalar1=-1.0, scalar2=None, op0=ALU.mult), reads=[cT["cim"]], writes=[Cpad[1]])
                        for part in range(2):
                            S.op("pe", lambda h, part=part: h.transpose(out=ptb[:], in_=Bpad[part][:], identity=k.identf[:]), reads=[Bpad[part], k.identf], writes=[ptb])
                            S.op("act", lambda h, part=part: h.copy(out=Bl[part][:], in_=ptb[:]), reads=[ptb], writes=[Bl[part]])
                        S.op("dve", lambda h: h.memset(Tc[:, 0:1], 1.0), writes=[Tc])
                        S.op("dve", lambda h: h.memset(Ts[:, 0:1], 0.0), writes=[Ts])
                        for lv in range(13):
                            n = 1 << lv
                            mt = min(n, NTOK - n)
                            wr_ = Pm["wr"][:, lv, ix:ix + 1]
                            wi_ = Pm["wi"][:, lv, ix:ix + 1]
                            for o in range(0, mt, 1024):
                                m = min(1024, mt - o)
                                S.op("dve", lambda h, m=m, o=o, wi_=wi_: h.tensor_scalar(out=tw1[:, 0:m], in0=Ts[:, o:o + m], scalar1=wi_, scalar2=None, op0=ALU.mult), reads=[Ts, Pm["wi"]], writes=[tw1])
                                S.op("dve", lambda h, m=m, o=o, wi_=wi_: h.tensor_scalar(out=tw2[:, 0:m], in0=Tc[:, o:o + m], scalar1=wi_, scalar2=None, op0=ALU.mult), reads=[Tc, Pm["wi"]], writes=[tw2])
                                S.op("dve", lambda h, n=n, m=m, o=o, wr_=wr_: h.scalar_tensor_tensor(out=Tc[:, n + o:n + o + m], in0=Tc[:, o:o + m], scalar=wr_, in1=tw1[:, 0:m], op0=ALU.mult, op1=ALU.subtract),
                                     reads=[Tc, Pm["wr"], tw1], writes=[Tc])
                                S.op("dve", lambda h, n=n, m=m, o=o, wr_=wr_: h.scalar_tensor_tensor(out=Ts[:, n + o:n + o + m], in0=Ts[:, o:o + m], scalar=wr_, in1=tw2[:, 0:m], op0=ALU.mult, op1=ALU.add),
                                     reads=[Ts, Pm["wr"], tw2], writes=[Ts])
                        for (c0, cn) in S5_CHUNKS:
                            q2 = cc[0] % 2
                            cc[0] += 1
                            S.op("pe", lambda h, q2=q2, c0=c0, cn=cn: h.matmul(pbr[q2][:, 0:cn], lhsT=Bl[0][:], rhs=uT[:, c0:c0 + cn], start=True, stop=True), reads=[Bl[0], uT], writes=[pbr[q2]])
                            S.op("pe", lambda h, q2=q2, c0=c0, cn=cn: h.matmul(pbi[q2][:, 0:cn], lhsT=Bl[1][:], rhs=uT[:, c0:c0 + cn], start=True, stop=True), reads=[Bl[1], uT], writes=[pbi[q2]])
                            br, bi = ck["br"][q2], ck["bi"][q2]
                            S.op("act", lambda h, q2=q2, cn=cn, br=br: h.copy(out=br[:, 0:cn], in_=pbr[q2][:, 0:cn]), reads=[pbr[q2]], writes=[br])
                            S.op("act", lambda h, q2=q2, cn=cn, bi=bi: h.copy(out=bi[:, 0:cn], in_=pbi[q2][:, 0:cn]), reads=[pbi[q2]], writes=[bi])
                            tc_, ts_ = tab(Tc, dd, c0, cn), tab(Ts, dd, c0, cn)
                            S.op("dve", lambda h, cn=cn, br=br, tc_=tc_: h.tensor_tensor(out=tt[0][:, 0:cn], in0=br[:, 0:cn], in1=tc_, op=ALU.mult), reads=[br, Tc], writes=[tt[0]])
                            S.op("dve", lambda h, cn=cn, bi=bi, ts_=ts_: h.tensor_tensor(out=tt[1][:, 0:cn], in0=bi[:, 0:cn], in1=ts_, op=ALU.mult), reads=[bi, Ts], writes=[tt[1]])
                            S.op("dve", lambda h, c0=c0, cn=cn: h.tensor_tensor(out=bpr[:, c0:c0 + cn], in0=tt[0][:, 0:cn], in1=tt[1][:, 0:cn], op=ALU.add), reads=[tt[0], tt[1]], writes=[bpr])
                            S.op("pool", lambda h, cn=cn, bi=bi, tc_=tc_: h.tensor_tensor(out=tt[2][:, 0:cn], in0=bi[:, 0:cn], in1=tc_, op=ALU.mult), reads=[bi, Tc], writes=[tt[2]])
                            S.op("pool", lambda h, cn=cn, br=br, ts_=ts_: h.tensor_tensor(out=tt[3][:, 0:cn], in0=br[:, 0:cn], in1=ts_, op=ALU.mult), reads=[br, Ts], writes=[tt[3]])
                            S.op("pool", lambda h, c0=c0, cn=cn: h.tensor_tensor(out=bpi[:, c0:c0 + cn], in0=tt[2][:, 0:cn], in1=tt[3][:, 0:cn], op=ALU.subtract), reads=[tt[2], tt[3]], writes=[bpi])
                        erb = Pm["er"][:, ix:ix + 1]
                        for (src_b, dst_b) in ((bpr, bpr), (bpi, bpi)):
                            if dd == 0:
                                S.op("dve", lambda h, src_b=src_b, dst_b=dst_b: h.tensor_tensor_scan(
                                    out=dst_b[:], data0=erb.to_broadcast([128, NTOK]), data1=src_b[:], initial=0.0, op0=ALU.mult, op1=ALU.add),
                                    reads=[src_b, Pm["er"]], writes=[dst_b])
                            else:
                                S.op("dve", lambda h, src_b=src_b, dst_b=dst_b: h.tensor_tensor_scan(
                                    out=dst_b[:, 255::-1], data0=erb.to_broadcast([128, 256]), data1=src_b[:, 255::-1], initial=0.0, op0=ALU.mult, op1=ALU.add),
                                    reads=[src_b, Pm["er"]], writes=[dst_b])
                                S.op("dve", lambda h, src_b=src_b, dst_b=dst_b: h.tensor_tensor_scan(
                                    out=dst_b[:, NTOK - 1:255:-1], data0=erb.to_broadcast([128, NTOK - 256]), data1=src_b[:, NTOK - 1:255:-1],
                                    initial=dst_b[:, 0:1], op0=ALU.mult, op1=ALU.add), reads=[src_b, Pm["er"]], writes=[dst_b])
                        for (c0, cn) in S5_CHUNKS:
                            q2 = cc[0] % 2
                            cc[0] += 1
                            tc_, ts_ = tab(Tc, dd, c0, cn), tab(Ts, dd, c0, cn)
                            hr, hi = ck["hr"][q2], ck["hi"][q2]
                            S.op("dve", lambda h, c0=c0, cn=cn, tc_=tc_: h.tensor_tensor(out=tt[0][:, 0:cn], in0=gr[:, c0:c0 + cn], in1=tc_, op=ALU.mult), reads=[gr, Tc], writes=[tt[0]])
                            S.op("dve", lambda h, c0=c0, cn=cn, ts_=ts_: h.tensor_tensor(out=tt[1][:, 0:cn], in0=gi[:, c0:c0 + cn], in1=ts_, op=ALU.mult), reads=[gi, Ts], writes=[tt[1]])
                            S.op("dve", lambda h, cn=cn, hr=hr: h.tensor_tensor(out=hr[:, 0:cn], in0=tt[0][:, 0:cn], in1=tt[1][:, 0:cn], op=ALU.subtract), reads=[tt[0], tt[1]], writes=[hr])
                            S.op("pool", lambda h, c0=c0, cn=cn, ts_=ts_: h.tensor_tensor(out=tt[2][:, 0:cn], in0=gr[:, c0:c0 + cn], in1=ts_, op=ALU.mult), reads=[gr, Ts], writes=[tt[2]])
                            S.op("pool", lambda h, c0=c0, cn=cn, tc_=tc_: h.tensor_tensor(out=tt[3][:, 0:cn], in0=gi[:, c0:c0 + cn], in1=tc_, op=ALU.mult), reads=[gi, Tc], writes=[tt[3]])
                            S.op("pool", lambda h, cn=cn, hi=hi: h.tensor_tensor(out=hi[:, 0:cn], in0=tt[2][:, 0:cn], in1=tt[3][:, 0:cn], op=ALU.add), reads=[tt[2], tt[3]], writes=[hi])
                            S.op("pe", lambda h, q2=q2, cn=cn, hr=hr: h.matmul(pyc[q2][:, 0:cn], lhsT=Cpad[0][:], rhs=hr[:, 0:cn], start=True, stop=False), reads=[Cpad[0], hr], writes=[pyc[q2]])
                            S.op("pe", lambda h, q2=q2, cn=cn, hi=hi: h.matmul(pyc[q2][:, 0:cn], lhsT=Cpad[1][:], rhs=hi[:, 0:cn], start=False, stop=True), reads=[Cpad[1], hi], writes=[pyc[q2]])
                            if first:
                                S.op("act", lambda h, q2=q2, c0=c0, cn=cn: h.copy(out=yc[:, c0:c0 + cn], in_=pyc[q2][:, 0:cn]), reads=[pyc[q2]], writes=[yc])
                            else:
                                S.op("dve", lambda h, q2=q2, c0=c0, cn=cn: h.tensor_tensor(out=yc[:, c0:c0 + cn], in0=pyc[q2][:, 0:cn], in1=yc[:, c0:c0 + cn], op=ALU.add),
                                     reads=[pyc[q2], yc], writes=[yc])
                        first = False
                S.op("dve", lambda h, ft=ft: h.scalar_tensor_tensor(out=yc[:], in0=uT[:], scalar=Pm["dcol"][:, ft:ft + 1], in1=yc[:], op0=ALU.mult, op1=ALU.add),
                     reads=[uT, Pm["dcol"], yc], writes=[yc])
                S.op("act", lambda h, ft=ft: h.activation(out=ygb[:, ft, :], in_=yc[:], func=AF.Gelu_apprx_tanh), reads=[yc], writes=[ygb])
            for jt in range(2):
                for (c0, cn) in S5_CHUNKS:
                    q2 = cc[0] % 2
                    cc[0] += 1
                    for kc in range(2):
                        S.op("pe", lambda h, q2=q2, kc=kc, jt=jt, c0=c0, cn=cn: h.matmul(pyc[q2][:, 0:cn], lhsT=gluw[:, kc, jt * 128:(jt + 1) * 128], rhs=ygb[:, kc, c0:c0 + cn],
                                                                                          start=(kc == 0), stop=(kc == 1)), reads=[gluw, ygb], writes=[pyc[q2]])
                    S.op("act", lambda h, q2=q2, jt=jt, cn=cn: h.activation(out=sgm[:, 0:cn], in_=pyc[q2][:, 0:cn], func=AF.Sigmoid, bias=Pm["gbcol"][:, jt:jt + 1], scale=1.0),
                         reads=[pyc[q2], Pm["gbcol"]], writes=[sgm])
                    S.op("dve", lambda h, jt=jt, c0=c0, cn=cn: h.tensor_tensor(out=ybT[:, jt, c0:c0 + cn], in0=ygb[:, jt, c0:c0 + cn], in1=sgm[:, 0:cn], op=ALU.mult),
                         reads=[ygb, sgm], writes=[ybT])
        S.barrier()
        with ExitStack() as e3:
            wout = S.sb("wout", [128, 8, D], BF16, e3)
            load_weight_cast(k, wout, lambda kc, c0, c1: wout[:, kc, c0:c1], d["ab_w_out"], D, D, 1024)
            gbc = load_gbc(k, e3, i, j)
            yaT = [S.sb(f"yaT{q}", [128, 6, 128], BF16, e3) for q in range(2)]
            xrs = [S.sb(f"xr{q}", [128, D], F32, e3) for q in range(2)]
            eb = {"junk": S.sb("junk", [128, D], BF16, e3), "ss2": S.sb("ss2", [128, 2], F32, e3),
                  "rs2": S.sb("rs2", [128, 4], F32, e3), "tmp": S.sb("tmp", [128, D], F32, e3), "epsc": epsc}
            pys = [[S.ps(f"py{q}{n}", [128, 512], F32, e3) for n in range(2)] for q in range(2)]
            for t in range(NT):
                ya, xr, py = yaT[t % 2], xrs[t % 2], pys[t % 2]
                S.dma("sp", [(ya[:], d["yaT_d"][t])], ya, writes=[ya])
                S.dma("sp", [(xr[:], src[t * 128:(t + 1) * 128, :])], xr, writes=[xr])
                for n in range(2):
                    for kc in range(8):
                        lh = ya[:, kc, :] if kc < 6 else ybT[:, kc - 6, t * 128:(t + 1) * 128]
                        S.op("pe", lambda h, n=n, kc=kc, lh=lh, py=py: h.matmul(py[n][:], lhsT=lh, rhs=wout[:, kc, n * 512:(n + 1) * 512], start=(kc == 0), stop=(kc == 7)),
                             reads=[ya, ybT, wout], writes=[py[n]])
                post_res(k, eb, py, xr, gbc[1 if t < 2 else 0], dst[t * 128:(t + 1) * 128, :])
        S.barrier()


def build(debug=None):
    nc = bass.Bass("TRN2", target_bir_lowering=False)
    k = K()
    k.nc = nc
    k.d = {}
    for nm, shp in PARAM_SPECS:
        k.d[nm] = nc.dram_tensor(nm, list(shp), F32, kind="ExternalInput").ap()
    k.d["out"] = nc.dram_tensor("out", [4096, D], F32, kind="ExternalOutput").ap()
    k.d["grow_d"] = nc.dram_tensor("grow_d", [2, 3, 2, D], F32, kind="Internal").ap()
    k.d["qT_d"] = nc.dram_tensor("qT_d", [32, 128, 8, 128], BF16, kind="Internal").ap()
    k.d["uT_d"] = nc.dram_tensor("uT_d", [2, 128, NTOK], F32, kind="Internal").ap()
    k.d["yaT_d"] = nc.dram_tensor("yaT_d", [NT, 128, 6, 128], BF16, kind="Internal").ap()
    for nm in ("sA", "sB", "sC", "sD", "sE"):
        k.d[nm] = nc.dram_tensor(nm, [NTOK, D], F32, kind="Internal").ap()
    if debug:
        k.d["dbg_in"] = nc.dram_tensor("dbg_in", [NTOK, D], F32, kind="ExternalInput").ap()
        k.d["dbg_out"] = nc.dram_tensor("dbg_out", [NTOK, D], F32, kind="ExternalOutput").ap()
    with ExitStack() as es:
        k.S = Sched(nc, es)
        setup_globals(k)
        allt = list(range(NT))
        lat = list(range(2, NT))
        phase_mod(k)
        dd = k.d
        if debug is None:
            phase_ffn(k, 0, 0, dd["xs"], dd["sA"], allt)
            phase_mix0(k, dd["sA"], dd["sB"])
            phase_ffn(k, 0, 1, dd["sB"], dd["sC"], allt)
            phase_ffn(k, 1, 0, dd["sC"], dd["sD"], allt)
            phase_attn(k, dd["sD"], dd["sE"])
            phase_ffn(k, 1, 1, dd["sE"], dd["out"], lat, dst_off=-2)
        elif debug == "attn":
            phase_attn(k, dd["dbg_in"], dd["dbg_out"])
        elif debug == "mix0":
            phase_mix0(k, dd["dbg_in"], dd["dbg_out"])
        elif debug == "ffn":
            phase_ffn(k, 0, 0, dd["dbg_in"], dd["dbg_out"], allt)
        k.S.barrier()
        print("ninst", k.S.ninst, "nwait", k.S.nwait, "nsem", len(k.S.sems))
    return nc

def make_in_maps(inputs):
    f = lambda a: np.ascontiguousarray(np.asarray(a, dtype=np.float32))
    shared = {
        "w_mod": f(inputs["w_mod"]), "b_mod": f(inputs["b_mod"]), "norm_pre": f(inputs["norm_pre"]),
        "norm_post": f(inputs["norm_post"]), "ffn_w_in": f(inputs["ffn_w_in"]), "ffn_w_out": f(inputs["ffn_w_out"]),
        "ab_w_in": f(inputs["ab_w_in"][0]), "ab_w_out": f(inputs["ab_w_out"][0]), "sgu_norm_g": f(inputs["sgu_norm_g"][0]),
        "sgu_w": f(inputs["sgu_w"][0]), "sgu_b": f(inputs["sgu_b"][0]), "s5_lam_re": f(inputs["s5_lam_re"][0]),
        "s5_lam_im": f(inputs["s5_lam_im"][0]), "s5_log_step": f(inputs["s5_log_step"][0]),
        "s5_b_re": f(inputs["s5_b_re"][0]), "s5_b_im": f(inputs["s5_b_im"][0]), "s5_c_re": f(inputs["s5_c_re"][0]),
        "s5_c_im": f(inputs["s5_c_im"][0]), "s5_d": f(inputs["s5_d"][0]), "s5_glu_w": f(inputs["s5_glu_w"][0]),
        "s5_glu_b": f(inputs["s5_glu_b"][0]), "attn_w_qkv": f(inputs["attn_w_qkv"][0]), "attn_w_out": f(inputs["attn_w_out"][0]),
        "attn_q_norm": f(inputs["attn_q_norm"][0]), "attn_k_norm": f(inputs["attn_k_norm"][0]),
    }
    maps = []
    for b in range(8):
        m = dict(shared)
        m["xs"] = np.ascontiguousarray(np.concatenate([inputs["ctx"][b], inputs["x"][b]], axis=0).astype(np.float32))
        m["cond"] = np.ascontiguousarray(np.stack([inputs["c"][b], inputs["c_ctx"]], axis=0).astype(np.float32))
        maps.append(m)
    return maps


def kernel(**inputs):
    nc = build()
    maps = make_in_maps(inputs)
    res = run_bass_kernel_spmd(nc, maps, core_ids=list(range(8)))
    return np.stack([np.asarray(r["out"]) for r in res.results], axis=0).astype(np.float32)
```

```python
import numpy as np
from contextlib import ExitStack
import concourse.bass as bass
import concourse.mybir as mybir
from concourse.bass_utils import run_bass_kernel_spmd

F32 = mybir.dt.float32
BF16 = mybir.dt.bfloat16
I32 = mybir.dt.int32
AF = mybir.ActivationFunctionType
ALU = mybir.AluOpType
AX = mybir.AxisListType


class Buf:
    __slots__ = ("name", "t", "w", "r", "dsid", "dcnt")

    def __init__(self, name, t):
        self.name = name
        self.t = t
        self.w = None
        self.r = {}
        self.dsid = None
        self.dcnt = 0

    def __getitem__(self, idx):
        return self.t[idx]


class Sched:
    ENG = ("pe", "act", "dve", "pool", "sp")

    def __init__(self, nc, es):
        self.nc = nc
        self.es = es
        self.sems = []
        self.final = []
        self.e = {}
        hs = {"pe": nc.tensor, "act": nc.scalar, "dve": nc.vector, "pool": nc.gpsimd, "sp": nc.sync}
        for nm in self.ENG:
            sid = self._newsem("e_" + nm)
            self.e[nm] = {"h": hs[nm], "sid": sid, "cnt": 0, "seen": {}}
        self.bufs = []
        self.nwait = 0
        self.ninst = 0

    def _newsem(self, name):
        h = self.es.enter_context(self.nc.semaphore(name))
        self.sems.append(h)
        self.final.append(0)
        return len(self.sems) - 1

    def sb(self, name, shape, dt=F32, es=None):
        self.uid = getattr(self, "uid", 0) + 1
        name = f"{name}_{self.uid}"
        t = (es or self.es).enter_context(self.nc.sbuf_tensor(name, list(shape), dt))
        b = Buf(name, t)
        self.bufs.append(b)
        return b

    def ps(self, name, shape, dt=F32, es=None):
        self.uid = getattr(self, "uid", 0) + 1
        name = f"{name}_{self.uid}"
        t = (es or self.es).enter_context(self.nc.psum_tensor(name, list(shape), dt))
        b = Buf(name, t)
        self.bufs.append(b)
        return b

    def wrap(self, name, t):
        b = Buf(name, t)
        self.bufs.append(b)
        return b

    def _collect(self, reads, writes):
        deps = {}
        for b in reads:
            if b.w is not None:
                s, v = b.w
                if deps.get(s, 0) < v:
                    deps[s] = v
        for b in writes:
            if b.w is not None:
                s, v = b.w
                if deps.get(s, 0) < v:
                    deps[s] = v
            for s, v in b.r.items():
                if deps.get(s, 0) < v:
                    deps[s] = v
        return deps

    def _wait(self, eng, deps):
        E = self.e[eng]
        for s, v in deps.items():
            if eng == "pe" and s == E["sid"]:
                continue
            if E["seen"].get(s, 0) >= v:
                continue
            E["h"].wait_ge(self.sems[s], v)
            E["seen"][s] = v
            self.nwait += 1

    def op(self, eng, fn, reads=(), writes=()):
        E = self.e[eng]
        self._wait(eng, self._collect(reads, writes))
        inst = fn(E["h"])
        E["cnt"] += 1
        inst.then_inc(self.sems[E["sid"]], 1)
        self.final[E["sid"]] = E["cnt"]
        ev = (E["sid"], E["cnt"])
        for b in reads:
            if b.r.get(ev[0], 0) < ev[1]:
                b.r[ev[0]] = ev[1]
        for b in writes:
            b.w = ev
            b.r = {}
        self.ninst += 1
        return inst

    def dma(self, eng, pairs, semb, reads=(), writes=(), **kw):
        E = self.e[eng]
        self._wait(eng, self._collect(reads, writes))
        if semb.dsid is None:
            semb.dsid = self._newsem("d_" + semb.name)
        for (o, i) in pairs:
            E["h"].dma_start(out=o, in_=i, **kw).then_inc(self.sems[semb.dsid], 16)
            semb.dcnt += 16
            self.ninst += 1
        self.final[semb.dsid] = semb.dcnt
        ev = (semb.dsid, semb.dcnt)
        for b in reads:
            if b.r.get(ev[0], 0) < ev[1]:
                b.r[ev[0]] = ev[1]
        for b in writes:
            b.w = ev
            b.r = {}

    def barrier(self, engs=None):
        deps = {s: v for s, v in enumerate(self.final) if v > 0}
        for nm in (engs or self.ENG):
            self._wait(nm, deps)
        for b in self.bufs:
            b.w = None
            b.r = {}


D = 1024
NTOK = 4352
NT = NTOK // 128
DFF = 2816
NF = DFF // 128
EPS = 1e-6
RES_W = (0.5, 1.0, 0.5)

PARAM_SPECS = [
    ("xs", [NTOK, D]), ("cond", [2, D]),
    ("w_mod", [2, D, 9 * D]), ("b_mod", [2, 9 * D]), ("norm_pre", [2, 3, D]), ("norm_post", [2, 3, D]),
    ("ffn_w_in", [2, 2, D, 2 * DFF]), ("ffn_w_out", [2, 2, DFF, D]),
    ("ab_w_in", [D, 1792]), ("ab_w_out", [D, D]), ("sgu_norm_g", [768]), ("sgu_w", [6, 128, 128]),
    ("sgu_b", [6, 128]), ("s5_lam_re", [2, 16, 64]), ("s5_lam_im", [2, 16, 64]), ("s5_log_step", [2, 16]),
    ("s5_b_re", [2, 16, 64, 16]), ("s5_b_im", [2, 16, 64, 16]), ("s5_c_re", [2, 16, 16, 64]),
    ("s5_c_im", [2, 16, 16, 64]), ("s5_d", [256]), ("s5_glu_w", [256, 256]), ("s5_glu_b", [256]),
    ("attn_w_qkv", [D, 1536]), ("attn_w_out", [D, D]), ("attn_q_norm", [128]), ("attn_k_norm", [128]),
]


class K:
    pass


def setup_globals(k):
    S, nc = k.S, k.nc
    k.ii = S.sb("ii", [128, 128], I32)
    k.identf = S.sb("identf", [128, 128], F32)
    k.identb = S.sb("identb", [128, 128], BF16)
    S.op("pool", lambda h: h.iota(k.ii[:], pattern=[[1, 128]], base=0, channel_multiplier=-1), writes=[k.ii])
    S.op("dve", lambda h: h.tensor_single_scalar(out=k.identf[:], in_=k.ii[:], scalar=0, op=ALU.is_equal),
         reads=[k.ii], writes=[k.identf])
    S.op("dve", lambda h: h.tensor_copy(out=k.identb[:], in_=k.identf[:]), reads=[k.identf], writes=[k.identb])
    k.acol = S.sb("acol", [128, 2, 3, 2, 8], F32)
    k.bcol = S.sb("bcol", [128, 2, 3, 2, 8], F32)


def phase_mod(k):
    S, nc, d = k.S, k.nc, k.d
    with ExitStack() as es:
        condT = S.sb("condT", [128, 8, 2], F32, es)
        gpre = S.sb("gpre", [128, 2, 3, 8], F32, es)
        modrow = S.sb("modrow", [2, 9 * D], F32, es)
        bmod2 = S.sb("bmod2", [2, 9 * D], F32, es)
        gp2 = S.sb("gp2", [2, 3, D], F32, es)
        grow = S.sb("grow", [2, 3, D], F32, es)
        wm = [S.sb(f"wm{q}", [128, 8, 512], F32, es) for q in range(2)]
        pm = [S.ps(f"pm{q}", [128, 512], F32, es) for q in range(2)]
        ptr = S.ps("ptrm", [128, 144], F32, es)
        modcol = S.sb("modcol", [128, 72, 2], F32, es)
        tmpc = S.sb("tmpc", [128, 8], F32, es)
        S.dma("sp", [(condT[:, kc, :], d["cond"][:, kc * 128:(kc + 1) * 128].rearrange("r p -> p r"))
                     for kc in range(8)], condT, writes=[condT], allow_slow_non_contiguous=True)
        S.op("act", lambda h: h.activation(out=condT[:], in_=condT[:], func=AF.Silu), reads=[condT], writes=[condT])
        S.dma("sp", [(gpre[:, i, j, :], d["norm_pre"][i, j, :].rearrange("(kc p) -> p kc", p=128))
                     for i in range(2) for j in range(3)], gpre, writes=[gpre], allow_slow_non_contiguous=True)
        for i in range(2):
            S.dma("sp", [(bmod2[r:r + 1, :], d["b_mod"][i:i + 1, :]) for r in range(2)], bmod2, writes=[bmod2])
            S.dma("sp", [(gp2[r:r + 1, :, :], d["norm_post"][i:i + 1, :, :]) for r in range(2)], gp2, writes=[gp2])
            for n in range(18):
                w = wm[n % 2]
                p = pm[n % 2]
                S.dma("sp", [(w[:], d["w_mod"][i, :, n * 512:(n + 1) * 512].rearrange("(kc p) n -> p kc n", p=128))],
                      w, writes=[w])
                for kc in range(8):
                    S.op("pe", lambda h, kc=kc, w=w, p=p: h.matmul(p[0:2, :], lhsT=condT[:, kc, :], rhs=w[:, kc, :],
                                                                    start=(kc == 0), stop=(kc == 7)),
                         reads=[condT, w], writes=[p])
                S.op("dve", lambda h, p=p, n=n: h.tensor_tensor(out=modrow[:, n * 512:(n + 1) * 512], in0=p[0:2, :],
                                                               in1=bmod2[:, n * 512:(n + 1) * 512], op=ALU.add),
                     reads=[p, bmod2], writes=[modrow])
            for j in range(3):
                S.op("dve", lambda h, j=j: h.scalar_tensor_tensor(
                    out=grow[:, j, :], in0=modrow[:, (3 * j + 2) * D:(3 * j + 3) * D], scalar=float(RES_W[j]),
                    in1=gp2[:, j, :], op0=ALU.mult, op1=ALU.mult), reads=[modrow, gp2], writes=[grow])
            S.dma("sp", [(d["grow_d"][i, :, :, :].rearrange("j r n -> r j n"), grow[:])], grow, reads=[grow])
            for c in range(72):
                S.op("pe", lambda h, c=c: h.transpose(out=ptr[:, 2 * c:2 * c + 2], in_=modrow[0:2, c * 128:(c + 1) * 128],
                                                      identity=k.identf[0:2, 0:2]), reads=[modrow, k.identf], writes=[ptr])
            S.op("dve", lambda h: h.tensor_copy(out=modcol[:].rearrange("p c r -> p (c r)"), in_=ptr[:]), reads=[ptr], writes=[modcol])
            for j in range(3):
                for r in range(2):
                    S.op("dve", lambda h, j=j, r=r: h.scalar_tensor_tensor(
                        out=k.acol[:, i, j, r, :], in0=modcol[:, (3 * j + 1) * 8:(3 * j + 2) * 8, r], scalar=1.0,
                        in1=gpre[:, i, j, :], op0=ALU.add, op1=ALU.mult), reads=[modcol, gpre], writes=[k.acol])
                    S.op("dve", lambda h, j=j, r=r: h.tensor_copy(out=k.bcol[:, i, j, r, :], in_=modcol[:, (3 * j) * 8:(3 * j + 1) * 8, r]),
                         reads=[modcol], writes=[k.bcol])
        S.barrier()


def load_weight_cast(k, buf, dst_fn, src2d, nrows, ncols, colblk):
    S = k.S
    pairs = []
    for kc in range(nrows // 128):
        for c0 in range(0, ncols, colblk):
            c1 = min(ncols, c0 + colblk)
            pairs.append((dst_fn(kc, c0, c1), src2d[kc * 128:(kc + 1) * 128, c0:c1]))
    S.dma("pool", pairs, buf, writes=[buf])


def norm_prep(k, es_bufs, xt, tix, hT, i, j, r, T0):
    S = k.S
    junk, ssq, rst, ptr = es_bufs["junk"], es_bufs["ssq"], es_bufs["rst"], es_bufs["ptr"]
    S.op("act", lambda h: h.activation(out=junk[:], in_=xt[:], func=AF.Square, accum_out=ssq[:, 0:1]),
         reads=[xt], writes=[junk, ssq])
    S.op("act", lambda h: h.activation(out=rst[:, 0:1], in_=ssq[:, 0:1], func=AF.Sqrt, bias=es_bufs["epsc"][:, 0:1], scale=1.0 / D),
         reads=[ssq, es_bufs["epsc"]], writes=[rst])
    S.op("dve", lambda h: h.reciprocal(out=rst[:, 1:2], in_=rst[:, 0:1]), reads=[rst], writes=[rst])
    S.op("act", lambda h: h.activation(out=xt[:], in_=xt[:], func=AF.Identity, scale=rst[:, 1:2]), reads=[xt, rst], writes=[xt])
    for half in range(2):
        p = ptr[half]
        for q in range(4):
            kc = half * 4 + q
            S.op("pe", lambda h, kc=kc, q=q, p=p: h.transpose(out=p[:, q * 128:(q + 1) * 128], in_=xt[:, kc * 128:(kc + 1) * 128],
                                                               identity=k.identf[:]), reads=[xt, k.identf], writes=[p])
        for q in range(4):
            kc = half * 4 + q
            if q % 2 == 0:
                S.op("act", lambda h, kc=kc, q=q, p=p: h.activation(
                    out=hT[:, kc, T0:T0 + 128], in_=p[:, q * 128:(q + 1) * 128], func=AF.Identity,
                    scale=k.acol[:, i, j, r, kc:kc + 1], bias=k.bcol[:, i, j, r, kc:kc + 1]),
                    reads=[p, k.acol, k.bcol], writes=[hT])
            else:
                S.op("dve", lambda h, kc=kc, q=q, p=p: h.tensor_scalar(
                    out=hT[:, kc, T0:T0 + 128], in0=p[:, q * 128:(q + 1) * 128],
                    scalar1=k.acol[:, i, j, r, kc:kc + 1], scalar2=k.bcol[:, i, j, r, kc:kc + 1], op0=ALU.mult, op1=ALU.add),
                    reads=[p, k.acol, k.bcol], writes=[hT])


def post_res(k, eb, py, xr, gbc, dst_ap, halves=False):
    S = k.S
    junk, ss2, rs2, tmp = eb["junk"], eb["ss2"], eb["rs2"], eb["tmp"]
    pin = [(py[n][:, n * 512:(n + 1) * 512] if halves else py[n][:]) for n in range(2)]
    for n in range(2):
        S.op("act", lambda h, n=n: h.activation(out=junk[:, n * 512:(n + 1) * 512], in_=pin[n], func=AF.Square,
                                                accum_out=ss2[:, n:n + 1]), reads=[py[n]], writes=[junk, ss2])
    S.op("dve", lambda h: h.tensor_scalar(out=rs2[:, 0:1], in0=ss2[:, 0:1], scalar1=ss2[:, 1:2], scalar2=1.0 / D,
                                          op0=ALU.add, op1=ALU.mult), reads=[ss2], writes=[rs2])
    S.op("act", lambda h: h.activation(out=rs2[:, 1:2], in_=rs2[:, 0:1], func=AF.Sqrt, bias=eb["epsc"][:, 0:1], scale=1.0),
         reads=[rs2, eb["epsc"]], writes=[rs2])
    S.op("dve", lambda h: h.reciprocal(out=rs2[:, 2:3], in_=rs2[:, 1:2]), reads=[rs2], writes=[rs2])
    for n in range(2):
        S.op("dve", lambda h, n=n: h.scalar_tensor_tensor(out=tmp[:, n * 512:(n + 1) * 512], in0=pin[n], scalar=rs2[:, 2:3],
                                                          in1=gbc[:, n * 512:(n + 1) * 512], op0=ALU.mult, op1=ALU.mult),
             reads=[py[n], rs2, gbc], writes=[tmp])
    S.op("pool", lambda h: h.tensor_tensor(out=xr[:], in0=tmp[:], in1=xr[:], op=ALU.add), reads=[tmp, xr], writes=[xr])
    S.dma("sp", [(dst_ap, xr[:])], xr, reads=[xr])


def mk_groups(tiles):
    gs = []
    ctx = [t for t in tiles if t < 2]
    lat = [t for t in tiles if t >= 2]
    if ctx:
        gs.append(ctx)
    for a in range(0, len(lat), 4):
        gs.append(lat[a:a + 4])
    return gs


def load_gbc(k, es, i, j):
    S, d = k.S, k.d
    g = []
    for r in range(2):
        b = S.sb(f"gbc{r}", [128, D], F32, es)
        S.dma("sp", [(b[:], d["grow_d"][i, j, r:r + 1, :].partition_broadcast(128))], b, writes=[b])
        g.append(b)
    return g


def phase_ffn(k, i, w, src, dst, tiles, dst_off=0):
    S, nc, d = k.S, k.nc, k.d
    j = 0 if w == 0 else 2
    with ExitStack() as es:
        win = S.sb("win", [128, 8, 2 * DFF], BF16, es)
        wout = S.sb("wout", [128, NF, D], BF16, es)
        load_weight_cast(k, win, lambda kc, c0, c1: win[:, kc, c0:c1], d["ffn_w_in"][i, w], D, 2 * DFF, 1408)
        load_weight_cast(k, wout, lambda kc, c0, c1: wout[:, kc, c0:c1], d["ffn_w_out"][i, w], DFF, D, 1024)
        gbc = load_gbc(k, es, i, j)
        hT = S.sb("hT", [128, 8, 512], BF16, es)
        act = [S.sb(f"act{f}", [128, 512], BF16, es) for f in range(NF)]
        xts = [S.sb(f"xt{q}", [128, D], F32, es) for q in range(2)]
        xrs = [S.sb(f"xr{q}", [128, D], F32, es) for q in range(2)]
        sg = [S.sb(f"sg{q}", [128, 512], BF16, es) for q in range(2)]
        eb = {"junk": S.sb("junk", [128, D], BF16, es), "ssq": S.sb("ssq", [128, 2], F32, es),
              "rst": S.sb("rst", [128, 2], F32, es), "ss2": S.sb("ss2", [128, 2], F32, es),
              "rs2": S.sb("rs2", [128, 4], F32, es), "tmp": S.sb("tmp", [128, D], F32, es),
              "epsc": S.sb("epsc", [128, 1], F32, es),
              "ptr": [S.ps(f"ptr{q}", [128, 512], F32, es) for q in range(2)]}
        S.op("dve", lambda h: h.memset(eb["epsc"][:], EPS), writes=[eb["epsc"]])
        pg = [S.ps(f"pg{q}", [128, 512], F32, es) for q in range(2)]
        pu = [S.ps(f"pu{q}", [128, 512], F32, es) for q in range(2)]
        py = [S.ps(f"py{q}", [128, 512], F32, es) for q in range(2)]
        groups = mk_groups(tiles)
        cnt = [0, 0]

        def prep(g):
            for ti, t in enumerate(groups[g]):
                xt = xts[cnt[0] % 2]
                cnt[0] += 1
                S.dma("sp", [(xt[:], src[t * 128:(t + 1) * 128, :])], xt, writes=[xt])
                norm_prep(k, eb, xt, ti, hT, i, j, 1 if t < 2 else 0, ti * 128)

        def stage_a(g):
            T = 128 * len(groups[g])
            for f in range(NF):
                for (pp, c0) in ((pg[f % 2], f * 128), (pu[f % 2], DFF + f * 128)):
                    for kc in range(8):
                        S.op("pe", lambda h, pp=pp, c0=c0, kc=kc: h.matmul(pp[:, 0:T], lhsT=win[:, kc, c0:c0 + 128], rhs=hT[:, kc, 0:T],
                                                                           start=(kc == 0), stop=(kc == 7)),
                             reads=[win, hT], writes=[pp])
                s = sg[f % 2]
                S.op("act", lambda h, s=s, f=f: h.activation(out=s[:, 0:T], in_=pg[f % 2][:, 0:T], func=AF.Silu),
                     reads=[pg[f % 2]], writes=[s])
                S.op("dve", lambda h, s=s, f=f: h.tensor_tensor(out=act[f][:, 0:T], in0=pu[f % 2][:, 0:T], in1=s[:, 0:T], op=ALU.mult),
                     reads=[pu[f % 2], s], writes=[act[f]])

        def stage_b(g):
            for ti, t in enumerate(groups[g]):
                xr = xrs[cnt[1] % 2]
                cnt[1] += 1
                S.dma("sp", [(xr[:], src[t * 128:(t + 1) * 128, :])], xr, writes=[xr])
                for n in range(2):
                    for f in range(NF):
                        S.op("pe", lambda h, n=n, f=f, ti=ti: h.matmul(py[n][:], lhsT=act[f][:, ti * 128:(ti + 1) * 128],
                                                                      rhs=wout[:, f, n * 512:(n + 1) * 512],
                                                                      start=(f == 0), stop=(f == NF - 1)),
                             reads=[act[f], wout], writes=[py[n]])
                to = t + dst_off
                post_res(k, eb, py, xr, gbc[1 if t < 2 else 0], dst[to * 128:(to + 1) * 128, :])

        prep(0)
        for g in range(len(groups)):
            stage_a(g)
            if g + 1 < len(groups):
                prep(g + 1)
            stage_b(g)
        S.barrier()


ATT_SCALE = 128 ** -0.5
TWO_PI_SAFE = 6.283184
MAGIC = 12582912.0


def rope_prep(k, es):
    S = k.S
    t = {}
    pid = S.sb("pid", [128, 1], I32, es)
    pf = S.sb("pf", [128, 4], F32, es)
    invf = S.sb("invf", [128, 32], F32, es)
    S.op("pool", lambda h: h.iota(pid[:], pattern=[[0, 1]], base=0, channel_multiplier=1), writes=[pid])
    S.op("dve", lambda h: h.tensor_copy(out=pf[:, 0:1], in_=pid[:]), reads=[pid], writes=[pf])
    S.op("dve", lambda h: h.tensor_single_scalar(out=pf[:, 1:2], in_=pf[:, 0:1], scalar=64.0, op=ALU.is_ge), reads=[pf], writes=[pf])
    S.op("dve", lambda h: h.scalar_tensor_tensor(out=pf[:, 2:3], in0=pf[:, 1:2], scalar=-64.0, in1=pf[:, 0:1], op0=ALU.mult, op1=ALU.add),
         reads=[pf], writes=[pf])
    for q in range(32):
        S.op("dve", lambda h, q=q: h.memset(invf[:, q:q + 1], float(10000.0 ** (-(2.0 * q) / 64.0))), writes=[invf])
    t["pf"], t["invf"] = pf, invf
    t["ang"] = S.sb("ang", [128, 32], F32, es)
    t["ang2"] = S.sb("ang2", [128, 32], F32, es)
    t["CT"] = S.sb("CT", [128, 4, 32], F32, es)
    t["ST"] = S.sb("ST", [128, 4, 32], F32, es)
    t["rowpos"] = S.sb("rowpos", [128, 1], F32, es)
    return t


def rope_sincos(k, t, pos_ap, pos_buf, ax):
    S = k.S
    ang, ang2, CT, ST = t["ang"], t["ang2"], t["CT"], t["ST"]
    S.op("dve", lambda h: h.tensor_scalar(out=ang[:], in0=t["invf"][:], scalar1=pos_ap, scalar2=1.0 / (2 * np.pi), op0=ALU.mult, op1=ALU.mult),
         reads=[t["invf"], pos_buf], writes=[ang])
    S.op("dve", lambda h: h.tensor_scalar(out=ang2[:], in0=ang[:], scalar1=MAGIC, scalar2=-MAGIC, op0=ALU.add, op1=ALU.add), reads=[ang], writes=[ang2])
    S.op("dve", lambda h: h.tensor_tensor(out=ang[:], in0=ang[:], in1=ang2[:], op=ALU.subtract), reads=[ang, ang2], writes=[ang])
    S.op("act", lambda h: h.activation(out=ST[:, 2 * ax + 1, :], in_=ang[:], func=AF.Sin, scale=TWO_PI_SAFE), reads=[ang], writes=[ST])
    S.op("act", lambda h: h.activation(out=ST[:, 2 * ax, :], in_=ang[:], func=AF.Sin, scale=-TWO_PI_SAFE), reads=[ang], writes=[ST])
    S.op("dve", lambda h: h.tensor_scalar(out=ang[:], in0=ang[:], scalar1=0.25, scalar2=None, op0=ALU.add), reads=[ang], writes=[ang])
    S.op("dve", lambda h: h.tensor_scalar(out=ang2[:], in0=ang[:], scalar1=MAGIC, scalar2=-MAGIC, op0=ALU.add, op1=ALU.add), reads=[ang], writes=[ang2])
    S.op("dve", lambda h: h.tensor_tensor(out=ang[:], in0=ang[:], in1=ang2[:], op=ALU.subtract), reads=[ang, ang2], writes=[ang])
    S.op("act", lambda h: h.activation(out=CT[:, 2 * ax, :], in_=ang[:], func=AF.Sin, scale=TWO_PI_SAFE), reads=[ang], writes=[CT])
    S.op("act", lambda h: h.activation(out=CT[:, 2 * ax + 1, :], in_=ang[:], func=AF.Sin, scale=TWO_PI_SAFE), reads=[ang], writes=[CT])


def phase_attn(k, src, dst):
    S, nc, d = k.S, k.nc, k.d
    i, j = 1, 1
    with ExitStack() as es:
        wo = S.sb("wo", [128, 8, D], BF16, es)
        load_weight_cast(k, wo, lambda kc, c0, c1: wo[:, kc, c0:c1], d["attn_w_out"], D, D, 1024)
        KT = S.sb("KT", [128, 2, NTOK], BF16, es)
        V = S.sb("V", [128, NT, 256], BF16, es)
        negm = S.sb("negm", [128, 4], F32, es)
        epsc = S.sb("epsc", [128, 1], F32, es)
        S.op("dve", lambda h: h.memset(epsc[:], EPS), writes=[epsc])
        with ExitStack() as e1:
            wqkv = S.sb("wqkv", [128, 8, 1536], BF16, e1)
            load_weight_cast(k, wqkv, lambda kc, c0, c1: wqkv[:, kc, c0:c1], d["attn_w_qkv"], D, 1536, 1536)
            gq = S.sb("gq", [128, 128], F32, e1)
            gk = S.sb("gk", [128, 128], F32, e1)
            S.dma("sp", [(gq[:], d["attn_q_norm"].rearrange("(o n) -> o n", o=1).partition_broadcast(128))], gq, writes=[gq])
            S.dma("sp", [(gk[:], d["attn_k_norm"].rearrange("(o n) -> o n", o=1).partition_broadcast(128))], gk, writes=[gk])
            S.op("dve", lambda h: h.tensor_reduce(out=negm[:, 0:1], in_=gq[:], axis=AX.X, op=ALU.max, apply_absolute_value=True), reads=[gq], writes=[negm])
            S.op("dve", lambda h: h.tensor_reduce(out=negm[:, 1:2], in_=gk[:], axis=AX.X, op=ALU.max, apply_absolute_value=True), reads=[gk], writes=[negm])
            S.op("dve", lambda h: h.scalar_tensor_tensor(out=negm[:, 2:3], in0=negm[:, 0:1], scalar=-float(128 ** 0.5), in1=negm[:, 1:2],
                                                         op0=ALU.mult, op1=ALU.mult), reads=[negm], writes=[negm])
            rts = [rope_prep(k, e1) for _ in range(2)]
            for rt in rts:
                rope_sincos(k, rt, rt["pf"][:, 2:3], rt["pf"], 1)
            pq = [S.ps(f"pq{q}", [128, 512], F32, e1) for q in range(3)]
            ptq = S.ps("ptq", [128, 1024], BF16, e1)
            ptrs = [S.ps(f"ptr{q}", [128, 512], F32, e1) for q in range(2)]
            sets = []
            for z in range(2):
                ebz = {"junk": S.sb("junk", [128, D], BF16, e1), "ssq": S.sb("ssq", [128, 2], F32, e1),
                       "rst": S.sb("rst", [128, 2], F32, e1), "epsc": epsc, "ptr": ptrs}
                sets.append((S.sb("hT", [128, 8, 128], BF16, e1), S.sb("xt", [128, D], F32, e1), ebz,
                             S.sb("sq", [128, 1280], F32, e1), S.sb("ssh", [128, 10], F32, e1), S.sb("rsh", [128, 10], F32, e1),
                             S.sb("qg", [128, 1280], F32, e1), S.sb("t1", [128, 1280], F32, e1), S.sb("t2", [128, 1280], F32, e1),
                             S.sb("qb", [128, 1280], BF16, e1), S.sb("qTs", [128, 8, 128], BF16, e1)))
            def stA(t):
                lat = t >= 2
                hT, xt, eb, sq, ss, rs, qg, t1, t2, qb, qTs = sets[t % 2]
                rt = rts[t % 2]
                S.dma("sp", [(xt[:], src[t * 128:(t + 1) * 128, :])], xt, writes=[xt])
                norm_prep(k, eb, xt, 0, hT, i, j, 0 if lat else 1, 0)
                for b in ((0, 1, 2) if lat else (2,)):
                    for kc in range(8):
                        S.op("pe", lambda h, b=b, kc=kc: h.matmul(pq[b][:], lhsT=hT[:, kc, :], rhs=wqkv[:, kc, b * 512:(b + 1) * 512],
                                                                   start=(kc == 0), stop=(kc == 7)), reads=[hT, wqkv], writes=[pq[b]])

            def stM(t):
                lat = t >= 2
                hT, xt, eb, sq, ss, rs, qg, t1, t2, qb, qTs = sets[t % 2]
                rt = rts[t % 2]
                S.op("act", lambda h, t=t: h.copy(out=V[:, t, :], in_=pq[2][:, 256:512]), reads=[pq[2]], writes=[V])
                S.op("act", lambda h: h.activation(out=sq[:, 0:256], in_=pq[2][:, 0:256], func=AF.Square), reads=[pq[2]], writes=[sq])
                nh = 10 if lat else 2
                if lat:
                    for b in range(2):
                        S.op("act", lambda h, b=b: h.activation(out=sq[:, 256 + b * 512:256 + (b + 1) * 512], in_=pq[b][:], func=AF.Square),
                             reads=[pq[b]], writes=[sq])
                W = nh * 128
                S.op("dve", lambda h: h.tensor_reduce(out=ss[:, 0:nh], in_=sq[:, 0:W].rearrange("p (h c) -> p h c", c=128), axis=AX.X, op=ALU.add),
                     reads=[sq], writes=[ss])
                S.op("act", lambda h: h.activation(out=rs[:, 0:nh], in_=ss[:, 0:nh], func=AF.Sqrt, bias=epsc[:, 0:1], scale=1.0 / 128), reads=[ss, epsc], writes=[rs])
                S.op("dve", lambda h: h.reciprocal(out=ss[:, 0:nh], in_=rs[:, 0:nh]), reads=[rs], writes=[ss])
                S.op("dve", lambda h: h.tensor_tensor(out=qg[:, 0:256].rearrange("p (h c) -> p h c", c=128), in0=pq[2][:, 0:256].rearrange("p (h c) -> p h c", c=128),
                                                      in1=ss[:, 0:2].unsqueeze(2).to_broadcast([128, 2, 128]), op=ALU.mult), reads=[pq[2], ss], writes=[qg])
                S.op("pool", lambda h: h.tensor_tensor(out=qg[:, 0:256].rearrange("p (h c) -> p h c", c=128), in0=qg[:, 0:256].rearrange("p (h c) -> p h c", c=128),
                                                       in1=gk[:].unsqueeze(1).to_broadcast([128, 2, 128]), op=ALU.mult), reads=[qg, gk], writes=[qg])
                if lat:
                    for b in range(2):
                        S.op("dve", lambda h, b=b: h.tensor_tensor(out=qg[:, 256 + b * 512:256 + (b + 1) * 512].rearrange("p (h c) -> p h c", c=128),
                                                                   in0=pq[b][:].rearrange("p (h c) -> p h c", c=128),
                                                                   in1=ss[:, 2 + 4 * b:6 + 4 * b].unsqueeze(2).to_broadcast([128, 4, 128]), op=ALU.mult),
                             reads=[pq[b], ss], writes=[qg])
                    S.op("pool", lambda h: h.tensor_tensor(out=qg[:, 256:1280].rearrange("p (h c) -> p h c", c=128), in0=qg[:, 256:1280].rearrange("p (h c) -> p h c", c=128),
                                                           in1=gq[:].unsqueeze(1).to_broadcast([128, 8, 128]), op=ALU.mult), reads=[qg, gq], writes=[qg])

            def stB(t):
                lat = t >= 2
                hT, xt, eb, sq, ss, rs, qg, t1, t2, qb, qTs = sets[t % 2]
                rt = rts[t % 2]
                if lat:
                    l = t - 2
                    S.op("dve", lambda h, l=l: h.tensor_scalar(out=rt["rowpos"][:], in0=rt["pf"][:, 1:2], scalar1=float(2 * l), scalar2=None, op0=ALU.add),
                         reads=[rt["pf"]], writes=[rt["rowpos"]])
                    rope_sincos(k, rt, rt["rowpos"][:, 0:1], rt["rowpos"], 0)
                    CTf = rt["CT"][:].rearrange("p b c -> p (b c)")
                    S.op("dve", lambda h: h.tensor_tensor(out=t1[:].rearrange("p (h c) -> p h c", c=128), in0=qg[:].rearrange("p (h c) -> p h c", c=128),
                                                          in1=CTf.unsqueeze(1).to_broadcast([128, 10, 128]), op=ALU.mult), reads=[qg, rt["CT"]], writes=[t1])
                    for ax in range(2):
                        qv = qg[:].rearrange("p (h a s c) -> p h a s c", a=2, s=2, c=32)[:, :, ax, ::-1, :]
                        tv = t2[:].rearrange("p (h a s c) -> p h a s c", a=2, s=2, c=32)[:, :, ax, :, :]
                        sv = rt["ST"][:, 2 * ax:2 * ax + 2, :].unsqueeze(1).to_broadcast([128, 10, 2, 32])
                        S.op("pool", lambda h, qv=qv, tv=tv, sv=sv: h.tensor_tensor(out=tv, in0=qv, in1=sv, op=ALU.mult), reads=[qg, rt["ST"]], writes=[t2])
                    S.op("dve", lambda h: h.tensor_tensor(out=qb[:], in0=t1[:], in1=t2[:], op=ALU.add), reads=[t1, t2], writes=[qb])
                else:
                    S.op("dve", lambda h: h.tensor_copy(out=qb[:, 0:256], in_=qg[:, 0:256]), reads=[qg], writes=[qb])
                for kh in range(2):
                    S.op("pe", lambda h, kh=kh: h.transpose(out=ptq[:, kh * 128:(kh + 1) * 128], in_=qb[:, kh * 128:(kh + 1) * 128], identity=k.identb[:]),
                         reads=[qb, k.identb], writes=[ptq])
                S.op("act", lambda h, t=t: h.copy(out=KT[:, :, t * 128:(t + 1) * 128], in_=ptq[:, 0:256].rearrange("p (h c) -> p h c", c=128)), reads=[ptq], writes=[KT])
                if lat:
                    for hh in range(8):
                        S.op("pe", lambda h, hh=hh: h.transpose(out=ptq[:, hh * 128:(hh + 1) * 128], in_=qb[:, 256 + hh * 128:256 + (hh + 1) * 128], identity=k.identb[:]),
                             reads=[qb, k.identb], writes=[ptq])
                    S.op("dve", lambda h: h.tensor_copy(out=qTs[:].rearrange("p h c -> p (h c)"), in_=ptq[:]), reads=[ptq], writes=[qTs])
                    S.dma("sp", [(d["qT_d"][t - 2], qTs[:])], qTs, reads=[qTs])

            stA(0)
            stM(0)
            for t in range(NT):
                if t + 1 < NT:
                    stA(t + 1)
                stB(t)
                if t + 1 < NT:
                    stM(t + 1)
        S.barrier()
        with ExitStack() as e2:
            gbc = load_gbc(k, e2, i, j)[0]
            NP2 = NT // 2
            PTt = [e2.enter_context(nc.sbuf_tensor(f"PTt{q}", [128, NT, 512], BF16)) for q in range(2)]
            PT = [[S.wrap(f"PT{q}_{p}", PTt[q]) for p in range(NP2)] for q in range(2)]
            qT = [S.sb(f"qT{q}", [128, 8, 128], BF16, e2) for q in range(2)]
            OT = [S.sb(f"OT{q}", [128, 512], BF16, e2) for q in range(2)]
            onesb = S.sb("onesb", [128, 128], BF16, e2)
            S.op("dve", lambda h: h.memset(onesb[:], 1.0), writes=[onesb])
            xrs = [S.sb(f"xr{q}", [128, D], F32, e2) for q in range(2)]
            eb = {"junk": S.sb("junk", [128, D], BF16, e2), "ss2": S.sb("ss2", [128, 2], F32, e2),
                  "rs2": S.sb("rs2", [128, 4], F32, e2), "tmp": S.sb("tmp", [128, D], F32, e2), "epsc": epsc}
            pS = [S.ps(f"pS{q}", [128, 1024], F32, e2) for q in range(2)]
            pOs = [S.ps(f"pO{q}", [128, 512], F32, e2) for q in range(2)]
            prss = [S.ps(f"prs{q}", [128, 512], F32, e2) for q in range(2)]
            lnb = [S.sb(f"lnb{q}", [128, 512], F32, e2) for q in range(2)]
            un = [0]
            for l in range(32):
                q_ = qT[l % 2]
                xr = xrs[l % 2]
                S.dma("sp", [(q_[:], d["qT_d"][l])], q_, writes=[q_])
                S.dma("sp", [(xr[:], src[(l + 2) * 128:(l + 3) * 128, :])], xr, writes=[xr])
                for kvh in range(2):
                    par = un[0] % 2
                    un[0] += 1
                    ptt, ptb = PTt[par], PT[par]
                    pO, prs, rcp = pOs[par], prss[par], lnb[par]

                    def pv(p):
                        for z in range(2):
                            kt = 2 * p + z
                            S.op("pe", lambda h, kt=kt: h.matmul(pO[:], lhsT=V[:, kt, kvh * 128:(kvh + 1) * 128], rhs=ptt[:, kt, :],
                                                                 start=(kt == 0), stop=(kt == NT - 1)), reads=[V, ptb[p]], writes=[pO])
                            S.op("pe", lambda h, kt=kt: h.matmul(prs[:], lhsT=onesb[:], rhs=ptt[:, kt, :],
                                                                 start=(kt == 0), stop=(kt == NT - 1)), reads=[onesb, ptb[p]], writes=[prs])
                    for p in range(NP2):
                        ps = pS[p % 2]
                        for z in range(2):
                            kt = 2 * p + z
                            S.op("pe", lambda h, ps=ps, z=z, kt=kt: h.matmul(
                                ps[:, z * 512:(z + 1) * 512], lhsT=KT[:, kvh, kt * 128:(kt + 1) * 128],
                                rhs=q_[:, kvh * 4:(kvh + 1) * 4, :].rearrange("p h c -> p (h c)"), start=True, stop=True),
                                reads=[KT, q_], writes=[ps])
                        S.op("act", lambda h, ps=ps, p=p: h.activation(
                            out=ptt[:, 2 * p:2 * p + 2, :].rearrange("p a c -> p (a c)"), in_=ps[:], func=AF.Exp, scale=ATT_SCALE, bias=negm[:, 2:3]),
                            reads=[ps, negm], writes=[ptb[p]])
                        if p >= 2:
                            pv(p - 2)
                    pv(NP2 - 2)
                    pv(NP2 - 1)
                    S.op("act", lambda h: h.activation(out=rcp[:], in_=prs[:], func=AF.Ln), reads=[prs], writes=[rcp])
                    S.op("act", lambda h: h.activation(out=rcp[:], in_=rcp[:], func=AF.Exp, scale=-1.0), reads=[rcp], writes=[rcp])
                    S.op("dve", lambda h, kvh=kvh: h.tensor_tensor(out=OT[kvh][:], in0=pO[:], in1=rcp[:], op=ALU.mult), reads=[pO, rcp], writes=[OT[kvh]])
                pyb = pS[0]
                for n in range(2):
                    for hh in range(8):
                        S.op("pe", lambda h, hh=hh, n=n: h.matmul(pyb[:, n * 512:(n + 1) * 512], lhsT=OT[hh // 4][:, (hh % 4) * 128:(hh % 4 + 1) * 128], rhs=wo[:, hh, n * 512:(n + 1) * 512],
                                                                  start=(hh == 0), stop=(hh == 7)), reads=[OT[hh // 4], wo], writes=[pyb])
                post_res(k, eb, [pyb, pyb], xr, gbc, dst[(l + 2) * 128:(l + 3) * 128, :], halves=True)
        S.barrier()


S5_CHUNKS = [(0, 256)] + [(256 + 512 * q, 512) for q in range(8)]


def tab(T, dd, c0, cn):
    if dd == 0:
        return T[:, c0:c0 + cn]
    hi = (255 - c0) if c0 < 256 else (4607 - c0)
    lo = hi - cn
    return T[:, hi:(lo if lo >= 0 else None):-1]


def s5_params(k, es):
    S, d = k.S, k.d
    P = {}

    def col(name, n=16):
        return S.sb(name, [128, n], F32, es)
    lr, li, ls = col("lr"), col("li"), col("ls")
    S.dma("sp", [(lr[:, dd * 8:(dd + 1) * 8], d["s5_lam_re"][dd].rearrange("g p -> (g p)").rearrange("(ct q) -> q ct", q=128)) for dd in range(2)],
          lr, writes=[lr], allow_slow_non_contiguous=True)
    S.dma("sp", [(li[:, dd * 8:(dd + 1) * 8], d["s5_lam_im"][dd].rearrange("g p -> (g p)").rearrange("(ct q) -> q ct", q=128)) for dd in range(2)],
          li, writes=[li], allow_slow_non_contiguous=True)
    S.dma("sp", [(ls[gl * 64:(gl + 1) * 64, dd * 8:(dd + 1) * 8],
                  d["s5_log_step"][dd].rearrange("(ct gl) -> gl ct", gl=2)[gl:gl + 1, :].partition_broadcast(64))
                 for dd in range(2) for gl in range(2)], ls, writes=[ls], allow_slow_non_contiguous=True)
    dt, zr, zi, em1, er = col("dt"), col("zr"), col("zi"), col("em1"), col("er")
    S.op("act", lambda h: h.activation(out=dt[:], in_=ls[:], func=AF.Exp), reads=[ls], writes=[dt])
    S.op("dve", lambda h: h.tensor_tensor(out=zr[:], in0=lr[:], in1=dt[:], op=ALU.mult), reads=[lr, dt], writes=[zr])
    S.op("dve", lambda h: h.tensor_tensor(out=zi[:], in0=li[:], in1=dt[:], op=ALU.mult), reads=[li, dt], writes=[zi])
    S.op("dve", lambda h: h.tensor_scalar(out=em1[:], in0=zr[:], scalar1=1.0 / 6, scalar2=1.0, op0=ALU.mult, op1=ALU.add), reads=[zr], writes=[em1])
    for n in (5, 4, 3, 2):
        S.op("dve", lambda h: h.tensor_tensor(out=em1[:], in0=em1[:], in1=zr[:], op=ALU.mult), reads=[em1, zr], writes=[em1])
        S.op("dve", lambda h, n=n: h.tensor_scalar(out=em1[:], in0=em1[:], scalar1=1.0 / n, scalar2=1.0, op0=ALU.mult, op1=ALU.add), reads=[em1], writes=[em1])
    S.op("dve", lambda h: h.tensor_tensor(out=em1[:], in0=em1[:], in1=zr[:], op=ALU.mult), reads=[em1, zr], writes=[em1])
    S.op("dve", lambda h: h.tensor_scalar(out=er[:], in0=em1[:], scalar1=1.0, scalar2=None, op0=ALU.add), reads=[em1], writes=[er])
    a, a2, sh, sn, cs, cm1 = col("a"), col("a2"), col("sh"), col("sn"), col("cs"), col("cm1")

    def reduce_turns(scale):
        S.op("dve", lambda h: h.tensor_scalar(out=a[:], in0=zi[:], scalar1=scale, scalar2=None, op0=ALU.mult), reads=[zi], writes=[a])
        S.op("dve", lambda h: h.tensor_scalar(out=a2[:], in0=a[:], scalar1=MAGIC, scalar2=-MAGIC, op0=ALU.add, op1=ALU.add), reads=[a], writes=[a2])
        S.op("dve", lambda h: h.tensor_tensor(out=a[:], in0=a[:], in1=a2[:], op=ALU.subtract), reads=[a, a2], writes=[a])
    reduce_turns(1.0 / (4 * np.pi))
    S.op("act", lambda h: h.activation(out=sh[:], in_=a[:], func=AF.Sin, scale=TWO_PI_SAFE), reads=[a], writes=[sh])
    S.op("dve", lambda h: h.scalar_tensor_tensor(out=cm1[:], in0=sh[:], scalar=-2.0, in1=sh[:], op0=ALU.mult, op1=ALU.mult), reads=[sh], writes=[cm1])
    reduce_turns(1.0 / (2 * np.pi))
    S.op("act", lambda h: h.activation(out=sn[:], in_=a[:], func=AF.Sin, scale=TWO_PI_SAFE), reads=[a], writes=[sn])
    S.op("dve", lambda h: h.tensor_scalar(out=cs[:], in0=cm1[:], scalar1=1.0, scalar2=None, op0=ALU.add), reads=[cm1], writes=[cs])
    l1r, l1i, den, qr, qi, t0 = col("l1r"), col("l1i"), col("den"), col("qr"), col("qi"), col("t0")
    S.op("dve", lambda h: h.tensor_tensor(out=l1r[:], in0=em1[:], in1=cs[:], op=ALU.mult), reads=[em1, cs], writes=[l1r])
    S.op("dve", lambda h: h.tensor_tensor(out=l1r[:], in0=l1r[:], in1=cm1[:], op=ALU.add), reads=[l1r, cm1], writes=[l1r])
    S.op("dve", lambda h: h.tensor_tensor(out=l1i[:], in0=er[:], in1=sn[:], op=ALU.mult), reads=[er, sn], writes=[l1i])
    S.op("dve", lambda h: h.tensor_tensor(out=den[:], in0=lr[:], in1=lr[:], op=ALU.mult), reads=[lr], writes=[den])
    S.op("dve", lambda h: h.tensor_tensor(out=t0[:], in0=li[:], in1=li[:], op=ALU.mult), reads=[li], writes=[t0])
    S.op("dve", lambda h: h.tensor_tensor(out=den[:], in0=den[:], in1=t0[:], op=ALU.add), reads=[den, t0], writes=[den])
    S.op("dve", lambda h: h.reciprocal(out=den[:], in_=den[:]), reads=[den], writes=[den])
    S.op("dve", lambda h: h.tensor_tensor(out=qr[:], in0=l1r[:], in1=lr[:], op=ALU.mult), reads=[l1r, lr], writes=[qr])
    S.op("dve", lambda h: h.tensor_tensor(out=t0[:], in0=l1i[:], in1=li[:], op=ALU.mult), reads=[l1i, li], writes=[t0])
    S.op("dve", lambda h: h.tensor_tensor(out=qr[:], in0=qr[:], in1=t0[:], op=ALU.add), reads=[qr, t0], writes=[qr])
    S.op("dve", lambda h: h.tensor_tensor(out=qr[:], in0=qr[:], in1=den[:], op=ALU.mult), reads=[qr, den], writes=[qr])
    S.op("dve", lambda h: h.tensor_tensor(out=qi[:], in0=l1i[:], in1=lr[:], op=ALU.mult), reads=[l1i, lr], writes=[qi])
    S.op("dve", lambda h: h.tensor_tensor(out=t0[:], in0=l1r[:], in1=li[:], op=ALU.mult), reads=[l1r, li], writes=[t0])
    S.op("dve", lambda h: h.tensor_tensor(out=qi[:], in0=qi[:], in1=t0[:], op=ALU.subtract), reads=[qi, t0], writes=[qi])
    S.op("dve", lambda h: h.tensor_tensor(out=qi[:], in0=qi[:], in1=den[:], op=ALU.mult), reads=[qi, den], writes=[qi])
    wr = S.sb("wr", [128, 13, 16], F32, es)
    wi = S.sb("wi", [128, 13, 16], F32, es)
    S.op("dve", lambda h: h.tensor_tensor(out=t0[:], in0=cs[:], in1=cs[:], op=ALU.mult), reads=[cs], writes=[t0])
    S.op("dve", lambda h: h.tensor_tensor(out=a[:], in0=sn[:], in1=sn[:], op=ALU.mult), reads=[sn], writes=[a])
    S.op("dve", lambda h: h.tensor_tensor(out=t0[:], in0=t0[:], in1=a[:], op=ALU.add), reads=[t0, a], writes=[t0])
    S.op("act", lambda h: h.activation(out=t0[:], in_=t0[:], func=AF.Sqrt), reads=[t0], writes=[t0])
    S.op("dve", lambda h: h.reciprocal(out=t0[:], in_=t0[:]), reads=[t0], writes=[t0])
    S.op("dve", lambda h: h.tensor_tensor(out=wr[:, 0, :], in0=cs[:], in1=t0[:], op=ALU.mult), reads=[cs, t0], writes=[wr])
    S.op("dve", lambda h: h.tensor_tensor(out=wi[:, 0, :], in0=sn[:], in1=t0[:], op=ALU.mult), reads=[sn, t0], writes=[wi])
    for lv in range(12):
        S.op("dve", lambda h, lv=lv: h.tensor_tensor(out=a[:], in0=wr[:, lv, :], in1=wr[:, lv, :], op=ALU.mult), reads=[wr], writes=[a])
        S.op("dve", lambda h, lv=lv: h.tensor_tensor(out=a2[:], in0=wi[:, lv, :], in1=wi[:, lv, :], op=ALU.mult), reads=[wi], writes=[a2])
        S.op("dve", lambda h, lv=lv: h.scalar_tensor_tensor(out=wi[:, lv + 1, :], in0=wr[:, lv, :], scalar=2.0, in1=wi[:, lv, :], op0=ALU.mult, op1=ALU.mult),
             reads=[wr, wi], writes=[wi])
        S.op("dve", lambda h, lv=lv: h.tensor_tensor(out=wr[:, lv + 1, :], in0=a[:], in1=a2[:], op=ALU.subtract), reads=[a, a2], writes=[wr])
    P.update(er=er, qr=qr, qi=qi, wr=wr, wi=wi)
    P["bre"] = S.sb("bre", [128, 16, 16], F32, es)
    P["bim"] = S.sb("bim", [128, 16, 16], F32, es)
    for nm, key in (("bre", "s5_b_re"), ("bim", "s5_b_im")):
        S.dma("sp", [(P[nm][:, dd * 8:(dd + 1) * 8, :], d[key][dd].rearrange("g p c -> (g p) c").rearrange("(ct q) c -> q ct c", q=128)) for dd in range(2)],
              P[nm], writes=[P[nm]])
    P["cf"] = {}
    for nm, key in (("cre", "s5_c_re"), ("cim", "s5_c_im")):
        b = S.sb(nm, [128, 4, 128], F32, es)
        S.dma("sp", [(b[:, dd * 2 + ft, h2 * 64:(h2 + 1) * 64], d[key][dd].rearrange("g c p -> (g c) p")[ft * 128:(ft + 1) * 128, :])
                     for dd in range(2) for ft in range(2) for h2 in range(2)], b, writes=[b])
        P["cf"][nm] = b
    P["dcol"] = S.sb("dcol", [128, 2], F32, es)
    P["gbcol"] = S.sb("gbcol", [128, 2], F32, es)
    S.dma("sp", [(P["dcol"][:], d["s5_d"].rearrange("(ft p) -> p ft", p=128))], P["dcol"], writes=[P["dcol"]], allow_slow_non_contiguous=True)
    S.dma("sp", [(P["gbcol"][:], d["s5_glu_b"].rearrange("(ft p) -> p ft", p=128))], P["gbcol"], writes=[P["gbcol"]], allow_slow_non_contiguous=True)
    return P


def phase_mix0(k, src, dst):
    S, nc, d = k.S, k.nc, k.d
    i, j = 0, 1
    with ExitStack() as es:
        epsc = S.sb("epsc", [128, 1], F32, es)
        S.op("dve", lambda h: h.memset(epsc[:], EPS), writes=[epsc])
        ybT = S.sb("ybT", [128, 2, NTOK], BF16, es)
        with ExitStack() as e1:
            win = S.sb("win", [128, 8, 1792], BF16, e1)
            load_weight_cast(k, win, lambda kc, c0, c1: win[:, kc, c0:c1], d["ab_w_in"], D, 1792, 1792)
            wsr = S.sb("wsr", [128, 6, 128], F32, e1)
            wsT = S.sb("wsT", [128, 6, 128], BF16, e1)
            S.dma("sp", [(wsr[:], d["sgu_w"].rearrange("g t s -> t g s"))], wsr, writes=[wsr])
            sbc = S.sb("sbc", [128, 6], F32, e1)
            S.dma("sp", [(sbc[:], d["sgu_b"].rearrange("g t -> t g"))], sbc, writes=[sbc], allow_slow_non_contiguous=True)
            gsb = S.sb("gsb", [128, 768], F32, e1)
            S.dma("sp", [(gsb[:], d["sgu_norm_g"].rearrange("(o n) -> o n", o=1).partition_broadcast(128))], gsb, writes=[gsb])
            ptrs = [S.ps(f"ptr{q}", [128, 512], F32, e1) for q in range(2)]
            eb = {"ptr": ptrs}
            pq = [S.ps(f"pq{q}", [128, 512], F32, e1) for q in range(3)]
            pmA = S.ps("pmA", [128, 512], F32, e1)
            pBt = e1.enter_context(nc.psum_tensor("pBshared", [128, 512], F32))
            pmB = S.wrap("pmB", pBt)
            puT = S.wrap("puT", pBt)
            pya = S.ps("pya", [128, 1024], BF16, e1)
            for g in range(6):
                S.op("pe", lambda h, g=g: h.transpose(out=eb["ptr"][0][:, 0:128], in_=wsr[:, g, :], identity=k.identf[:]), reads=[wsr, k.identf], writes=[eb["ptr"][0]])
                S.op("dve", lambda h, g=g: h.tensor_copy(out=wsT[:, g, :], in_=eb["ptr"][0][:, 0:128]), reads=[eb["ptr"][0]], writes=[wsT])
            sets = []
            for z in range(2):
                ebz = {"junk": S.sb("junk", [128, D], BF16, e1), "ssq": S.sb("ssq", [128, 2], F32, e1),
                       "rst": S.sb("rst", [128, 2], F32, e1), "epsc": epsc, "ptr": ptrs}
                sets.append((S.sb("hT", [128, 8, 128], BF16, e1), S.sb("xt", [128, D], F32, e1), ebz,
                             S.sb("ug", [128, 768], F32, e1), S.sb("vg", [128, 768], F32, e1), S.sb("vh", [128, 768], BF16, e1),
                             S.sb("st6", [128, 6, 6], F32, e1), S.sb("mv", [128, 6, 2], F32, e1), S.sb("rsd", [128, 6], F32, e1),
                             S.sb("nmr", [128, 6], F32, e1), S.sb("tma", [128, 768], F32, e1), S.sb("yab", [128, 768], BF16, e1),
                             S.sb("yaTs", [128, 6, 128], BF16, e1), S.sb("uTs", [128, 2, 128], F32, e1)))
            GEL = AF.Gelu_apprx_tanh
            def mA(t):
                hT, xt, eb, ug, vg, vh, st6, mv, rsd, nmr, tma, yab, yaTs, uTs = sets[t % 2]
                S.dma("sp", [(xt[:], src[t * 128:(t + 1) * 128, :])], xt, writes=[xt])
                norm_prep(k, eb, xt, 0, hT, i, j, 1 if t < 2 else 0, 0)
                for b in range(3):
                    for kc in range(8):
                        S.op("pe", lambda h, b=b, kc=kc: h.matmul(pq[b][:], lhsT=hT[:, kc, :], rhs=win[:, kc, b * 512:(b + 1) * 512],
                                                                   start=(kc == 0), stop=(kc == 7)), reads=[hT, win], writes=[pq[b]])
                for ft in range(2):
                    for kc in range(8):
                        S.op("pe", lambda h, ft=ft, kc=kc: h.matmul(puT[:, 256 + ft * 128:256 + (ft + 1) * 128], lhsT=win[:, kc, 1536 + ft * 128:1536 + (ft + 1) * 128],
                                                                     rhs=hT[:, kc, :], start=(kc == 0), stop=(kc == 7)), reads=[hT, win], writes=[puT])

            def mM(t):
                hT, xt, eb, ug, vg, vh, st6, mv, rsd, nmr, tma, yab, yaTs, uTs = sets[t % 2]
                S.op("dve", lambda h: h.tensor_copy(out=uTs[:].rearrange("p f c -> p (f c)"), in_=puT[:, 256:512]), reads=[puT], writes=[uTs])
                S.dma("sp", [(d["uT_d"][:, :, t * 128:(t + 1) * 128].rearrange("f p c -> p f c"), uTs[:])], uTs, reads=[uTs])
                S.op("act", lambda h: h.activation(out=ug[:, 0:512], in_=pq[0][:], func=GEL), reads=[pq[0]], writes=[ug])
                S.op("act", lambda h: h.activation(out=ug[:, 512:768], in_=pq[1][:, 0:256], func=GEL), reads=[pq[1]], writes=[ug])
                S.op("act", lambda h: h.activation(out=vg[:, 0:256], in_=pq[1][:, 256:512], func=GEL), reads=[pq[1]], writes=[vg])
                S.op("act", lambda h: h.activation(out=vg[:, 256:768], in_=pq[2][:], func=GEL), reads=[pq[2]], writes=[vg])

            def mB(t):
                hT, xt, eb, ug, vg, vh, st6, mv, rsd, nmr, tma, yab, yaTs, uTs = sets[t % 2]
                for g in range(6):
                    S.op("dve", lambda h, g=g: h.bn_stats(out=st6[:, g, :], in_=vg[:, g * 128:(g + 1) * 128]), reads=[vg], writes=[st6])
                for g in range(6):
                    S.op("dve", lambda h, g=g: h.bn_aggr(out=mv[:, g, :], in_=st6[:, g, :]), reads=[st6], writes=[mv])
                S.op("act", lambda h: h.activation(out=rsd[:], in_=mv[:, :, 1], func=AF.Sqrt, bias=epsc[:, 0:1], scale=1.0), reads=[mv, epsc], writes=[rsd])
                S.op("dve", lambda h: h.reciprocal(out=rsd[:], in_=rsd[:]), reads=[rsd], writes=[rsd])
                S.op("dve", lambda h: h.scalar_tensor_tensor(out=nmr[:], in0=mv[:, :, 0], scalar=-1.0, in1=rsd[:], op0=ALU.mult, op1=ALU.mult), reads=[mv, rsd], writes=[nmr])
                for g in range(6):
                    if g % 2 == 0:
                        S.op("act", lambda h, g=g: h.activation(out=vh[:, g * 128:(g + 1) * 128], in_=vg[:, g * 128:(g + 1) * 128], func=AF.Identity,
                                                                 scale=rsd[:, g:g + 1], bias=nmr[:, g:g + 1]), reads=[vg, rsd, nmr], writes=[vh])
                    else:
                        S.op("dve", lambda h, g=g: h.tensor_scalar(out=vh[:, g * 128:(g + 1) * 128], in0=vg[:, g * 128:(g + 1) * 128],
                                                                   scalar1=rsd[:, g:g + 1], scalar2=nmr[:, g:g + 1], op0=ALU.mult, op1=ALU.add), reads=[vg, rsd, nmr], writes=[vh])
                for g in range(6):
                    pm, c0 = (pmA, g * 128) if g < 4 else (pmB, (g - 4) * 128)
                    S.op("pe", lambda h, g=g, pm=pm, c0=c0: h.matmul(pm[:, c0:c0 + 128], lhsT=wsT[:, g, :], rhs=vh[:, g * 128:(g + 1) * 128], start=True, stop=True),
                         reads=[wsT, vh], writes=[pm])
                S.op("dve", lambda h: h.tensor_tensor(out=tma[:, 0:512], in0=pmA[:], in1=gsb[:, 0:512], op=ALU.mult), reads=[pmA, gsb], writes=[tma])
                S.op("dve", lambda h: h.tensor_tensor(out=tma[:, 512:768], in0=pmB[:, 0:256], in1=gsb[:, 512:768], op=ALU.mult), reads=[pmB, gsb], writes=[tma])
                for g in range(6):
                    S.op("dve" if g % 2 else "pool", lambda h, g=g: (h.scalar_tensor_tensor(
                        out=yab[:, g * 128:(g + 1) * 128], in0=tma[:, g * 128:(g + 1) * 128], scalar=sbc[:, g:g + 1], in1=ug[:, g * 128:(g + 1) * 128],
                        op0=ALU.add, op1=ALU.mult)), reads=[tma, sbc, ug], writes=[yab]) if g % 2 else None
                    if g % 2 == 0:
                        S.op("dve", lambda h, g=g: h.scalar_tensor_tensor(
                            out=yab[:, g * 128:(g + 1) * 128], in0=tma[:, g * 128:(g + 1) * 128], scalar=sbc[:, g:g + 1], in1=ug[:, g * 128:(g + 1) * 128],
                            op0=ALU.add, op1=ALU.mult), reads=[tma, sbc, ug], writes=[yab])
                for g in range(6):
                    S.op("pe", lambda h, g=g: h.transpose(out=pya[:, g * 128:(g + 1) * 128], in_=yab[:, g * 128:(g + 1) * 128], identity=k.identb[:]),
                         reads=[yab, k.identb], writes=[pya])
                S.op("act", lambda h: h.copy(out=yaTs[:].rearrange("p g c -> p (g c)"), in_=pya[:, 0:768]), reads=[pya], writes=[yaTs])
                S.dma("sp", [(d["yaT_d"][t], yaTs[:])], yaTs, reads=[yaTs])

            mA(0)
            mM(0)
            for t in range(NT):
                if t + 1 < NT:
                    mA(t + 1)
                mB(t)
                if t + 1 < NT:
                    mM(t + 1)
        S.barrier()
        with ExitStack() as e2:
            Pm = s5_params(k, e2)
            gluw = S.sb("gluw", [128, 2, 256], BF16, e2)
            load_weight_cast(k, gluw, lambda kc, c0, c1: gluw[:, kc, c0:c1], d["s5_glu_w"], 256, 256, 256)
            uT = S.sb("uT", [128, NTOK], F32, e2)
            yc = S.sb("yc", [128, NTOK], F32, e2)
            ygb = S.sb("ygb", [128, 2, NTOK], BF16, e2)
            Tc = S.sb("Tc", [128, NTOK], F32, e2)
            Ts = S.sb("Ts", [128, NTOK], F32, e2)
            bpr = S.sb("bpr", [128, NTOK], F32, e2)
            bpi = S.sb("bpi", [128, NTOK], F32, e2)
            gr, gi = bpr, bpi
            tw1 = S.sb("tw1", [128, 1024], F32, e2)
            tw2 = S.sb("tw2", [128, 1024], F32, e2)
            tw3 = S.sb("tw3", [128, 1024], F32, e2)
            tw4 = S.sb("tw4", [128, 1024], F32, e2)
            Bpad = [S.sb(f"Bpad{q}", [128, 128], F32, e2) for q in range(2)]
            Bl = [S.sb(f"Bl{q}", [128, 128], F32, e2) for q in range(2)]
            Cpad = [S.sb(f"Cpad{q}", [128, 128], F32, e2) for q in range(2)]
            cT = {nm: S.sb("cT" + nm, [128, 4, 128], F32, e2) for nm in ("cre", "cim")}
            tq = S.sb("tq", [128, 16], F32, e2)
            ck = {n: [S.sb(f"{n}{q}", [128, 512], F32, e2) for q in range(2)] for n in ("br", "bi", "hr", "hi")}
            tt = [S.sb(f"tt{q}", [128, 512], F32, e2) for q in range(4)]
            sgm = S.sb("sgm", [128, 512], F32, e2)
            pbr = [S.ps(f"pbr{q}", [128, 512], F32, e2) for q in range(2)]
            pbi = [S.ps(f"pbi{q}", [128, 512], F32, e2) for q in range(2)]
            pyc = [S.ps(f"pyc{q}", [128, 512], F32, e2) for q in range(2)]
            ptb = S.ps("ptb", [128, 128], F32, e2)
            for nm in ("cre", "cim"):
                for q in range(4):
                    S.op("pe", lambda h, nm=nm, q=q: h.transpose(out=ptb[:], in_=Pm["cf"][nm][:, q, :], identity=k.identf[:]), reads=[Pm["cf"][nm], k.identf], writes=[ptb])
                    S.op("dve", lambda h, nm=nm, q=q: h.tensor_copy(out=cT[nm][:, q, :], in_=ptb[:]), reads=[ptb], writes=[cT[nm]])
            cc = [0]
            for ft in range(2):
                S.dma("sp", [(uT[:], d["uT_d"][ft])], uT, writes=[uT])
                first = True
                for ctl in range(4):
                    ct = ft * 4 + ctl
                    for dd in range(2):
                        ix = dd * 8 + ct
                        for part in range(2):
                            S.op("pool", lambda h, part=part: h.memset(Bpad[part][:], 0.0), writes=[Bpad[part]])
                            S.op("pool", lambda h, part=part: h.memset(Cpad[part][:], 0.0), writes=[Cpad[part]])
                        S.op("dve", lambda h, ix=ix: h.tensor_scalar(out=tq[:], in0=Pm["bim"][:, ix, :], scalar1=Pm["qi"][:, ix:ix + 1], scalar2=None, op0=ALU.mult),
                             reads=[Pm["bim"], Pm["qi"]], writes=[tq])
                        for gl in range(2):
                            cb = (2 * ctl + gl) * 16
                            sl = slice(gl * 64, (gl + 1) * 64)
                            S.op("dve", lambda h, ix=ix, sl=sl, cb=cb: h.scalar_tensor_tensor(
                                out=Bpad[0][sl, cb:cb + 16], in0=Pm["bre"][sl, ix, :], scalar=Pm["qr"][sl, ix:ix + 1], in1=tq[sl, :], op0=ALU.mult, op1=ALU.subtract),
                                reads=[Pm["bre"], Pm["qr"], tq], writes=[Bpad[0]])
                        S.op("dve", lambda h, ix=ix: h.tensor_scalar(out=tq[:], in0=Pm["bre"][:, ix, :], scalar1=Pm["qi"][:, ix:ix + 1], scalar2=None, op0=ALU.mult),
                             reads=[Pm["bre"], Pm["qi"], Bpad[0]], writes=[tq])
                        for gl in range(2):
                            cb = (2 * ctl + gl) * 16
                            sl = slice(gl * 64, (gl + 1) * 64)
                            S.op("dve", lambda h, ix=ix, sl=sl, cb=cb: h.scalar_tensor_tensor(
                                out=Bpad[1][sl, cb:cb + 16], in0=Pm["bim"][sl, ix, :], scalar=Pm["qr"][sl, ix:ix + 1], in1=tq[sl, :], op0=ALU.mult, op1=ALU.add),
                                reads=[Pm["bim"], Pm["qr"], tq], writes=[Bpad[1]])
                            S.op("dve", lambda h, sl=sl, cb=cb, dd=dd: h.tensor_copy(out=Cpad[0][sl, cb:cb + 16], in_=cT["cre"][sl, dd * 2 + ft, cb:cb + 16]),
                                 reads=[cT["cre"]], writes=[Cpad[0]])
                            S.op("dve", lambda h, sl=sl, cb=cb, dd=dd: h.tensor_scalar(out=Cpad[1][sl, cb:cb + 16], in0=cT["cim"][sl, dd * 2 + ft, cb:cb + 16],
                                                                                      scalar1=-1.0, scalar2=None, op0=ALU.mult), reads=[cT["cim"]], writes=[Cpad[1]])
                        for part in range(2):
                            S.op("pe", lambda h, part=part: h.transpose(out=ptb[:], in_=Bpad[part][:], identity=k.identf[:]), reads=[Bpad[part], k.identf], writes=[ptb])
                            S.op("act", lambda h, part=part: h.copy(out=Bl[part][:], in_=ptb[:]), reads=[ptb], writes=[Bl[part]])
                        S.op("dve", lambda h: h.memset(Tc[:, 0:1], 1.0), writes=[Tc])
                        S.op("dve", lambda h: h.memset(Ts[:, 0:1], 0.0), writes=[Ts])
                        for lv in range(13):
                            n = 1 << lv
                            mt = min(n, NTOK - n)
                            wr_ = Pm["wr"][:, lv, ix:ix + 1]
                            wi_ = Pm["wi"][:, lv, ix:ix + 1]
                            for o in range(0, mt, 1024):
                                m = min(1024, mt - o)
                                S.op("dve", lambda h, m=m, o=o, wi_=wi_: h.tensor_scalar(out=tw1[:, 0:m], in0=Ts[:, o:o + m], scalar1=wi_, scalar2=None, op0=ALU.mult), reads=[Ts, Pm["wi"]], writes=[tw1])
                                S.op("dve", lambda h, m=m, o=o, wi_=wi_: h.tensor_scalar(out=tw2[:, 0:m], in0=Tc[:, o:o + m], scalar1=wi_, scalar2=None, op0=ALU.mult), reads=[Tc, Pm["wi"]], writes=[tw2])
                                S.op("dve", lambda h, n=n, m=m, o=o, wr_=wr_: h.scalar_tensor_tensor(out=Tc[:, n + o:n + o + m], in0=Tc[:, o:o + m], scalar=wr_, in1=tw1[:, 0:m], op0=ALU.mult, op1=ALU.subtract),
                                     reads=[Tc, Pm["wr"], tw1], writes=[Tc])
                                S.op("dve", lambda h, n=n, m=m, o=o, wr_=wr_: h.scalar_tensor_tensor(out=Ts[:, n + o:n + o + m], in0=Ts[:, o:o + m], scalar=wr_, in1=tw2[:, 0:m], op0=ALU.mult, op1=ALU.add),
                                     reads=[Ts, Pm["wr"], tw2], writes=[Ts])
                        for (c0, cn) in S5_CHUNKS:
                            q2 = cc[0] % 2
                            cc[0] += 1
                            S.op("pe", lambda h, q2=q2, c0=c0, cn=cn: h.matmul(pbr[q2][:, 0:cn], lhsT=Bl[0][:], rhs=uT[:, c0:c0 + cn], start=True, stop=True), reads=[Bl[0], uT], writes=[pbr[q2]])
                            S.op("pe", lambda h, q2=q2, c0=c0, cn=cn: h.matmul(pbi[q2][:, 0:cn], lhsT=Bl[1][:], rhs=uT[:, c0:c0 + cn], start=True, stop=True), reads=[Bl[1], uT], writes=[pbi[q2]])
                            br, bi = ck["br"][q2], ck["bi"][q2]
                            S.op("act", lambda h, q2=q2, cn=cn, br=br: h.copy(out=br[:, 0:cn], in_=pbr[q2][:, 0:cn]), reads=[pbr[q2]], writes=[br])
                            S.op("act", lambda h, q2=q2, cn=cn, bi=bi: h.copy(out=bi[:, 0:cn], in_=pbi[q2][:, 0:cn]), reads=[pbi[q2]], writes=[bi])
                            tc_, ts_ = tab(Tc, dd, c0, cn), tab(Ts, dd, c0, cn)
                            S.op("dve", lambda h, cn=cn, br=br, tc_=tc_: h.tensor_tensor(out=tt[0][:, 0:cn], in0=br[:, 0:cn], in1=tc_, op=ALU.mult), reads=[br, Tc], writes=[tt[0]])
                            S.op("dve", lambda h, cn=cn, bi=bi, ts_=ts_: h.tensor_tensor(out=tt[1][:, 0:cn], in0=bi[:, 0:cn], in1=ts_, op=ALU.mult), reads=[bi, Ts], writes=[tt[1]])
                            S.op("dve", lambda h, c0=c0, cn=cn: h.tensor_tensor(out=bpr[:, c0:c0 + cn], in0=tt[0][:, 0:cn], in1=tt[1][:, 0:cn], op=ALU.add), reads=[tt[0], tt[1]], writes=[bpr])
                            S.op("pool", lambda h, cn=cn, bi=bi, tc_=tc_: h.tensor_tensor(out=tt[2][:, 0:cn], in0=bi[:, 0:cn], in1=tc_, op=ALU.mult), reads=[bi, Tc], writes=[tt[2]])
                            S.op("pool", lambda h, cn=cn, br=br, ts_=ts_: h.tensor_tensor(out=tt[3][:, 0:cn], in0=br[:, 0:cn], in1=ts_, op=ALU.mult), reads=[br, Ts], writes=[tt[3]])
                            S.op("pool", lambda h, c0=c0, cn=cn: h.tensor_tensor(out=bpi[:, c0:c0 + cn], in0=tt[2][:, 0:cn], in1=tt[3][:, 0:cn], op=ALU.subtract), reads=[tt[2], tt[3]], writes=[bpi])
                        erb = Pm["er"][:, ix:ix + 1]
                        for (src_b, dst_b) in ((bpr, bpr), (bpi, bpi)):
                            if dd == 0:
                                S.op("dve", lambda h, src_b=src_b, dst_b=dst_b: h.tensor_tensor_scan(
                                    out=dst_b[:], data0=erb.to_broadcast([128, NTOK]), data1=src_b[:], initial=0.0, op0=ALU.mult, op1=ALU.add),
                                    reads=[src_b, Pm["er"]], writes=[dst_b])
                            else:
                                S.op("dve", lambda h, src_b=src_b, dst_b=dst_b: h.tensor_tensor_scan(
                                    out=dst_b[:, 255::-1], data0=erb.to_broadcast([128, 256]), data1=src_b[:, 255::-1], initial=0.0, op0=ALU.mult, op1=ALU.add),
                                    reads=[src_b, Pm["er"]], writes=[dst_b])
                                S.op("dve", lambda h, src_b=src_b, dst_b=dst_b: h.tensor_tensor_scan(
                                    out=dst_b[:, NTOK - 1:255:-1], data0=erb.to_broadcast([128, NTOK - 256]), data1=src_b[:, NTOK - 1:255:-1],
                                    initial=dst_b[:, 0:1], op0=ALU.mult, op1=ALU.add), reads=[src_b, Pm["er"]], writes=[dst_b])
                        for (c0, cn) in S5_CHUNKS:
                            q2 = cc[0] % 2
                            cc[0] += 1
                            tc_, ts_ = tab(Tc, dd, c0, cn), tab(Ts, dd, c0, cn)
                            hr, hi = ck["hr"][q2], ck["hi"][q2]
                            S.op("dve", lambda h, c0=c0, cn=cn, tc_=tc_: h.tensor_tensor(out=tt[0][:, 0:cn], in0=gr[:, c0:c0 + cn], in1=tc_, op=ALU.mult), reads=[gr, Tc], writes=[tt[0]])
                            S.op("dve", lambda h, c0=c0, cn=cn, ts_=ts_: h.tensor_tensor(out=tt[1][:, 0:cn], in0=gi[:, c0:c0 + cn], in1=ts_, op=ALU.mult), reads=[gi, Ts], writes=[tt[1]])
                            S.op("dve", lambda h, cn=cn, hr=hr: h.tensor_tensor(out=hr[:, 0:cn], in0=tt[0][:, 0:cn], in1=tt[1][:, 0:cn], op=ALU.subtract), reads=[tt[0], tt[1]], writes=[hr])
                            S.op("pool", lambda h, c0=c0, cn=cn, ts_=ts_: h.tensor_tensor(out=tt[2][:, 0:cn], in0=gr[:, c0:c0 + cn], in1=ts_, op=ALU.mult), reads=[gr, Ts], writes=[tt[2]])
                            S.op("pool", lambda h, c0=c0, cn=cn, tc_=tc_: h.tensor_tensor(out=tt[3][:, 0:cn], in0=gi[:, c0:c0 + cn], in1=tc_, op=ALU.mult), reads=[gi, Tc], writes=[tt[3]])
                            S.op("pool", lambda h, cn=cn, hi=hi: h.tensor_tensor(out=hi[:, 0:cn], in0=tt[2][:, 0:cn], in1=tt[3][:, 0:cn], op=ALU.add), reads=[tt[2], tt[3]], writes=[hi])
                            S.op("pe", lambda h, q2=q2, cn=cn, hr=hr: h.matmul(pyc[q2][:, 0:cn], lhsT=Cpad[0][:], rhs=hr[:, 0:cn], start=True, stop=False), reads=[Cpad[0], hr], writes=[pyc[q2]])
                            S.op("pe", lambda h, q2=q2, cn=cn, hi=hi: h.matmul(pyc[q2][:, 0:cn], lhsT=Cpad[1][:], rhs=hi[:, 0:cn], start=False, stop=True), reads=[Cpad[1], hi], writes=[pyc[q2]])
                            if first:
                                S.op("act", lambda h, q2=q2, c0=c0, cn=cn: h.copy(out=yc[:, c0:c0 + cn], in_=pyc[q2][:, 0:cn]), reads=[pyc[q2]], writes=[yc])
                            else:
                                S.op("dve", lambda h, q2=q2, c0=c0, cn=cn: h.tensor_tensor(out=yc[:, c0:c0 + cn], in0=pyc[q2][:, 0:cn], in1=yc[:, c0:c0 + cn], op=ALU.add),
                                     reads=[pyc[q2], yc], writes=[yc])
                        first = False
                S.op("dve", lambda h, ft=ft: h.scalar_tensor_tensor(out=yc[:], in0=uT[:], scalar=Pm["dcol"][:, ft:ft + 1], in1=yc[:], op0=ALU.mult, op1=ALU.add),
                     reads=[uT, Pm["dcol"], yc], writes=[yc])
                S.op("act", lambda h, ft=ft: h.activation(out=ygb[:, ft, :], in_=yc[:], func=AF.Gelu_apprx_tanh), reads=[yc], writes=[ygb])
            for jt in range(2):
                for (c0, cn) in S5_CHUNKS:
                    q2 = cc[0] % 2
                    cc[0] += 1
                    for kc in range(2):
                        S.op("pe", lambda h, q2=q2, kc=kc, jt=jt, c0=c0, cn=cn: h.matmul(pyc[q2][:, 0:cn], lhsT=gluw[:, kc, jt * 128:(jt + 1) * 128], rhs=ygb[:, kc, c0:c0 + cn],
                                                                                          start=(kc == 0), stop=(kc == 1)), reads=[gluw, ygb], writes=[pyc[q2]])
                    S.op("act", lambda h, q2=q2, jt=jt, cn=cn: h.activation(out=sgm[:, 0:cn], in_=pyc[q2][:, 0:cn], func=AF.Sigmoid, bias=Pm["gbcol"][:, jt:jt + 1], scale=1.0),
                         reads=[pyc[q2], Pm["gbcol"]], writes=[sgm])
                    S.op("dve", lambda h, jt=jt, c0=c0, cn=cn: h.tensor_tensor(out=ybT[:, jt, c0:c0 + cn], in0=ygb[:, jt, c0:c0 + cn], in1=sgm[:, 0:cn], op=ALU.mult),
                         reads=[ygb, sgm], writes=[ybT])
        S.barrier()
        with ExitStack() as e3:
            wout = S.sb("wout", [128, 8, D], BF16, e3)
            load_weight_cast(k, wout, lambda kc, c0, c1: wout[:, kc, c0:c1], d["ab_w_out"], D, D, 1024)
            gbc = load_gbc(k, e3, i, j)
            yaT = [S.sb(f"yaT{q}", [128, 6, 128], BF16, e3) for q in range(2)]
            xrs = [S.sb(f"xr{q}", [128, D], F32, e3) for q in range(2)]
            ebs = [{"junk": S.sb("junk", [128, D], BF16, e3), "ss2": S.sb("ss2", [128, 2], F32, e3),
                    "rs2": S.sb("rs2", [128, 4], F32, e3), "tmp": S.sb("tmp", [128, D], F32, e3), "epsc": epsc} for _ in range(2)]
            pys = [[S.ps(f"py{q}{n}", [128, 512], F32, e3) for n in range(2)] for q in range(2)]
            for t in range(NT):
                ya, xr, py = yaT[t % 2], xrs[t % 2], pys[t % 2]
                S.dma("sp", [(ya[:], d["yaT_d"][t])], ya, writes=[ya])
                S.dma("sp", [(xr[:], src[t * 128:(t + 1) * 128, :])], xr, writes=[xr])
                for n in range(2):
                    for kc in range(8):
                        lh = ya[:, kc, :] if kc < 6 else ybT[:, kc - 6, t * 128:(t + 1) * 128]
                        S.op("pe", lambda h, n=n, kc=kc, lh=lh, py=py: h.matmul(py[n][:], lhsT=lh, rhs=wout[:, kc, n * 512:(n + 1) * 512], start=(kc == 0), stop=(kc == 7)),
                             reads=[ya, ybT, wout], writes=[py[n]])
                post_res(k, ebs[t % 2], py, xr, gbc[1 if t < 2 else 0], dst[t * 128:(t + 1) * 128, :])
        S.barrier()


def build(debug=None):
    nc = bass.Bass("TRN2", target_bir_lowering=False)
    k = K()
    k.nc = nc
    k.d = {}
    for nm, shp in PARAM_SPECS:
        k.d[nm] = nc.dram_tensor(nm, list(shp), F32, kind="ExternalInput").ap()
    k.d["out"] = nc.dram_tensor("out", [4096, D], F32, kind="ExternalOutput").ap()
    k.d["grow_d"] = nc.dram_tensor("grow_d", [2, 3, 2, D], F32, kind="Internal").ap()
    k.d["qT_d"] = nc.dram_tensor("qT_d", [32, 128, 8, 128], BF16, kind="Internal").ap()
    k.d["uT_d"] = nc.dram_tensor("uT_d", [2, 128, NTOK], F32, kind="Internal").ap()
    k.d["yaT_d"] = nc.dram_tensor("yaT_d", [NT, 128, 6, 128], BF16, kind="Internal").ap()
    for nm in ("sA", "sB", "sC", "sD", "sE"):
        k.d[nm] = nc.dram_tensor(nm, [NTOK, D], F32, kind="Internal").ap()
    if debug:
        k.d["dbg_in"] = nc.dram_tensor("dbg_in", [NTOK, D], F32, kind="ExternalInput").ap()
        k.d["dbg_out"] = nc.dram_tensor("dbg_out", [NTOK, D], F32, kind="ExternalOutput").ap()
    with ExitStack() as es:
        k.S = Sched(nc, es)
        setup_globals(k)
        allt = list(range(NT))
        lat = list(range(2, NT))
        phase_mod(k)
        dd = k.d
        if debug is None:
            phase_ffn(k, 0, 0, dd["xs"], dd["sA"], allt)
            phase_mix0(k, dd["sA"], dd["sB"])
            phase_ffn(k, 0, 1, dd["sB"], dd["sC"], allt)
            phase_ffn(k, 1, 0, dd["sC"], dd["sD"], allt)
            phase_attn(k, dd["sD"], dd["sE"])
            phase_ffn(k, 1, 1, dd["sE"], dd["out"], lat, dst_off=-2)
        elif debug == "attn":
            phase_attn(k, dd["dbg_in"], dd["dbg_out"])
        elif debug == "mix0":
            phase_mix0(k, dd["dbg_in"], dd["dbg_out"])
        elif debug == "ffn":
            phase_ffn(k, 0, 0, dd["dbg_in"], dd["dbg_out"], allt)
        k.S.barrier()
        print("ninst", k.S.ninst, "nwait", k.S.nwait, "nsem", len(k.S.sems))
    return nc

def make_in_maps(inputs):
    f = lambda a: np.ascontiguousarray(np.asarray(a, dtype=np.float32))
    shared = {
        "w_mod": f(inputs["w_mod"]), "b_mod": f(inputs["b_mod"]), "norm_pre": f(inputs["norm_pre"]),
        "norm_post": f(inputs["norm_post"]), "ffn_w_in": f(inputs["ffn_w_in"]), "ffn_w_out": f(inputs["ffn_w_out"]),
        "ab_w_in": f(inputs["ab_w_in"][0]), "ab_w_out": f(inputs["ab_w_out"][0]), "sgu_norm_g": f(inputs["sgu_norm_g"][0]),
        "sgu_w": f(inputs["sgu_w"][0]), "sgu_b": f(inputs["sgu_b"][0]), "s5_lam_re": f(inputs["s5_lam_re"][0]),
        "s5_lam_im": f(inputs["s5_lam_im"][0]), "s5_log_step": f(inputs["s5_log_step"][0]),
        "s5_b_re": f(inputs["s5_b_re"][0]), "s5_b_im": f(inputs["s5_b_im"][0]), "s5_c_re": f(inputs["s5_c_re"][0]),
        "s5_c_im": f(inputs["s5_c_im"][0]), "s5_d": f(inputs["s5_d"][0]), "s5_glu_w": f(inputs["s5_glu_w"][0]),
        "s5_glu_b": f(inputs["s5_glu_b"][0]), "attn_w_qkv": f(inputs["attn_w_qkv"][0]), "attn_w_out": f(inputs["attn_w_out"][0]),
        "attn_q_norm": f(inputs["attn_q_norm"][0]), "attn_k_norm": f(inputs["attn_k_norm"][0]),
    }
    maps = []
    for b in range(8):
        m = dict(shared)
        m["xs"] = np.ascontiguousarray(np.concatenate([inputs["ctx"][b], inputs["x"][b]], axis=0).astype(np.float32))
        m["cond"] = np.ascontiguousarray(np.stack([inputs["c"][b], inputs["c_ctx"]], axis=0).astype(np.float32))
        maps.append(m)
    return maps


def kernel(**inputs):
    nc = build()
    maps = make_in_maps(inputs)
    res = run_bass_kernel_spmd(nc, maps, core_ids=list(range(8)))
    return np.stack([np.asarray(r["out"]) for r in res.results], axis=0).astype(np.float32)
```

```python
import numpy as np
from contextlib import ExitStack
import concourse.bass as bass
import concourse.mybir as mybir
from concourse.bass_utils import run_bass_kernel_spmd

F32 = mybir.dt.float32
BF16 = mybir.dt.bfloat16
I32 = mybir.dt.int32
AF = mybir.ActivationFunctionType
ALU = mybir.AluOpType
AX = mybir.AxisListType


class Buf:
    __slots__ = ("name", "t", "w", "r", "dsid", "dcnt")

    def __init__(self, name, t):
        self.name = name
        self.t = t
        self.w = None
        self.r = {}
        self.dsid = None
        self.dcnt = 0

    def __getitem__(self, idx):
        return self.t[idx]


class Sched:
    ENG = ("pe", "act", "dve", "pool", "sp")

    def __init__(self, nc, es):
        self.nc = nc
        self.es = es
        self.sems = []
        self.final = []
        self.e = {}
        hs = {"pe": nc.tensor, "act": nc.scalar, "dve": nc.vector, "pool": nc.gpsimd, "sp": nc.sync}
        for nm in self.ENG:
            sid = self._newsem("e_" + nm)
            self.e[nm] = {"h": hs[nm], "sid": sid, "cnt": 0, "seen": {}}
        self.bufs = []
        self.nwait = 0
        self.ninst = 0

    def _newsem(self, name):
        h = self.es.enter_context(self.nc.semaphore(name))
        self.sems.append(h)
        self.final.append(0)
        return len(self.sems) - 1

    def sb(self, name, shape, dt=F32, es=None):
        self.uid = getattr(self, "uid", 0) + 1
        name = f"{name}_{self.uid}"
        t = (es or self.es).enter_context(self.nc.sbuf_tensor(name, list(shape), dt))
        b = Buf(name, t)
        self.bufs.append(b)
        return b

    def ps(self, name, shape, dt=F32, es=None):
        self.uid = getattr(self, "uid", 0) + 1
        name = f"{name}_{self.uid}"
        t = (es or self.es).enter_context(self.nc.psum_tensor(name, list(shape), dt))
        b = Buf(name, t)
        self.bufs.append(b)
        return b

    def wrap(self, name, t):
        b = Buf(name, t)
        self.bufs.append(b)
        return b

    def _collect(self, reads, writes):
        deps = {}
        for b in reads:
            if b.w is not None:
                s, v = b.w
                if deps.get(s, 0) < v:
                    deps[s] = v
        for b in writes:
            if b.w is not None:
                s, v = b.w
                if deps.get(s, 0) < v:
                    deps[s] = v
            for s, v in b.r.items():
                if deps.get(s, 0) < v:
                    deps[s] = v
        return deps

    def _wait(self, eng, deps):
        E = self.e[eng]
        for s, v in deps.items():
            if eng == "pe" and s == E["sid"]:
                continue
            if E["seen"].get(s, 0) >= v:
                continue
            E["h"].wait_ge(self.sems[s], v)
            E["seen"][s] = v
            self.nwait += 1

    def op(self, eng, fn, reads=(), writes=()):
        E = self.e[eng]
        self._wait(eng, self._collect(reads, writes))
        inst = fn(E["h"])
        E["cnt"] += 1
        inst.then_inc(self.sems[E["sid"]], 1)
        self.final[E["sid"]] = E["cnt"]
        ev = (E["sid"], E["cnt"])
        for b in reads:
            if b.r.get(ev[0], 0) < ev[1]:
                b.r[ev[0]] = ev[1]
        for b in writes:
            b.w = ev
            b.r = {}
        self.ninst += 1
        return inst

    def dma(self, eng, pairs, semb, reads=(), writes=(), **kw):
        E = self.e[eng]
        self._wait(eng, self._collect(reads, writes))
        if semb.dsid is None:
            semb.dsid = self._newsem("d_" + semb.name)
        for (o, i) in pairs:
            E["h"].dma_start(out=o, in_=i, **kw).then_inc(self.sems[semb.dsid], 16)
            semb.dcnt += 16
            self.ninst += 1
        self.final[semb.dsid] = semb.dcnt
        ev = (semb.dsid, semb.dcnt)
        for b in reads:
            if b.r.get(ev[0], 0) < ev[1]:
                b.r[ev[0]] = ev[1]
        for b in writes:
            b.w = ev
            b.r = {}

    def barrier(self, engs=None):
        deps = {s: v for s, v in enumerate(self.final) if v > 0}
        for nm in (engs or self.ENG):
            self._wait(nm, deps)
        for b in self.bufs:
            b.w = None
            b.r = {}


D = 1024
NTOK = 4352
NT = NTOK // 128
DFF = 2816
NF = DFF // 128
EPS = 1e-6
RES_W = (0.5, 1.0, 0.5)

PARAM_SPECS = [
    ("xs", [NTOK, D]), ("cond", [2, D]),
    ("w_mod", [2, D, 9 * D]), ("b_mod", [2, 9 * D]), ("norm_pre", [2, 3, D]), ("norm_post", [2, 3, D]),
    ("ffn_w_in", [2, 2, D, 2 * DFF]), ("ffn_w_out", [2, 2, DFF, D]),
    ("ab_w_in", [D, 1792]), ("ab_w_out", [D, D]), ("sgu_norm_g", [768]), ("sgu_w", [6, 128, 128]),
    ("sgu_b", [6, 128]), ("s5_lam_re", [2, 16, 64]), ("s5_lam_im", [2, 16, 64]), ("s5_log_step", [2, 16]),
    ("s5_b_re", [2, 16, 64, 16]), ("s5_b_im", [2, 16, 64, 16]), ("s5_c_re", [2, 16, 16, 64]),
    ("s5_c_im", [2, 16, 16, 64]), ("s5_d", [256]), ("s5_glu_w", [256, 256]), ("s5_glu_b", [256]),
    ("attn_w_qkv", [D, 1536]), ("attn_w_out", [D, D]), ("attn_q_norm", [128]), ("attn_k_norm", [128]),
]


class K:
    pass


def setup_globals(k):
    S, nc = k.S, k.nc
    k.ii = S.sb("ii", [128, 128], I32)
    k.identf = S.sb("identf", [128, 128], F32)
    k.identb = S.sb("identb", [128, 128], BF16)
    S.op("pool", lambda h: h.iota(k.ii[:], pattern=[[1, 128]], base=0, channel_multiplier=-1), writes=[k.ii])
    S.op("dve", lambda h: h.tensor_single_scalar(out=k.identf[:], in_=k.ii[:], scalar=0, op=ALU.is_equal),
         reads=[k.ii], writes=[k.identf])
    S.op("dve", lambda h: h.tensor_copy(out=k.identb[:], in_=k.identf[:]), reads=[k.identf], writes=[k.identb])
    k.acol = S.sb("acol", [128, 2, 3, 2, 8], F32)
    k.bcol = S.sb("bcol", [128, 2, 3, 2, 8], F32)


def phase_mod(k):
    S, nc, d = k.S, k.nc, k.d
    with ExitStack() as es:
        condT = S.sb("condT", [128, 8, 2], F32, es)
        gpre = S.sb("gpre", [128, 2, 3, 8], F32, es)
        modrow = S.sb("modrow", [2, 9 * D], F32, es)
        bmod2 = S.sb("bmod2", [2, 9 * D], F32, es)
        gp2 = S.sb("gp2", [2, 3, D], F32, es)
        grow = S.sb("grow", [2, 3, D], F32, es)
        wm = [S.sb(f"wm{q}", [128, 8, 512], F32, es) for q in range(2)]
        pm = [S.ps(f"pm{q}", [128, 512], F32, es) for q in range(2)]
        ptr = S.ps("ptrm", [128, 144], F32, es)
        modcol = S.sb("modcol", [128, 72, 2], F32, es)
        tmpc = S.sb("tmpc", [128, 8], F32, es)
        S.dma("sp", [(condT[:, kc, :], d["cond"][:, kc * 128:(kc + 1) * 128].rearrange("r p -> p r"))
                     for kc in range(8)], condT, writes=[condT], allow_slow_non_contiguous=True)
        S.op("act", lambda h: h.activation(out=condT[:], in_=condT[:], func=AF.Silu), reads=[condT], writes=[condT])
        S.dma("sp", [(gpre[:, i, j, :], d["norm_pre"][i, j, :].rearrange("(kc p) -> p kc", p=128))
                     for i in range(2) for j in range(3)], gpre, writes=[gpre], allow_slow_non_contiguous=True)
        for i in range(2):
            S.dma("sp", [(bmod2[r:r + 1, :], d["b_mod"][i:i + 1, :]) for r in range(2)], bmod2, writes=[bmod2])
            S.dma("sp", [(gp2[r:r + 1, :, :], d["norm_post"][i:i + 1, :, :]) for r in range(2)], gp2, writes=[gp2])
            for n in range(18):
                w = wm[n % 2]
                p = pm[n % 2]
                S.dma("sp", [(w[:], d["w_mod"][i, :, n * 512:(n + 1) * 512].rearrange("(kc p) n -> p kc n", p=128))],
                      w, writes=[w])
                for kc in range(8):
                    S.op("pe", lambda h, kc=kc, w=w, p=p: h.matmul(p[0:2, :], lhsT=condT[:, kc, :], rhs=w[:, kc, :],
                                                                    start=(kc == 0), stop=(kc == 7)),
                         reads=[condT, w], writes=[p])
                S.op("dve", lambda h, p=p, n=n: h.tensor_tensor(out=modrow[:, n * 512:(n + 1) * 512], in0=p[0:2, :],
                                                               in1=bmod2[:, n * 512:(n + 1) * 512], op=ALU.add),
                     reads=[p, bmod2], writes=[modrow])
            for j in range(3):
                S.op("dve", lambda h, j=j: h.scalar_tensor_tensor(
                    out=grow[:, j, :], in0=modrow[:, (3 * j + 2) * D:(3 * j + 3) * D], scalar=float(RES_W[j]),
                    in1=gp2[:, j, :], op0=ALU.mult, op1=ALU.mult), reads=[modrow, gp2], writes=[grow])
            S.dma("sp", [(d["grow_d"][i, :, :, :].rearrange("j r n -> r j n"), grow[:])], grow, reads=[grow])
            for c in range(72):
                S.op("pe", lambda h, c=c: h.transpose(out=ptr[:, 2 * c:2 * c + 2], in_=modrow[0:2, c * 128:(c + 1) * 128],
                                                      identity=k.identf[0:2, 0:2]), reads=[modrow, k.identf], writes=[ptr])
            S.op("dve", lambda h: h.tensor_copy(out=modcol[:].rearrange("p c r -> p (c r)"), in_=ptr[:]), reads=[ptr], writes=[modcol])
            for j in range(3):
                for r in range(2):
                    S.op("dve", lambda h, j=j, r=r: h.scalar_tensor_tensor(
                        out=k.acol[:, i, j, r, :], in0=modcol[:, (3 * j + 1) * 8:(3 * j + 2) * 8, r], scalar=1.0,
                        in1=gpre[:, i, j, :], op0=ALU.add, op1=ALU.mult), reads=[modcol, gpre], writes=[k.acol])
                    S.op("dve", lambda h, j=j, r=r: h.tensor_copy(out=k.bcol[:, i, j, r, :], in_=modcol[:, (3 * j) * 8:(3 * j + 1) * 8, r]),
                         reads=[modcol], writes=[k.bcol])
        S.barrier()


def load_weight_cast(k, buf, dst_fn, src2d, nrows, ncols, colblk):
    S = k.S
    pairs = []
    for kc in range(nrows // 128):
        for c0 in range(0, ncols, colblk):
            c1 = min(ncols, c0 + colblk)
            pairs.append((dst_fn(kc, c0, c1), src2d[kc * 128:(kc + 1) * 128, c0:c1]))
    S.dma("pool", pairs, buf, writes=[buf])


def norm_prep(k, es_bufs, xt, tix, hT, i, j, r, T0):
    S = k.S
    junk, ssq, rst, ptr = es_bufs["junk"], es_bufs["ssq"], es_bufs["rst"], es_bufs["ptr"]
    S.op("act", lambda h: h.activation(out=junk[:], in_=xt[:], func=AF.Square, accum_out=ssq[:, 0:1]),
         reads=[xt], writes=[junk, ssq])
    S.op("act", lambda h: h.activation(out=rst[:, 0:1], in_=ssq[:, 0:1], func=AF.Sqrt, bias=es_bufs["epsc"][:, 0:1], scale=1.0 / D),
         reads=[ssq, es_bufs["epsc"]], writes=[rst])
    S.op("dve", lambda h: h.reciprocal(out=rst[:, 1:2], in_=rst[:, 0:1]), reads=[rst], writes=[rst])
    S.op("act", lambda h: h.activation(out=xt[:], in_=xt[:], func=AF.Identity, scale=rst[:, 1:2]), reads=[xt, rst], writes=[xt])
    for half in range(2):
        p = ptr[half]
        for q in range(4):
            kc = half * 4 + q
            S.op("pe", lambda h, kc=kc, q=q, p=p: h.transpose(out=p[:, q * 128:(q + 1) * 128], in_=xt[:, kc * 128:(kc + 1) * 128],
                                                               identity=k.identf[:]), reads=[xt, k.identf], writes=[p])
        for q in range(4):
            kc = half * 4 + q
            if q % 2 == 0:
                S.op("act", lambda h, kc=kc, q=q, p=p: h.activation(
                    out=hT[:, kc, T0:T0 + 128], in_=p[:, q * 128:(q + 1) * 128], func=AF.Identity,
                    scale=k.acol[:, i, j, r, kc:kc + 1], bias=k.bcol[:, i, j, r, kc:kc + 1]),
                    reads=[p, k.acol, k.bcol], writes=[hT])
            else:
                S.op("dve", lambda h, kc=kc, q=q, p=p: h.tensor_scalar(
                    out=hT[:, kc, T0:T0 + 128], in0=p[:, q * 128:(q + 1) * 128],
                    scalar1=k.acol[:, i, j, r, kc:kc + 1], scalar2=k.bcol[:, i, j, r, kc:kc + 1], op0=ALU.mult, op1=ALU.add),
                    reads=[p, k.acol, k.bcol], writes=[hT])


def post_res(k, eb, py, xr, gbc, dst_ap, halves=False):
    S = k.S
    junk, ss2, rs2, tmp = eb["junk"], eb["ss2"], eb["rs2"], eb["tmp"]
    pin = [(py[n][:, n * 512:(n + 1) * 512] if halves else py[n][:]) for n in range(2)]
    for n in range(2):
        S.op("act", lambda h, n=n: h.activation(out=junk[:, n * 512:(n + 1) * 512], in_=pin[n], func=AF.Square,
                                                accum_out=ss2[:, n:n + 1]), reads=[py[n]], writes=[junk, ss2])
    S.op("dve", lambda h: h.tensor_scalar(out=rs2[:, 0:1], in0=ss2[:, 0:1], scalar1=ss2[:, 1:2], scalar2=1.0 / D,
                                          op0=ALU.add, op1=ALU.mult), reads=[ss2], writes=[rs2])
    S.op("act", lambda h: h.activation(out=rs2[:, 1:2], in_=rs2[:, 0:1], func=AF.Sqrt, bias=eb["epsc"][:, 0:1], scale=1.0),
         reads=[rs2, eb["epsc"]], writes=[rs2])
    S.op("dve", lambda h: h.reciprocal(out=rs2[:, 2:3], in_=rs2[:, 1:2]), reads=[rs2], writes=[rs2])
    for n in range(2):
        S.op("dve", lambda h, n=n: h.scalar_tensor_tensor(out=tmp[:, n * 512:(n + 1) * 512], in0=pin[n], scalar=rs2[:, 2:3],
                                                          in1=gbc[:, n * 512:(n + 1) * 512], op0=ALU.mult, op1=ALU.mult),
             reads=[py[n], rs2, gbc], writes=[tmp])
    S.op("pool", lambda h: h.tensor_tensor(out=xr[:], in0=tmp[:], in1=xr[:], op=ALU.add), reads=[tmp, xr], writes=[xr])
    S.dma("sp", [(dst_ap, xr[:])], xr, reads=[xr])


def mk_groups(tiles):
    gs = []
    ctx = [t for t in tiles if t < 2]
    lat = [t for t in tiles if t >= 2]
    if ctx:
        gs.append(ctx)
    for a in range(0, len(lat), 4):
        gs.append(lat[a:a + 4])
    return gs


def load_gbc(k, es, i, j):
    S, d = k.S, k.d
    g = []
    for r in range(2):
        b = S.sb(f"gbc{r}", [128, D], F32, es)
        S.dma("sp", [(b[:], d["grow_d"][i, j, r:r + 1, :].partition_broadcast(128))], b, writes=[b])
        g.append(b)
    return g


def phase_ffn(k, i, w, src, dst, tiles, dst_off=0):
    S, nc, d = k.S, k.nc, k.d
    j = 0 if w == 0 else 2
    with ExitStack() as es:
        win = S.sb("win", [128, 8, 2 * DFF], BF16, es)
        wout = S.sb("wout", [128, NF, D], BF16, es)
        load_weight_cast(k, win, lambda kc, c0, c1: win[:, kc, c0:c1], d["ffn_w_in"][i, w], D, 2 * DFF, 1408)
        load_weight_cast(k, wout, lambda kc, c0, c1: wout[:, kc, c0:c1], d["ffn_w_out"][i, w], DFF, D, 1024)
        gbc = load_gbc(k, es, i, j)
        hT = S.sb("hT", [128, 8, 512], BF16, es)
        act = [S.sb(f"act{f}", [128, 512], BF16, es) for f in range(NF)]
        xts = [S.sb(f"xt{q}", [128, D], F32, es) for q in range(2)]
        xrs = [S.sb(f"xr{q}", [128, D], F32, es) for q in range(2)]
        sg = [S.sb(f"sg{q}", [128, 512], BF16, es) for q in range(2)]
        eb = {"junk": S.sb("junk", [128, D], BF16, es), "ssq": S.sb("ssq", [128, 2], F32, es),
              "rst": S.sb("rst", [128, 2], F32, es), "ss2": S.sb("ss2", [128, 2], F32, es),
              "rs2": S.sb("rs2", [128, 4], F32, es), "tmp": S.sb("tmp", [128, D], F32, es),
              "epsc": S.sb("epsc", [128, 1], F32, es),
              "ptr": [S.ps(f"ptr{q}", [128, 512], F32, es) for q in range(2)]}
        S.op("dve", lambda h: h.memset(eb["epsc"][:], EPS), writes=[eb["epsc"]])
        pg = [S.ps(f"pg{q}", [128, 512], F32, es) for q in range(2)]
        pu = [S.ps(f"pu{q}", [128, 512], F32, es) for q in range(2)]
        py = [S.ps(f"py{q}", [128, 512], F32, es) for q in range(2)]
        groups = mk_groups(tiles)
        cnt = [0, 0]

        def prep(g):
            for ti, t in enumerate(groups[g]):
                xt = xts[cnt[0] % 2]
                cnt[0] += 1
                S.dma("sp", [(xt[:], src[t * 128:(t + 1) * 128, :])], xt, writes=[xt])
                norm_prep(k, eb, xt, ti, hT, i, j, 1 if t < 2 else 0, ti * 128)

        def stage_a(g):
            T = 128 * len(groups[g])
            for f in range(NF):
                for (pp, c0) in ((pg[f % 2], f * 128), (pu[f % 2], DFF + f * 128)):
                    for kc in range(8):
                        S.op("pe", lambda h, pp=pp, c0=c0, kc=kc: h.matmul(pp[:, 0:T], lhsT=win[:, kc, c0:c0 + 128], rhs=hT[:, kc, 0:T],
                                                                           start=(kc == 0), stop=(kc == 7)),
                             reads=[win, hT], writes=[pp])
                s = sg[f % 2]
                S.op("act", lambda h, s=s, f=f: h.activation(out=s[:, 0:T], in_=pg[f % 2][:, 0:T], func=AF.Silu),
                     reads=[pg[f % 2]], writes=[s])
                S.op("dve", lambda h, s=s, f=f: h.tensor_tensor(out=act[f][:, 0:T], in0=pu[f % 2][:, 0:T], in1=s[:, 0:T], op=ALU.mult),
                     reads=[pu[f % 2], s], writes=[act[f]])

        def stage_b(g):
            tl = groups[g]
            S.dma("sp", [(xrs[cnt[1] % 2][:], src[tl[0] * 128:(tl[0] + 1) * 128, :])], xrs[cnt[1] % 2], writes=[xrs[cnt[1] % 2]])
            for ti, t in enumerate(tl):
                xr = xrs[cnt[1] % 2]
                cnt[1] += 1
                if ti + 1 < len(tl):
                    tn = tl[ti + 1]
                    S.dma("sp", [(xrs[cnt[1] % 2][:], src[tn * 128:(tn + 1) * 128, :])], xrs[cnt[1] % 2], writes=[xrs[cnt[1] % 2]])
                for n in range(2):
                    for f in range(NF):
                        S.op("pe", lambda h, n=n, f=f, ti=ti: h.matmul(py[n][:], lhsT=act[f][:, ti * 128:(ti + 1) * 128],
                                                                      rhs=wout[:, f, n * 512:(n + 1) * 512],
                                                                      start=(f == 0), stop=(f == NF - 1)),
                             reads=[act[f], wout], writes=[py[n]])
                to = t + dst_off
                post_res(k, eb, py, xr, gbc[1 if t < 2 else 0], dst[to * 128:(to + 1) * 128, :])

        prep(0)
        for g in range(len(groups)):
            stage_a(g)
            if g + 1 < len(groups):
                prep(g + 1)
            stage_b(g)
        S.barrier()


ATT_SCALE = 128 ** -0.5
TWO_PI_SAFE = 6.283184
MAGIC = 12582912.0


def rope_prep(k, es):
    S = k.S
    t = {}
    pid = S.sb("pid", [128, 1], I32, es)
    pf = S.sb("pf", [128, 4], F32, es)
    invf = S.sb("invf", [128, 32], F32, es)
    S.op("pool", lambda h: h.iota(pid[:], pattern=[[0, 1]], base=0, channel_multiplier=1), writes=[pid])
    S.op("dve", lambda h: h.tensor_copy(out=pf[:, 0:1], in_=pid[:]), reads=[pid], writes=[pf])
    S.op("dve", lambda h: h.tensor_single_scalar(out=pf[:, 1:2], in_=pf[:, 0:1], scalar=64.0, op=ALU.is_ge), reads=[pf], writes=[pf])
    S.op("dve", lambda h: h.scalar_tensor_tensor(out=pf[:, 2:3], in0=pf[:, 1:2], scalar=-64.0, in1=pf[:, 0:1], op0=ALU.mult, op1=ALU.add),
         reads=[pf], writes=[pf])
    for q in range(32):
        S.op("dve", lambda h, q=q: h.memset(invf[:, q:q + 1], float(10000.0 ** (-(2.0 * q) / 64.0))), writes=[invf])
    t["pf"], t["invf"] = pf, invf
    t["ang"] = S.sb("ang", [128, 32], F32, es)
    t["ang2"] = S.sb("ang2", [128, 32], F32, es)
    t["CT"] = S.sb("CT", [128, 4, 32], F32, es)
    t["ST"] = S.sb("ST", [128, 4, 32], F32, es)
    t["rowpos"] = S.sb("rowpos", [128, 1], F32, es)
    return t


def rope_sincos(k, t, pos_ap, pos_buf, ax):
    S = k.S
    ang, ang2, CT, ST = t["ang"], t["ang2"], t["CT"], t["ST"]
    S.op("dve", lambda h: h.tensor_scalar(out=ang[:], in0=t["invf"][:], scalar1=pos_ap, scalar2=1.0 / (2 * np.pi), op0=ALU.mult, op1=ALU.mult),
         reads=[t["invf"], pos_buf], writes=[ang])
    S.op("dve", lambda h: h.tensor_scalar(out=ang2[:], in0=ang[:], scalar1=MAGIC, scalar2=-MAGIC, op0=ALU.add, op1=ALU.add), reads=[ang], writes=[ang2])
    S.op("dve", lambda h: h.tensor_tensor(out=ang[:], in0=ang[:], in1=ang2[:], op=ALU.subtract), reads=[ang, ang2], writes=[ang])
    S.op("act", lambda h: h.activation(out=ST[:, 2 * ax + 1, :], in_=ang[:], func=AF.Sin, scale=TWO_PI_SAFE), reads=[ang], writes=[ST])
    S.op("act", lambda h: h.activation(out=ST[:, 2 * ax, :], in_=ang[:], func=AF.Sin, scale=-TWO_PI_SAFE), reads=[ang], writes=[ST])
    S.op("dve", lambda h: h.tensor_scalar(out=ang[:], in0=ang[:], scalar1=0.25, scalar2=None, op0=ALU.add), reads=[ang], writes=[ang])
    S.op("dve", lambda h: h.tensor_scalar(out=ang2[:], in0=ang[:], scalar1=MAGIC, scalar2=-MAGIC, op0=ALU.add, op1=ALU.add), reads=[ang], writes=[ang2])
    S.op("dve", lambda h: h.tensor_tensor(out=ang[:], in0=ang[:], in1=ang2[:], op=ALU.subtract), reads=[ang, ang2], writes=[ang])
    S.op("act", lambda h: h.activation(out=CT[:, 2 * ax, :], in_=ang[:], func=AF.Sin, scale=TWO_PI_SAFE), reads=[ang], writes=[CT])
    S.op("act", lambda h: h.activation(out=CT[:, 2 * ax + 1, :], in_=ang[:], func=AF.Sin, scale=TWO_PI_SAFE), reads=[ang], writes=[CT])


def phase_attn(k, src, dst):
    S, nc, d = k.S, k.nc, k.d
    i, j = 1, 1
    with ExitStack() as es:
        wo = S.sb("wo", [128, 8, D], BF16, es)
        load_weight_cast(k, wo, lambda kc, c0, c1: wo[:, kc, c0:c1], d["attn_w_out"], D, D, 1024)
        KT = S.sb("KT", [128, 2, NTOK], BF16, es)
        V = S.sb("V", [128, NT, 256], BF16, es)
        negm = S.sb("negm", [128, 4], F32, es)
        epsc = S.sb("epsc", [128, 1], F32, es)
        S.op("dve", lambda h: h.memset(epsc[:], EPS), writes=[epsc])
        with ExitStack() as e1:
            wqkv = S.sb("wqkv", [128, 8, 1536], BF16, e1)
            load_weight_cast(k, wqkv, lambda kc, c0, c1: wqkv[:, kc, c0:c1], d["attn_w_qkv"], D, 1536, 1536)
            gq = S.sb("gq", [128, 128], F32, e1)
            gk = S.sb("gk", [128, 128], F32, e1)
            S.dma("sp", [(gq[:], d["attn_q_norm"].rearrange("(o n) -> o n", o=1).partition_broadcast(128))], gq, writes=[gq])
            S.dma("sp", [(gk[:], d["attn_k_norm"].rearrange("(o n) -> o n", o=1).partition_broadcast(128))], gk, writes=[gk])
            S.op("dve", lambda h: h.tensor_reduce(out=negm[:, 0:1], in_=gq[:], axis=AX.X, op=ALU.max, apply_absolute_value=True), reads=[gq], writes=[negm])
            S.op("dve", lambda h: h.tensor_reduce(out=negm[:, 1:2], in_=gk[:], axis=AX.X, op=ALU.max, apply_absolute_value=True), reads=[gk], writes=[negm])
            S.op("dve", lambda h: h.scalar_tensor_tensor(out=negm[:, 2:3], in0=negm[:, 0:1], scalar=-float(128 ** 0.5), in1=negm[:, 1:2],
                                                         op0=ALU.mult, op1=ALU.mult), reads=[negm], writes=[negm])
            rts = [rope_prep(k, e1) for _ in range(2)]
            for rt in rts:
                rope_sincos(k, rt, rt["pf"][:, 2:3], rt["pf"], 1)
            pq = [S.ps(f"pq{q}", [128, 512], F32, e1) for q in range(3)]
            ptq = S.ps("ptq", [128, 1024], BF16, e1)
            ptrs = [S.ps(f"ptr{q}", [128, 512], F32, e1) for q in range(2)]
            sets = []
            for z in range(2):
                ebz = {"junk": S.sb("junk", [128, D], BF16, e1), "ssq": S.sb("ssq", [128, 2], F32, e1),
                       "rst": S.sb("rst", [128, 2], F32, e1), "epsc": epsc, "ptr": ptrs}
                sets.append((S.sb("hT", [128, 8, 128], BF16, e1), S.sb("xt", [128, D], F32, e1), ebz,
                             S.sb("sq", [128, 1280], F32, e1), S.sb("ssh", [128, 10], F32, e1), S.sb("rsh", [128, 10], F32, e1),
                             S.sb("qg", [128, 1280], F32, e1), S.sb("t1", [128, 1280], F32, e1), S.sb("t2", [128, 1280], F32, e1),
                             S.sb("qb", [128, 1280], BF16, e1), S.sb("qTs", [128, 8, 128], BF16, e1)))
            xt3 = [S.sb(f"xt3{q}", [128, D], F32, e1) for q in range(3)]

            def ldx(t):
                S.dma("sp", [(xt3[t % 3][:], src[t * 128:(t + 1) * 128, :])], xt3[t % 3], writes=[xt3[t % 3]])

            def stA(t):
                lat = t >= 2
                hT, xt, eb, sq, ss, rs, qg, t1, t2, qb, qTs = sets[t % 2]
                rt = rts[t % 2]
                xt = xt3[t % 3]
                norm_prep(k, eb, xt, 0, hT, i, j, 0 if lat else 1, 0)
                for b in ((0, 1, 2) if lat else (2,)):
                    for kc in range(8):
                        S.op("pe", lambda h, b=b, kc=kc: h.matmul(pq[b][:], lhsT=hT[:, kc, :], rhs=wqkv[:, kc, b * 512:(b + 1) * 512],
                                                                   start=(kc == 0), stop=(kc == 7)), reads=[hT, wqkv], writes=[pq[b]])

            def stM(t):
                lat = t >= 2
                hT, xt, eb, sq, ss, rs, qg, t1, t2, qb, qTs = sets[t % 2]
                rt = rts[t % 2]
                S.op("act", lambda h, t=t: h.copy(out=V[:, t, :], in_=pq[2][:, 256:512]), reads=[pq[2]], writes=[V])
                S.op("act", lambda h: h.activation(out=sq[:, 0:256], in_=pq[2][:, 0:256], func=AF.Square), reads=[pq[2]], writes=[sq])
                nh = 10 if lat else 2
                if lat:
                    for b in range(2):
                        S.op("act", lambda h, b=b: h.activation(out=sq[:, 256 + b * 512:256 + (b + 1) * 512], in_=pq[b][:], func=AF.Square),
                             reads=[pq[b]], writes=[sq])
                W = nh * 128
                S.op("dve", lambda h: h.tensor_reduce(out=ss[:, 0:nh], in_=sq[:, 0:W].rearrange("p (h c) -> p h c", c=128), axis=AX.X, op=ALU.add),
                     reads=[sq], writes=[ss])
                S.op("act", lambda h: h.activation(out=rs[:, 0:nh], in_=ss[:, 0:nh], func=AF.Sqrt, bias=epsc[:, 0:1], scale=1.0 / 128), reads=[ss, epsc], writes=[rs])
                S.op("dve", lambda h: h.reciprocal(out=ss[:, 0:nh], in_=rs[:, 0:nh]), reads=[rs], writes=[ss])
                S.op("dve", lambda h: h.tensor_tensor(out=qg[:, 0:256].rearrange("p (h c) -> p h c", c=128), in0=pq[2][:, 0:256].rearrange("p (h c) -> p h c", c=128),
                                                      in1=ss[:, 0:2].unsqueeze(2).to_broadcast([128, 2, 128]), op=ALU.mult), reads=[pq[2], ss], writes=[qg])
                S.op("pool", lambda h: h.tensor_tensor(out=qg[:, 0:256].rearrange("p (h c) -> p h c", c=128), in0=qg[:, 0:256].rearrange("p (h c) -> p h c", c=128),
                                                       in1=gk[:].unsqueeze(1).to_broadcast([128, 2, 128]), op=ALU.mult), reads=[qg, gk], writes=[qg])
                if lat:
                    for b in range(2):
                        S.op("dve", lambda h, b=b: h.tensor_tensor(out=qg[:, 256 + b * 512:256 + (b + 1) * 512].rearrange("p (h c) -> p h c", c=128),
                                                                   in0=pq[b][:].rearrange("p (h c) -> p h c", c=128),
                                                                   in1=ss[:, 2 + 4 * b:6 + 4 * b].unsqueeze(2).to_broadcast([128, 4, 128]), op=ALU.mult),
                             reads=[pq[b], ss], writes=[qg])
                    S.op("pool", lambda h: h.tensor_tensor(out=qg[:, 256:1280].rearrange("p (h c) -> p h c", c=128), in0=qg[:, 256:1280].rearrange("p (h c) -> p h c", c=128),
                                                           in1=gq[:].unsqueeze(1).to_broadcast([128, 8, 128]), op=ALU.mult), reads=[qg, gq], writes=[qg])

            def stB(t):
                lat = t >= 2
                hT, xt, eb, sq, ss, rs, qg, t1, t2, qb, qTs = sets[t % 2]
                rt = rts[t % 2]
                if lat:
                    l = t - 2
                    S.op("dve", lambda h, l=l: h.tensor_scalar(out=rt["rowpos"][:], in0=rt["pf"][:, 1:2], scalar1=float(2 * l), scalar2=None, op0=ALU.add),
                         reads=[rt["pf"]], writes=[rt["rowpos"]])
                    rope_sincos(k, rt, rt["rowpos"][:, 0:1], rt["rowpos"], 0)
                    CTf = rt["CT"][:].rearrange("p b c -> p (b c)")
                    S.op("dve", lambda h: h.tensor_tensor(out=t1[:].rearrange("p (h c) -> p h c", c=128), in0=qg[:].rearrange("p (h c) -> p h c", c=128),
                                                          in1=CTf.unsqueeze(1).to_broadcast([128, 10, 128]), op=ALU.mult), reads=[qg, rt["CT"]], writes=[t1])
                    for ax in range(2):
                        qv = qg[:].rearrange("p (h a s c) -> p h a s c", a=2, s=2, c=32)[:, :, ax, ::-1, :]
                        tv = t2[:].rearrange("p (h a s c) -> p h a s c", a=2, s=2, c=32)[:, :, ax, :, :]
                        sv = rt["ST"][:, 2 * ax:2 * ax + 2, :].unsqueeze(1).to_broadcast([128, 10, 2, 32])
                        S.op("pool", lambda h, qv=qv, tv=tv, sv=sv: h.tensor_tensor(out=tv, in0=qv, in1=sv, op=ALU.mult), reads=[qg, rt["ST"]], writes=[t2])
                    S.op("dve", lambda h: h.tensor_tensor(out=qb[:], in0=t1[:], in1=t2[:], op=ALU.add), reads=[t1, t2], writes=[qb])
                else:
                    S.op("dve", lambda h: h.tensor_copy(out=qb[:, 0:256], in_=qg[:, 0:256]), reads=[qg], writes=[qb])
                for kh in range(2):
                    S.op("pe", lambda h, kh=kh: h.transpose(out=ptq[:, kh * 128:(kh + 1) * 128], in_=qb[:, kh * 128:(kh + 1) * 128], identity=k.identb[:]),
                         reads=[qb, k.identb], writes=[ptq])
                S.op("act", lambda h, t=t: h.copy(out=KT[:, :, t * 128:(t + 1) * 128], in_=ptq[:, 0:256].rearrange("p (h c) -> p h c", c=128)), reads=[ptq], writes=[KT])
                if lat:
                    for hh in range(8):
                        S.op("pe", lambda h, hh=hh: h.transpose(out=ptq[:, hh * 128:(hh + 1) * 128], in_=qb[:, 256 + hh * 128:256 + (hh + 1) * 128], identity=k.identb[:]),
                             reads=[qb, k.identb], writes=[ptq])
                    S.op("dve", lambda h: h.tensor_copy(out=qTs[:].rearrange("p h c -> p (h c)"), in_=ptq[:]), reads=[ptq], writes=[qTs])
                    S.dma("sp", [(d["qT_d"][t - 2], qTs[:])], qTs, reads=[qTs])

            ldx(0)
            ldx(1)
            stA(0)
            stM(0)
            for t in range(NT):
                if t + 2 < NT:
                    ldx(t + 2)
                if t + 1 < NT:
                    stA(t + 1)
                stB(t)
                if t + 1 < NT:
                    stM(t + 1)
        S.barrier()
        with ExitStack() as e2:
            gbc = load_gbc(k, e2, i, j)[0]
            NP2 = NT // 2
            PTt = [e2.enter_context(nc.sbuf_tensor(f"PTt{q}", [128, NT, 512], BF16)) for q in range(2)]
            PT = [[S.wrap(f"PT{q}_{p}", PTt[q]) for p in range(NP2)] for q in range(2)]
            qT = [S.sb(f"qT{q}", [128, 8, 128], BF16, e2) for q in range(2)]
            OT = [S.sb(f"OT{q}", [128, 512], BF16, e2) for q in range(2)]
            onesb = S.sb("onesb", [128, 128], BF16, e2)
            S.op("dve", lambda h: h.memset(onesb[:], 1.0), writes=[onesb])
            xrs = [S.sb(f"xr{q}", [128, D], F32, e2) for q in range(2)]
            eb = {"junk": S.sb("junk", [128, D], BF16, e2), "ss2": S.sb("ss2", [128, 2], F32, e2),
                  "rs2": S.sb("rs2", [128, 4], F32, e2), "tmp": S.sb("tmp", [128, D], F32, e2), "epsc": epsc}
            pS = [S.ps(f"pS{q}", [128, 1024], F32, e2) for q in range(2)]
            pOs = [S.ps(f"pO{q}", [128, 512], F32, e2) for q in range(2)]
            prss = [S.ps(f"prs{q}", [128, 512], F32, e2) for q in range(2)]
            lnb = [S.sb(f"lnb{q}", [128, 512], F32, e2) for q in range(2)]
            un = [0]
            def ld2(l):
                S.dma("sp", [(qT[l % 2][:], d["qT_d"][l])], qT[l % 2], writes=[qT[l % 2]])
                S.dma("sp", [(xrs[l % 2][:], src[(l + 2) * 128:(l + 3) * 128, :])], xrs[l % 2], writes=[xrs[l % 2]])
            ld2(0)
            for l in range(32):
                q_ = qT[l % 2]
                xr = xrs[l % 2]
                if l + 1 < 32:
                    ld2(l + 1)
                for kvh in range(2):
                    par = un[0] % 2
                    un[0] += 1
                    ptt, ptb = PTt[par], PT[par]
                    pO, prs, rcp = pOs[par], prss[par], lnb[par]

                    def pv(p):
                        for z in range(2):
                            kt = 2 * p + z
                            S.op("pe", lambda h, kt=kt: h.matmul(pO[:], lhsT=V[:, kt, kvh * 128:(kvh + 1) * 128], rhs=ptt[:, kt, :],
                                                                 start=(kt == 0), stop=(kt == NT - 1)), reads=[V, ptb[p]], writes=[pO])
                            S.op("pe", lambda h, kt=kt: h.matmul(prs[:], lhsT=onesb[:], rhs=ptt[:, kt, :],
                                                                 start=(kt == 0), stop=(kt == NT - 1)), reads=[onesb, ptb[p]], writes=[prs])
                    for p in range(NP2):
                        ps = pS[p % 2]
                        for z in range(2):
                            kt = 2 * p + z
                            S.op("pe", lambda h, ps=ps, z=z, kt=kt: h.matmul(
                                ps[:, z * 512:(z + 1) * 512], lhsT=KT[:, kvh, kt * 128:(kt + 1) * 128],
                                rhs=q_[:, kvh * 4:(kvh + 1) * 4, :].rearrange("p h c -> p (h c)"), start=True, stop=True),
                                reads=[KT, q_], writes=[ps])
                        S.op("act", lambda h, ps=ps, p=p: h.activation(
                            out=ptt[:, 2 * p:2 * p + 2, :].rearrange("p a c -> p (a c)"), in_=ps[:], func=AF.Exp, scale=ATT_SCALE, bias=negm[:, 2:3]),
                            reads=[ps, negm], writes=[ptb[p]])
                        if p >= 2:
                            pv(p - 2)
                    pv(NP2 - 2)
                    pv(NP2 - 1)
                    S.op("act", lambda h: h.activation(out=rcp[:], in_=prs[:], func=AF.Ln), reads=[prs], writes=[rcp])
                    S.op("act", lambda h: h.activation(out=rcp[:], in_=rcp[:], func=AF.Exp, scale=-1.0), reads=[rcp], writes=[rcp])
                    S.op("dve", lambda h, kvh=kvh: h.tensor_tensor(out=OT[kvh][:], in0=pO[:], in1=rcp[:], op=ALU.mult), reads=[pO, rcp], writes=[OT[kvh]])
                pyb = pS[0]
                for n in range(2):
                    for hh in range(8):
                        S.op("pe", lambda h, hh=hh, n=n: h.matmul(pyb[:, n * 512:(n + 1) * 512], lhsT=OT[hh // 4][:, (hh % 4) * 128:(hh % 4 + 1) * 128], rhs=wo[:, hh, n * 512:(n + 1) * 512],
                                                                  start=(hh == 0), stop=(hh == 7)), reads=[OT[hh // 4], wo], writes=[pyb])
                post_res(k, eb, [pyb, pyb], xr, gbc, dst[(l + 2) * 128:(l + 3) * 128, :], halves=True)
        S.barrier()


S5_CHUNKS = [(0, 256)] + [(256 + 512 * q, 512) for q in range(8)]


def tab(T, dd, c0, cn):
    if dd == 0:
        return T[:, c0:c0 + cn]
    hi = (255 - c0) if c0 < 256 else (4607 - c0)
    lo = hi - cn
    return T[:, hi:(lo if lo >= 0 else None):-1]


def s5_params(k, es):
    S, d = k.S, k.d
    P = {}

    def col(name, n=16):
        return S.sb(name, [128, n], F32, es)
    lr, li, ls = col("lr"), col("li"), col("ls")
    S.dma("sp", [(lr[:, dd * 8:(dd + 1) * 8], d["s5_lam_re"][dd].rearrange("g p -> (g p)").rearrange("(ct q) -> q ct", q=128)) for dd in range(2)],
          lr, writes=[lr], allow_slow_non_contiguous=True)
    S.dma("sp", [(li[:, dd * 8:(dd + 1) * 8], d["s5_lam_im"][dd].rearrange("g p -> (g p)").rearrange("(ct q) -> q ct", q=128)) for dd in range(2)],
          li, writes=[li], allow_slow_non_contiguous=True)
    S.dma("sp", [(ls[gl * 64:(gl + 1) * 64, dd * 8:(dd + 1) * 8],
                  d["s5_log_step"][dd].rearrange("(ct gl) -> gl ct", gl=2)[gl:gl + 1, :].partition_broadcast(64))
                 for dd in range(2) for gl in range(2)], ls, writes=[ls], allow_slow_non_contiguous=True)
    dt, zr, zi, em1, er = col("dt"), col("zr"), col("zi"), col("em1"), col("er")
    S.op("act", lambda h: h.activation(out=dt[:], in_=ls[:], func=AF.Exp), reads=[ls], writes=[dt])
    S.op("dve", lambda h: h.tensor_tensor(out=zr[:], in0=lr[:], in1=dt[:], op=ALU.mult), reads=[lr, dt], writes=[zr])
    S.op("dve", lambda h: h.tensor_tensor(out=zi[:], in0=li[:], in1=dt[:], op=ALU.mult), reads=[li, dt], writes=[zi])
    S.op("dve", lambda h: h.tensor_scalar(out=em1[:], in0=zr[:], scalar1=1.0 / 6, scalar2=1.0, op0=ALU.mult, op1=ALU.add), reads=[zr], writes=[em1])
    for n in (5, 4, 3, 2):
        S.op("dve", lambda h: h.tensor_tensor(out=em1[:], in0=em1[:], in1=zr[:], op=ALU.mult), reads=[em1, zr], writes=[em1])
        S.op("dve", lambda h, n=n: h.tensor_scalar(out=em1[:], in0=em1[:], scalar1=1.0 / n, scalar2=1.0, op0=ALU.mult, op1=ALU.add), reads=[em1], writes=[em1])
    S.op("dve", lambda h: h.tensor_tensor(out=em1[:], in0=em1[:], in1=zr[:], op=ALU.mult), reads=[em1, zr], writes=[em1])
    S.op("dve", lambda h: h.tensor_scalar(out=er[:], in0=em1[:], scalar1=1.0, scalar2=None, op0=ALU.add), reads=[em1], writes=[er])
    a, a2, sh, sn, cs, cm1 = col("a"), col("a2"), col("sh"), col("sn"), col("cs"), col("cm1")

    def reduce_turns(scale):
        S.op("dve", lambda h: h.tensor_scalar(out=a[:], in0=zi[:], scalar1=scale, scalar2=None, op0=ALU.mult), reads=[zi], writes=[a])
        S.op("dve", lambda h: h.tensor_scalar(out=a2[:], in0=a[:], scalar1=MAGIC, scalar2=-MAGIC, op0=ALU.add, op1=ALU.add), reads=[a], writes=[a2])
        S.op("dve", lambda h: h.tensor_tensor(out=a[:], in0=a[:], in1=a2[:], op=ALU.subtract), reads=[a, a2], writes=[a])
    reduce_turns(1.0 / (4 * np.pi))
    S.op("act", lambda h: h.activation(out=sh[:], in_=a[:], func=AF.Sin, scale=TWO_PI_SAFE), reads=[a], writes=[sh])
    S.op("dve", lambda h: h.scalar_tensor_tensor(out=cm1[:], in0=sh[:], scalar=-2.0, in1=sh[:], op0=ALU.mult, op1=ALU.mult), reads=[sh], writes=[cm1])
    reduce_turns(1.0 / (2 * np.pi))
    S.op("act", lambda h: h.activation(out=sn[:], in_=a[:], func=AF.Sin, scale=TWO_PI_SAFE), reads=[a], writes=[sn])
    S.op("dve", lambda h: h.tensor_scalar(out=cs[:], in0=cm1[:], scalar1=1.0, scalar2=None, op0=ALU.add), reads=[cm1], writes=[cs])
    l1r, l1i, den, qr, qi, t0 = col("l1r"), col("l1i"), col("den"), col("qr"), col("qi"), col("t0")
    S.op("dve", lambda h: h.tensor_tensor(out=l1r[:], in0=em1[:], in1=cs[:], op=ALU.mult), reads=[em1, cs], writes=[l1r])
    S.op("dve", lambda h: h.tensor_tensor(out=l1r[:], in0=l1r[:], in1=cm1[:], op=ALU.add), reads=[l1r, cm1], writes=[l1r])
    S.op("dve", lambda h: h.tensor_tensor(out=l1i[:], in0=er[:], in1=sn[:], op=ALU.mult), reads=[er, sn], writes=[l1i])
    S.op("dve", lambda h: h.tensor_tensor(out=den[:], in0=lr[:], in1=lr[:], op=ALU.mult), reads=[lr], writes=[den])
    S.op("dve", lambda h: h.tensor_tensor(out=t0[:], in0=li[:], in1=li[:], op=ALU.mult), reads=[li], writes=[t0])
    S.op("dve", lambda h: h.tensor_tensor(out=den[:], in0=den[:], in1=t0[:], op=ALU.add), reads=[den, t0], writes=[den])
    S.op("dve", lambda h: h.reciprocal(out=den[:], in_=den[:]), reads=[den], writes=[den])
    S.op("dve", lambda h: h.tensor_tensor(out=qr[:], in0=l1r[:], in1=lr[:], op=ALU.mult), reads=[l1r, lr], writes=[qr])
    S.op("dve", lambda h: h.tensor_tensor(out=t0[:], in0=l1i[:], in1=li[:], op=ALU.mult), reads=[l1i, li], writes=[t0])
    S.op("dve", lambda h: h.tensor_tensor(out=qr[:], in0=qr[:], in1=t0[:], op=ALU.add), reads=[qr, t0], writes=[qr])
    S.op("dve", lambda h: h.tensor_tensor(out=qr[:], in0=qr[:], in1=den[:], op=ALU.mult), reads=[qr, den], writes=[qr])
    S.op("dve", lambda h: h.tensor_tensor(out=qi[:], in0=l1i[:], in1=lr[:], op=ALU.mult), reads=[l1i, lr], writes=[qi])
    S.op("dve", lambda h: h.tensor_tensor(out=t0[:], in0=l1r[:], in1=li[:], op=ALU.mult), reads=[l1r, li], writes=[t0])
    S.op("dve", lambda h: h.tensor_tensor(out=qi[:], in0=qi[:], in1=t0[:], op=ALU.subtract), reads=[qi, t0], writes=[qi])
    S.op("dve", lambda h: h.tensor_tensor(out=qi[:], in0=qi[:], in1=den[:], op=ALU.mult), reads=[qi, den], writes=[qi])
    wr = S.sb("wr", [128, 13, 16], F32, es)
    wi = S.sb("wi", [128, 13, 16], F32, es)
    S.op("dve", lambda h: h.tensor_tensor(out=t0[:], in0=cs[:], in1=cs[:], op=ALU.mult), reads=[cs], writes=[t0])
    S.op("dve", lambda h: h.tensor_tensor(out=a[:], in0=sn[:], in1=sn[:], op=ALU.mult), reads=[sn], writes=[a])
    S.op("dve", lambda h: h.tensor_tensor(out=t0[:], in0=t0[:], in1=a[:], op=ALU.add), reads=[t0, a], writes=[t0])
    S.op("act", lambda h: h.activation(out=t0[:], in_=t0[:], func=AF.Sqrt), reads=[t0], writes=[t0])
    S.op("dve", lambda h: h.reciprocal(out=t0[:], in_=t0[:]), reads=[t0], writes=[t0])
    S.op("dve", lambda h: h.tensor_tensor(out=wr[:, 0, :], in0=cs[:], in1=t0[:], op=ALU.mult), reads=[cs, t0], writes=[wr])
    S.op("dve", lambda h: h.tensor_tensor(out=wi[:, 0, :], in0=sn[:], in1=t0[:], op=ALU.mult), reads=[sn, t0], writes=[wi])
    for lv in range(12):
        S.op("dve", lambda h, lv=lv: h.tensor_tensor(out=a[:], in0=wr[:, lv, :], in1=wr[:, lv, :], op=ALU.mult), reads=[wr], writes=[a])
        S.op("dve", lambda h, lv=lv: h.tensor_tensor(out=a2[:], in0=wi[:, lv, :], in1=wi[:, lv, :], op=ALU.mult), reads=[wi], writes=[a2])
        S.op("dve", lambda h, lv=lv: h.scalar_tensor_tensor(out=wi[:, lv + 1, :], in0=wr[:, lv, :], scalar=2.0, in1=wi[:, lv, :], op0=ALU.mult, op1=ALU.mult),
             reads=[wr, wi], writes=[wi])
        S.op("dve", lambda h, lv=lv: h.tensor_tensor(out=wr[:, lv + 1, :], in0=a[:], in1=a2[:], op=ALU.subtract), reads=[a, a2], writes=[wr])
    P.update(er=er, qr=qr, qi=qi, wr=wr, wi=wi)
    P["bre"] = S.sb("bre", [128, 16, 16], F32, es)
    P["bim"] = S.sb("bim", [128, 16, 16], F32, es)
    for nm, key in (("bre", "s5_b_re"), ("bim", "s5_b_im")):
        S.dma("sp", [(P[nm][:, dd * 8:(dd + 1) * 8, :], d[key][dd].rearrange("g p c -> (g p) c").rearrange("(ct q) c -> q ct c", q=128)) for dd in range(2)],
              P[nm], writes=[P[nm]])
    P["cf"] = {}
    for nm, key in (("cre", "s5_c_re"), ("cim", "s5_c_im")):
        b = S.sb(nm, [128, 4, 128], F32, es)
        S.dma("sp", [(b[:, dd * 2 + ft, h2 * 64:(h2 + 1) * 64], d[key][dd].rearrange("g c p -> (g c) p")[ft * 128:(ft + 1) * 128, :])
                     for dd in range(2) for ft in range(2) for h2 in range(2)], b, writes=[b])
        P["cf"][nm] = b
    P["dcol"] = S.sb("dcol", [128, 2], F32, es)
    P["gbcol"] = S.sb("gbcol", [128, 2], F32, es)
    S.dma("sp", [(P["dcol"][:], d["s5_d"].rearrange("(ft p) -> p ft", p=128))], P["dcol"], writes=[P["dcol"]], allow_slow_non_contiguous=True)
    S.dma("sp", [(P["gbcol"][:], d["s5_glu_b"].rearrange("(ft p) -> p ft", p=128))], P["gbcol"], writes=[P["gbcol"]], allow_slow_non_contiguous=True)
    return P


def phase_mix0(k, src, dst):
    S, nc, d = k.S, k.nc, k.d
    i, j = 0, 1
    with ExitStack() as es:
        epsc = S.sb("epsc", [128, 1], F32, es)
        S.op("dve", lambda h: h.memset(epsc[:], EPS), writes=[epsc])
        ybT = S.sb("ybT", [128, 2, NTOK], BF16, es)
        with ExitStack() as e1:
            win = S.sb("win", [128, 8, 1792], BF16, e1)
            load_weight_cast(k, win, lambda kc, c0, c1: win[:, kc, c0:c1], d["ab_w_in"], D, 1792, 1792)
            wsr = S.sb("wsr", [128, 6, 128], F32, e1)
            wsT = S.sb("wsT", [128, 6, 128], BF16, e1)
            S.dma("sp", [(wsr[:], d["sgu_w"].rearrange("g t s -> t g s"))], wsr, writes=[wsr])
            sbc = S.sb("sbc", [128, 6], F32, e1)
            S.dma("sp", [(sbc[:], d["sgu_b"].rearrange("g t -> t g"))], sbc, writes=[sbc], allow_slow_non_contiguous=True)
            gsb = S.sb("gsb", [128, 768], F32, e1)
            S.dma("sp", [(gsb[:], d["sgu_norm_g"].rearrange("(o n) -> o n", o=1).partition_broadcast(128))], gsb, writes=[gsb])
            ptrs = [S.ps(f"ptr{q}", [128, 512], F32, e1) for q in range(2)]
            eb = {"ptr": ptrs}
            pq = [S.ps(f"pq{q}", [128, 512], F32, e1) for q in range(3)]
            pmA = S.ps("pmA", [128, 512], F32, e1)
            pBt = e1.enter_context(nc.psum_tensor("pBshared", [128, 512], F32))
            pmB = S.wrap("pmB", pBt)
            puT = S.wrap("puT", pBt)
            pya = S.ps("pya", [128, 1024], BF16, e1)
            for g in range(6):
                S.op("pe", lambda h, g=g: h.transpose(out=eb["ptr"][0][:, 0:128], in_=wsr[:, g, :], identity=k.identf[:]), reads=[wsr, k.identf], writes=[eb["ptr"][0]])
                S.op("dve", lambda h, g=g: h.tensor_copy(out=wsT[:, g, :], in_=eb["ptr"][0][:, 0:128]), reads=[eb["ptr"][0]], writes=[wsT])
            sets = []
            for z in range(2):
                ebz = {"junk": S.sb("junk", [128, D], BF16, e1), "ssq": S.sb("ssq", [128, 2], F32, e1),
                       "rst": S.sb("rst", [128, 2], F32, e1), "epsc": epsc, "ptr": ptrs}
                sets.append((S.sb("hT", [128, 8, 128], BF16, e1), S.sb("xt", [128, D], F32, e1), ebz,
                             S.sb("ug", [128, 768], F32, e1), S.sb("vg", [128, 768], F32, e1), S.sb("vh", [128, 768], BF16, e1),
                             S.sb("st6", [128, 6, 6], F32, e1), S.sb("mv", [128, 6, 2], F32, e1), S.sb("rsd", [128, 6], F32, e1),
                             S.sb("nmr", [128, 6], F32, e1), S.sb("tma", [128, 768], F32, e1), S.sb("yab", [128, 768], BF16, e1),
                             S.sb("yaTs", [128, 6, 128], BF16, e1), S.sb("uTs", [128, 2, 128], F32, e1)))
            GEL = AF.Gelu_apprx_tanh
            xt3 = [S.sb(f"xt3{q}", [128, D], F32, e1) for q in range(3)]

            def ldx(t):
                S.dma("sp", [(xt3[t % 3][:], src[t * 128:(t + 1) * 128, :])], xt3[t % 3], writes=[xt3[t % 3]])

            def mA(t):
                hT, xt, eb, ug, vg, vh, st6, mv, rsd, nmr, tma, yab, yaTs, uTs = sets[t % 2]
                xt = xt3[t % 3]
                norm_prep(k, eb, xt, 0, hT, i, j, 1 if t < 2 else 0, 0)
                for b in range(3):
                    for kc in range(8):
                        S.op("pe", lambda h, b=b, kc=kc: h.matmul(pq[b][:], lhsT=hT[:, kc, :], rhs=win[:, kc, b * 512:(b + 1) * 512],
                                                                   start=(kc == 0), stop=(kc == 7)), reads=[hT, win], writes=[pq[b]])
                for ft in range(2):
                    for kc in range(8):
                        S.op("pe", lambda h, ft=ft, kc=kc: h.matmul(puT[:, 256 + ft * 128:256 + (ft + 1) * 128], lhsT=win[:, kc, 1536 + ft * 128:1536 + (ft + 1) * 128],
                                                                     rhs=hT[:, kc, :], start=(kc == 0), stop=(kc == 7)), reads=[hT, win], writes=[puT])

            def mM(t):
                hT, xt, eb, ug, vg, vh, st6, mv, rsd, nmr, tma, yab, yaTs, uTs = sets[t % 2]
                S.op("dve", lambda h: h.tensor_copy(out=uTs[:].rearrange("p f c -> p (f c)"), in_=puT[:, 256:512]), reads=[puT], writes=[uTs])
                S.dma("sp", [(d["uT_d"][:, :, t * 128:(t + 1) * 128].rearrange("f p c -> p f c"), uTs[:])], uTs, reads=[uTs])
                S.op("act", lambda h: h.activation(out=ug[:, 0:512], in_=pq[0][:], func=GEL), reads=[pq[0]], writes=[ug])
                S.op("act", lambda h: h.activation(out=ug[:, 512:768], in_=pq[1][:, 0:256], func=GEL), reads=[pq[1]], writes=[ug])
                S.op("act", lambda h: h.activation(out=vg[:, 0:256], in_=pq[1][:, 256:512], func=GEL), reads=[pq[1]], writes=[vg])
                S.op("act", lambda h: h.activation(out=vg[:, 256:768], in_=pq[2][:], func=GEL), reads=[pq[2]], writes=[vg])

            def mB(t):
                hT, xt, eb, ug, vg, vh, st6, mv, rsd, nmr, tma, yab, yaTs, uTs = sets[t % 2]
                for g in range(6):
                    S.op("dve", lambda h, g=g: h.bn_stats(out=st6[:, g, :], in_=vg[:, g * 128:(g + 1) * 128]), reads=[vg], writes=[st6])
                for g in range(6):
                    S.op("dve", lambda h, g=g: h.bn_aggr(out=mv[:, g, :], in_=st6[:, g, :]), reads=[st6], writes=[mv])
                S.op("act", lambda h: h.activation(out=rsd[:], in_=mv[:, :, 1], func=AF.Sqrt, bias=epsc[:, 0:1], scale=1.0), reads=[mv, epsc], writes=[rsd])
                S.op("dve", lambda h: h.reciprocal(out=rsd[:], in_=rsd[:]), reads=[rsd], writes=[rsd])
                S.op("dve", lambda h: h.scalar_tensor_tensor(out=nmr[:], in0=mv[:, :, 0], scalar=-1.0, in1=rsd[:], op0=ALU.mult, op1=ALU.mult), reads=[mv, rsd], writes=[nmr])
                for g in range(6):
                    if g % 2 == 0:
                        S.op("act", lambda h, g=g: h.activation(out=vh[:, g * 128:(g + 1) * 128], in_=vg[:, g * 128:(g + 1) * 128], func=AF.Identity,
                                                                 scale=rsd[:, g:g + 1], bias=nmr[:, g:g + 1]), reads=[vg, rsd, nmr], writes=[vh])
                    else:
                        S.op("dve", lambda h, g=g: h.tensor_scalar(out=vh[:, g * 128:(g + 1) * 128], in0=vg[:, g * 128:(g + 1) * 128],
                                                                   scalar1=rsd[:, g:g + 1], scalar2=nmr[:, g:g + 1], op0=ALU.mult, op1=ALU.add), reads=[vg, rsd, nmr], writes=[vh])
                for g in range(6):
                    pm, c0 = (pmA, g * 128) if g < 4 else (pmB, (g - 4) * 128)
                    S.op("pe", lambda h, g=g, pm=pm, c0=c0: h.matmul(pm[:, c0:c0 + 128], lhsT=wsT[:, g, :], rhs=vh[:, g * 128:(g + 1) * 128], start=True, stop=True),
                         reads=[wsT, vh], writes=[pm])
                S.op("dve", lambda h: h.tensor_tensor(out=tma[:, 0:512], in0=pmA[:], in1=gsb[:, 0:512], op=ALU.mult), reads=[pmA, gsb], writes=[tma])
                S.op("dve", lambda h: h.tensor_tensor(out=tma[:, 512:768], in0=pmB[:, 0:256], in1=gsb[:, 512:768], op=ALU.mult), reads=[pmB, gsb], writes=[tma])
                for g in range(6):
                    S.op("dve" if g % 2 else "pool", lambda h, g=g: (h.scalar_tensor_tensor(
                        out=yab[:, g * 128:(g + 1) * 128], in0=tma[:, g * 128:(g + 1) * 128], scalar=sbc[:, g:g + 1], in1=ug[:, g * 128:(g + 1) * 128],
                        op0=ALU.add, op1=ALU.mult)), reads=[tma, sbc, ug], writes=[yab]) if g % 2 else None
                    if g % 2 == 0:
                        S.op("dve", lambda h, g=g: h.scalar_tensor_tensor(
                            out=yab[:, g * 128:(g + 1) * 128], in0=tma[:, g * 128:(g + 1) * 128], scalar=sbc[:, g:g + 1], in1=ug[:, g * 128:(g + 1) * 128],
                            op0=ALU.add, op1=ALU.mult), reads=[tma, sbc, ug], writes=[yab])
                for g in range(6):
                    S.op("pe", lambda h, g=g: h.transpose(out=pya[:, g * 128:(g + 1) * 128], in_=yab[:, g * 128:(g + 1) * 128], identity=k.identb[:]),
                         reads=[yab, k.identb], writes=[pya])
                S.op("act", lambda h: h.copy(out=yaTs[:].rearrange("p g c -> p (g c)"), in_=pya[:, 0:768]), reads=[pya], writes=[yaTs])
                S.dma("sp", [(d["yaT_d"][t], yaTs[:])], yaTs, reads=[yaTs])

            ldx(0)
            ldx(1)
            mA(0)
            mM(0)
            for t in range(NT):
                if t + 2 < NT:
                    ldx(t + 2)
                if t + 1 < NT:
                    mA(t + 1)
                mB(t)
                if t + 1 < NT:
                    mM(t + 1)
        S.barrier()
        with ExitStack() as e2:
            Pm = s5_params(k, e2)
            gluw = S.sb("gluw", [128, 2, 256], BF16, e2)
            load_weight_cast(k, gluw, lambda kc, c0, c1: gluw[:, kc, c0:c1], d["s5_glu_w"], 256, 256, 256)
            ygb = ybT
            yc = S.sb("yc", [128, NTOK], F32, e2)
            Tcs = [S.sb(f"Tc{q}", [128, NTOK], F32, e2) for q in range(2)]
            Tss = [S.sb(f"Ts{q}", [128, NTOK], F32, e2) for q in range(2)]
            bpr = S.sb("bpr", [128, NTOK], F32, e2)
            bpi = S.sb("bpi", [128, NTOK], F32, e2)
            gr, gi = bpr, bpi
            tw = [S.sb(f"tw{q}", [128, 1024], F32, e2) for q in range(4)]
            Bpad = [S.sb(f"Bpad{q}", [128, 128], F32, e2) for q in range(2)]
            Bl = [S.sb(f"Bl{q}", [128, 128], F32, e2) for q in range(2)]
            Cpad = [S.sb(f"Cpad{q}", [128, 128], F32, e2) for q in range(2)]
            cT = {nm: S.sb("cT" + nm, [128, 4, 128], F32, e2) for nm in ("cre", "cim")}
            tq = S.sb("tq", [128, 16], F32, e2)
            ck = {n: [S.sb(f"{n}{q}", [128, 512], F32, e2) for q in range(2)] for n in ("br", "bi", "hr", "hi")}
            uTc = [S.sb(f"uTc{q}", [128, 512], F32, e2) for q in range(3)]
            tt = [S.sb(f"tt{q}", [128, 512], F32, e2) for q in range(4)]
            sgm = [S.sb(f"sgm{q}", [128, 512], F32, e2) for q in range(2)]
            pbr = [S.ps(f"pbr{q}", [128, 512], F32, e2) for q in range(2)]
            pbi = [S.ps(f"pbi{q}", [128, 512], F32, e2) for q in range(2)]
            pyc = [S.ps(f"pyc{q}", [128, 512], F32, e2) for q in range(2)]
            ptb = S.ps("ptb", [128, 128], F32, e2)
            for nm in ("cre", "cim"):
                for q in range(4):
                    S.op("pe", lambda h, nm=nm, q=q: h.transpose(out=ptb[:], in_=Pm["cf"][nm][:, q, :], identity=k.identf[:]), reads=[Pm["cf"][nm], k.identf], writes=[ptb])
                    S.op("dve", lambda h, nm=nm, q=q: h.tensor_copy(out=cT[nm][:, q, :], in_=ptb[:]), reads=[ptb], writes=[cT[nm]])
            cc = [0]
            uc = [0]
            its = [(ft, ctl, dd) for ft in range(2) for ctl in range(4) for dd in range(2)]

            def table_steps(n):
                ft, ctl, dd = its[n]
                ix = dd * 8 + ft * 4 + ctl
                Tc, Ts = Tcs[n % 2], Tss[n % 2]
                steps = []

                def init():
                    S.op("pool", lambda h: h.memset(Tc[:, 0:1], 1.0), writes=[Tc])
                    S.op("pool", lambda h: h.memset(Ts[:, 0:1], 0.0), writes=[Ts])
                steps.append(init)
                for lv in range(13):
                    nn = 1 << lv
                    mt = min(nn, NTOK - nn)
                    for o in range(0, mt, 1024):
                        m = min(1024, mt - o)

                        def piece(lv=lv, nn=nn, o=o, m=m):
                            wr_ = Pm["wr"][:, lv, ix:ix + 1]
                            wi_ = Pm["wi"][:, lv, ix:ix + 1]
                            S.op("act", lambda h: h.activation(out=tw[0][:, 0:m], in_=Ts[:, o:o + m], func=AF.Identity, scale=wi_), reads=[Ts, Pm["wi"]], writes=[tw[0]])
                            S.op("act", lambda h: h.activation(out=tw[1][:, 0:m], in_=Tc[:, o:o + m], func=AF.Identity, scale=wi_), reads=[Tc, Pm["wi"]], writes=[tw[1]])
                            S.op("act", lambda h: h.activation(out=tw[2][:, 0:m], in_=Tc[:, o:o + m], func=AF.Identity, scale=wr_), reads=[Tc, Pm["wr"]], writes=[tw[2]])
                            S.op("act", lambda h: h.activation(out=tw[3][:, 0:m], in_=Ts[:, o:o + m], func=AF.Identity, scale=wr_), reads=[Ts, Pm["wr"]], writes=[tw[3]])
                            S.op("pool", lambda h: h.tensor_tensor(out=Tc[:, nn + o:nn + o + m], in0=tw[2][:, 0:m], in1=tw[0][:, 0:m], op=ALU.subtract), reads=[tw[2], tw[0]], writes=[Tc])
                            S.op("pool", lambda h: h.tensor_tensor(out=Ts[:, nn + o:nn + o + m], in0=tw[3][:, 0:m], in1=tw[1][:, 0:m], op=ALU.add), reads=[tw[3], tw[1]], writes=[Ts])
                        steps.append(piece)
                return steps

            def body_steps(n):
                ft, ctl, dd = its[n]
                ct = ft * 4 + ctl
                ix = dd * 8 + ct
                Tc, Ts = Tcs[n % 2], Tss[n % 2]
                first = (ctl == 0 and dd == 0)
                steps = []

                def mats():
                    for part in range(2):
                        S.op("pool", lambda h, part=part: h.memset(Bpad[part][:], 0.0), writes=[Bpad[part]])
                        S.op("pool", lambda h, part=part: h.memset(Cpad[part][:], 0.0), writes=[Cpad[part]])
                    S.op("dve", lambda h: h.tensor_scalar(out=tq[:], in0=Pm["bim"][:, ix, :], scalar1=Pm["qi"][:, ix:ix + 1], scalar2=None, op0=ALU.mult),
                         reads=[Pm["bim"], Pm["qi"]], writes=[tq])
                    for gl in range(2):
                        cb = (2 * ctl + gl) * 16
                        sl = slice(gl * 64, (gl + 1) * 64)
                        S.op("dve", lambda h, sl=sl, cb=cb: h.scalar_tensor_tensor(
                            out=Bpad[0][sl, cb:cb + 16], in0=Pm["bre"][sl, ix, :], scalar=Pm["qr"][sl, ix:ix + 1], in1=tq[sl, :], op0=ALU.mult, op1=ALU.subtract),
                            reads=[Pm["bre"], Pm["qr"], tq], writes=[Bpad[0]])
                    S.op("dve", lambda h: h.tensor_scalar(out=tq[:], in0=Pm["bre"][:, ix, :], scalar1=Pm["qi"][:, ix:ix + 1], scalar2=None, op0=ALU.mult),
                         reads=[Pm["bre"], Pm["qi"], Bpad[0]], writes=[tq])
                    for gl in range(2):
                        cb = (2 * ctl + gl) * 16
                        sl = slice(gl * 64, (gl + 1) * 64)
                        S.op("dve", lambda h, sl=sl, cb=cb: h.scalar_tensor_tensor(
                            out=Bpad[1][sl, cb:cb + 16], in0=Pm["bim"][sl, ix, :], scalar=Pm["qr"][sl, ix:ix + 1], in1=tq[sl, :], op0=ALU.mult, op1=ALU.add),
                            reads=[Pm["bim"], Pm["qr"], tq], writes=[Bpad[1]])
                        S.op("dve", lambda h, sl=sl, cb=cb: h.tensor_copy(out=Cpad[0][sl, cb:cb + 16], in_=cT["cre"][sl, dd * 2 + ft, cb:cb + 16]),
                             reads=[cT["cre"]], writes=[Cpad[0]])
                        S.op("dve", lambda h, sl=sl, cb=cb: h.tensor_scalar(out=Cpad[1][sl, cb:cb + 16], in0=cT["cim"][sl, dd * 2 + ft, cb:cb + 16],
                                                                          scalar1=-1.0, scalar2=None, op0=ALU.mult), reads=[cT["cim"]], writes=[Cpad[1]])
                    for part in range(2):
                        S.op("pe", lambda h, part=part: h.transpose(out=ptb[:], in_=Bpad[part][:], identity=k.identf[:]), reads=[Bpad[part], k.identf], writes=[ptb])
                        S.op("act", lambda h, part=part: h.copy(out=Bl[part][:], in_=ptb[:]), reads=[ptb], writes=[Bl[part]])
                steps.append(mats)
                for (c0, cn) in S5_CHUNKS:
                    def rot(c0=c0, cn=cn):
                        q2 = cc[0] % 2
                        cc[0] += 1
                        u = uTc[uc[0] % 3]
                        uc[0] += 1
                        S.dma("sp", [(u[:, 0:cn], d["uT_d"][ft, :, c0:c0 + cn])], u, writes=[u])
                        S.op("pe", lambda h: h.matmul(pbr[q2][:, 0:cn], lhsT=Bl[0][:], rhs=u[:, 0:cn], start=True, stop=True), reads=[Bl[0], u], writes=[pbr[q2]])
                        S.op("pe", lambda h: h.matmul(pbi[q2][:, 0:cn], lhsT=Bl[1][:], rhs=u[:, 0:cn], start=True, stop=True), reads=[Bl[1], u], writes=[pbi[q2]])
                        br, bi = ck["br"][q2], ck["bi"][q2]
                        S.op("act", lambda h: h.copy(out=br[:, 0:cn], in_=pbr[q2][:, 0:cn]), reads=[pbr[q2]], writes=[br])
                        S.op("act", lambda h: h.copy(out=bi[:, 0:cn], in_=pbi[q2][:, 0:cn]), reads=[pbi[q2]], writes=[bi])
                        tc_, ts_ = tab(Tc, dd, c0, cn), tab(Ts, dd, c0, cn)
                        S.op("dve", lambda h: h.tensor_tensor(out=tt[0][:, 0:cn], in0=br[:, 0:cn], in1=tc_, op=ALU.mult), reads=[br, Tc], writes=[tt[0]])
                        S.op("dve", lambda h: h.tensor_tensor(out=tt[1][:, 0:cn], in0=bi[:, 0:cn], in1=ts_, op=ALU.mult), reads=[bi, Ts], writes=[tt[1]])
                        S.op("dve", lambda h: h.tensor_tensor(out=bpr[:, c0:c0 + cn], in0=tt[0][:, 0:cn], in1=tt[1][:, 0:cn], op=ALU.add), reads=[tt[0], tt[1]], writes=[bpr])
                        S.op("pool", lambda h: h.tensor_tensor(out=tt[2][:, 0:cn], in0=bi[:, 0:cn], in1=tc_, op=ALU.mult), reads=[bi, Tc], writes=[tt[2]])
                        S.op("pool", lambda h: h.tensor_tensor(out=tt[3][:, 0:cn], in0=br[:, 0:cn], in1=ts_, op=ALU.mult), reads=[br, Ts], writes=[tt[3]])
                        S.op("dve", lambda h: h.tensor_tensor(out=bpi[:, c0:c0 + cn], in0=tt[2][:, 0:cn], in1=tt[3][:, 0:cn], op=ALU.subtract), reads=[tt[2], tt[3]], writes=[bpi])
                    steps.append(rot)

                def scans():
                    erb = Pm["er"][:, ix:ix + 1]
                    for b_ in (bpr, bpi):
                        if dd == 0:
                            S.op("dve", lambda h, b_=b_: h.tensor_tensor_scan(out=b_[:], data0=erb.to_broadcast([128, NTOK]), data1=b_[:], initial=0.0, op0=ALU.mult, op1=ALU.add),
                                 reads=[b_, Pm["er"]], writes=[b_])
                        else:
                            S.op("dve", lambda h, b_=b_: h.tensor_tensor_scan(out=b_[:, 255::-1], data0=erb.to_broadcast([128, 256]), data1=b_[:, 255::-1], initial=0.0, op0=ALU.mult, op1=ALU.add),
                                 reads=[b_, Pm["er"]], writes=[b_])
                            S.op("dve", lambda h, b_=b_: h.tensor_tensor_scan(out=b_[:, NTOK - 1:255:-1], data0=erb.to_broadcast([128, NTOK - 256]), data1=b_[:, NTOK - 1:255:-1],
                                                                              initial=b_[:, 0:1], op0=ALU.mult, op1=ALU.add), reads=[b_, Pm["er"]], writes=[b_])
                steps.append(scans)
                for (c0, cn) in S5_CHUNKS:
                    def unrot(c0=c0, cn=cn):
                        q2 = cc[0] % 2
                        cc[0] += 1
                        tc_, ts_ = tab(Tc, dd, c0, cn), tab(Ts, dd, c0, cn)
                        hr, hi = ck["hr"][q2], ck["hi"][q2]
                        S.op("dve", lambda h: h.tensor_tensor(out=tt[0][:, 0:cn], in0=gr[:, c0:c0 + cn], in1=tc_, op=ALU.mult), reads=[gr, Tc], writes=[tt[0]])
                        S.op("dve", lambda h: h.tensor_tensor(out=tt[1][:, 0:cn], in0=gi[:, c0:c0 + cn], in1=ts_, op=ALU.mult), reads=[gi, Ts], writes=[tt[1]])
                        S.op("dve", lambda h: h.tensor_tensor(out=hr[:, 0:cn], in0=tt[0][:, 0:cn], in1=tt[1][:, 0:cn], op=ALU.subtract), reads=[tt[0], tt[1]], writes=[hr])
                        S.op("pool", lambda h: h.tensor_tensor(out=tt[2][:, 0:cn], in0=gr[:, c0:c0 + cn], in1=ts_, op=ALU.mult), reads=[gr, Ts], writes=[tt[2]])
                        S.op("pool", lambda h: h.tensor_tensor(out=tt[3][:, 0:cn], in0=gi[:, c0:c0 + cn], in1=tc_, op=ALU.mult), reads=[gi, Tc], writes=[tt[3]])
                        S.op("pool", lambda h: h.tensor_tensor(out=hi[:, 0:cn], in0=tt[2][:, 0:cn], in1=tt[3][:, 0:cn], op=ALU.add), reads=[tt[2], tt[3]], writes=[hi])
                        S.op("pe", lambda h: h.matmul(pyc[q2][:, 0:cn], lhsT=Cpad[0][:], rhs=hr[:, 0:cn], start=True, stop=False), reads=[Cpad[0], hr], writes=[pyc[q2]])
                        S.op("pe", lambda h: h.matmul(pyc[q2][:, 0:cn], lhsT=Cpad[1][:], rhs=hi[:, 0:cn], start=False, stop=True), reads=[Cpad[1], hi], writes=[pyc[q2]])
                        if first:
                            S.op("act", lambda h: h.copy(out=yc[:, c0:c0 + cn], in_=pyc[q2][:, 0:cn]), reads=[pyc[q2]], writes=[yc])
                        else:
                            S.op("dve", lambda h: h.tensor_tensor(out=yc[:, c0:c0 + cn], in0=pyc[q2][:, 0:cn], in1=yc[:, c0:c0 + cn], op=ALU.add), reads=[pyc[q2], yc], writes=[yc])
                    steps.append(unrot)
                if ctl == 3 and dd == 1:
                    for (c0, cn) in S5_CHUNKS:
                        def fin(c0=c0, cn=cn):
                            u = uTc[uc[0] % 3]
                            uc[0] += 1
                            S.dma("sp", [(u[:, 0:cn], d["uT_d"][ft, :, c0:c0 + cn])], u, writes=[u])
                            S.op("dve", lambda h: h.scalar_tensor_tensor(out=yc[:, c0:c0 + cn], in0=u[:, 0:cn], scalar=Pm["dcol"][:, ft:ft + 1], in1=yc[:, c0:c0 + cn], op0=ALU.mult, op1=ALU.add),
                                 reads=[u, Pm["dcol"], yc], writes=[yc])
                            S.op("act", lambda h: h.activation(out=ygb[:, ft, c0:c0 + cn], in_=yc[:, c0:c0 + cn], func=AF.Gelu_apprx_tanh), reads=[yc], writes=[ygb])
                        steps.append(fin)
                return steps

            for f_ in table_steps(0):
                f_()
            for n in range(len(its)):
                bs = body_steps(n)
                ts_l = table_steps(n + 1) if n + 1 < len(its) else []
                ti = 0
                for si, f_ in enumerate(bs):
                    f_()
                    want = (len(ts_l) * (si + 1)) // len(bs)
                    while ti < want:
                        ts_l[ti]()
                        ti += 1
                while ti < len(ts_l):
                    ts_l[ti]()
                    ti += 1
            for (c0, cn) in S5_CHUNKS:
                for jt in range(2):
                    for kc in range(2):
                        S.op("pe", lambda h, kc=kc, jt=jt, c0=c0, cn=cn: h.matmul(pyc[jt][:, 0:cn], lhsT=gluw[:, kc, jt * 128:(jt + 1) * 128], rhs=ygb[:, kc, c0:c0 + cn],
                                                                                   start=(kc == 0), stop=(kc == 1)), reads=[gluw, ygb], writes=[pyc[jt]])
                for jt in range(2):
                    S.op("act", lambda h, jt=jt, cn=cn: h.activation(out=sgm[jt][:, 0:cn], in_=pyc[jt][:, 0:cn], func=AF.Sigmoid, bias=Pm["gbcol"][:, jt:jt + 1], scale=1.0),
                         reads=[pyc[jt], Pm["gbcol"]], writes=[sgm[jt]])
                for jt in range(2):
                    S.op("dve", lambda h, jt=jt, c0=c0, cn=cn: h.tensor_tensor(out=ybT[:, jt, c0:c0 + cn], in0=ygb[:, jt, c0:c0 + cn], in1=sgm[jt][:, 0:cn], op=ALU.mult),
                         reads=[ygb, sgm[jt]], writes=[ybT])
        S.barrier()
        with ExitStack() as e3:
            wout = S.sb("wout", [128, 8, D], BF16, e3)
            load_weight_cast(k, wout, lambda kc, c0, c1: wout[:, kc, c0:c1], d["ab_w_out"], D, D, 1024)
            gbc = load_gbc(k, e3, i, j)
            yaT = [S.sb(f"yaT{q}", [128, 6, 128], BF16, e3) for q in range(2)]
            xrs = [S.sb(f"xr{q}", [128, D], F32, e3) for q in range(2)]
            ebs = [{"junk": S.sb("junk", [128, D], BF16, e3), "ss2": S.sb("ss2", [128, 2], F32, e3),
                    "rs2": S.sb("rs2", [128, 4], F32, e3), "tmp": S.sb("tmp", [128, D], F32, e3), "epsc": epsc} for _ in range(2)]
            pys = [[S.ps(f"py{q}{n}", [128, 512], F32, e3) for n in range(2)] for q in range(2)]
            def ld3(t):
                S.dma("sp", [(yaT[t % 2][:], d["yaT_d"][t])], yaT[t % 2], writes=[yaT[t % 2]])
                S.dma("sp", [(xrs[t % 2][:], src[t * 128:(t + 1) * 128, :])], xrs[t % 2], writes=[xrs[t % 2]])
            ld3(0)
            for t in range(NT):
                ya, xr, py = yaT[t % 2], xrs[t % 2], pys[t % 2]
                if t + 1 < NT:
                    ld3(t + 1)
                for n in range(2):
                    for kc in range(8):
                        lh = ya[:, kc, :] if kc < 6 else ybT[:, kc - 6, t * 128:(t + 1) * 128]
                        S.op("pe", lambda h, n=n, kc=kc, lh=lh, py=py: h.matmul(py[n][:], lhsT=lh, rhs=wout[:, kc, n * 512:(n + 1) * 512], start=(kc == 0), stop=(kc == 7)),
                             reads=[ya, ybT, wout], writes=[py[n]])
                post_res(k, ebs[t % 2], py, xr, gbc[1 if t < 2 else 0], dst[t * 128:(t + 1) * 128, :])
        S.barrier()


def build(debug=None):
    nc = bass.Bass("TRN2", target_bir_lowering=False)
    k = K()
    k.nc = nc
    k.d = {}
    for nm, shp in PARAM_SPECS:
        k.d[nm] = nc.dram_tensor(nm, list(shp), F32, kind="ExternalInput").ap()
    k.d["out"] = nc.dram_tensor("out", [4096, D], F32, kind="ExternalOutput").ap()
    k.d["grow_d"] = nc.dram_tensor("grow_d", [2, 3, 2, D], F32, kind="Internal").ap()
    k.d["qT_d"] = nc.dram_tensor("qT_d", [32, 128, 8, 128], BF16, kind="Internal").ap()
    k.d["uT_d"] = nc.dram_tensor("uT_d", [2, 128, NTOK], F32, kind="Internal").ap()
    k.d["yaT_d"] = nc.dram_tensor("yaT_d", [NT, 128, 6, 128], BF16, kind="Internal").ap()
    for nm in ("sA", "sB", "sC", "sD", "sE"):
        k.d[nm] = nc.dram_tensor(nm, [NTOK, D], F32, kind="Internal").ap()
    if debug:
        k.d["dbg_in"] = nc.dram_tensor("dbg_in", [NTOK, D], F32, kind="ExternalInput").ap()
        k.d["dbg_out"] = nc.dram_tensor("dbg_out", [NTOK, D], F32, kind="ExternalOutput").ap()
    with ExitStack() as es:
        k.S = Sched(nc, es)
        setup_globals(k)
        allt = list(range(NT))
        lat = list(range(2, NT))
        phase_mod(k)
        dd = k.d
        if debug is None:
            phase_ffn(k, 0, 0, dd["xs"], dd["sA"], allt)
            phase_mix0(k, dd["sA"], dd["sB"])
            phase_ffn(k, 0, 1, dd["sB"], dd["sC"], allt)
            phase_ffn(k, 1, 0, dd["sC"], dd["sD"], allt)
            phase_attn(k, dd["sD"], dd["sE"])
            phase_ffn(k, 1, 1, dd["sE"], dd["out"], lat, dst_off=-2)
        elif debug == "attn":
            phase_attn(k, dd["dbg_in"], dd["dbg_out"])
        elif debug == "mix0":
            phase_mix0(k, dd["dbg_in"], dd["dbg_out"])
        elif debug == "ffn":
            phase_ffn(k, 0, 0, dd["dbg_in"], dd["dbg_out"], allt)
        k.S.barrier()
        print("ninst", k.S.ninst, "nwait", k.S.nwait, "nsem", len(k.S.sems))
    return nc

def make_in_maps(inputs):
    f = lambda a: np.ascontiguousarray(np.asarray(a, dtype=np.float32))
    shared = {
        "w_mod": f(inputs["w_mod"]), "b_mod": f(inputs["b_mod"]), "norm_pre": f(inputs["norm_pre"]),
        "norm_post": f(inputs["norm_post"]), "ffn_w_in": f(inputs["ffn_w_in"]), "ffn_w_out": f(inputs["ffn_w_out"]),
        "ab_w_in": f(inputs["ab_w_in"][0]), "ab_w_out": f(inputs["ab_w_out"][0]), "sgu_norm_g": f(inputs["sgu_norm_g"][0]),
        "sgu_w": f(inputs["sgu_w"][0]), "sgu_b": f(inputs["sgu_b"][0]), "s5_lam_re": f(inputs["s5_lam_re"][0]),
        "s5_lam_im": f(inputs["s5_lam_im"][0]), "s5_log_step": f(inputs["s5_log_step"][0]),
        "s5_b_re": f(inputs["s5_b_re"][0]), "s5_b_im": f(inputs["s5_b_im"][0]), "s5_c_re": f(inputs["s5_c_re"][0]),
        "s5_c_im": f(inputs["s5_c_im"][0]), "s5_d": f(inputs["s5_d"][0]), "s5_glu_w": f(inputs["s5_glu_w"][0]),
        "s5_glu_b": f(inputs["s5_glu_b"][0]), "attn_w_qkv": f(inputs["attn_w_qkv"][0]), "attn_w_out": f(inputs["attn_w_out"][0]),
        "attn_q_norm": f(inputs["attn_q_norm"][0]), "attn_k_norm": f(inputs["attn_k_norm"][0]),
    }
    maps = []
    for b in range(8):
        m = dict(shared)
        m["xs"] = np.ascontiguousarray(np.concatenate([inputs["ctx"][b], inputs["x"][b]], axis=0).astype(np.float32))
        m["cond"] = np.ascontiguousarray(np.stack([inputs["c"][b], inputs["c_ctx"]], axis=0).astype(np.float32))
        maps.append(m)
    return maps


def kernel(**inputs):
    nc = build()
    maps = make_in_maps(inputs)
    res = run_bass_kernel_spmd(nc, maps, core_ids=list(range(8)))
    return np.stack([np.asarray(r["out"]) for r in res.results], axis=0).astype(np.float32)
```
